# Optimizing a Trainium2 kernel written in Bass

```python
import math
import jax, jax.numpy as jnp
from jax import lax
import numpy as np

D_MODEL = 1024
BATCH = 8
SEQ = 4096
DEPTH = 4

GRID_W = 64
CTX_LEN = 256
N_MOD = 9
D_FF = 2816
CONV_DIM = 512
CONV_WIDTH = 31
GDN_HEADS = 4
GDN_DK = 128
GDN_DV = 128
GDN_DIM = GDN_HEADS * GDN_DV
SHORT_CONV = 5
GDN_CHUNK = 64
EVEN_IN = 2 * CONV_DIM + 4 * GDN_DIM + 4 * GDN_HEADS
EVEN_MIX = CONV_DIM + GDN_DIM
S5_DIM = 512
S5_GROUP_CH = 16
S5_GROUPS = S5_DIM // S5_GROUP_CH
S5_STATE = 64
EPS = 1e-6

kernel_name = 'hybrid_conv_deltanet_s5_prefix_trunk'


def rms_norm(x, g):
    xf = x.astype(jnp.float32)
    y = xf * lax.rsqrt(jnp.mean(xf * xf, axis=-1, keepdims=True) + EPS)
    return (y * g.astype(jnp.float32)).astype(x.dtype)


def layer_norm(x, g, b):
    xf = x.astype(jnp.float32)
    xc = xf - jnp.mean(xf, axis=-1, keepdims=True)
    y = xc * lax.rsqrt(jnp.mean(xc * xc, axis=-1, keepdims=True) + EPS)
    return (y * g.astype(jnp.float32) + b.astype(jnp.float32)).astype(x.dtype)


def modulate(x, g, shift, scale):
    return rms_norm(x, g) * (1 + scale) + shift


def swiglu(x, w13, w2):
    a, b = jnp.split(x @ w13, 2, axis=-1)
    return (jax.nn.silu(a) * b) @ w2


def depthwise_conv(x, w):
    k = w.shape[0]
    return lax.conv_general_dilated(x, w[:, None, :], window_strides=(1,),
                                    padding=[(k // 2, k // 2)],
                                    dimension_numbers=('NWC', 'WIO', 'NWC'),
                                    feature_group_count=x.shape[-1])


def conformer_conv(glu_in, conv_w, conv_b, ln_g, ln_b, rows):
    a, g = jnp.split(glu_in, 2, axis=-1)
    h = a * jax.nn.sigmoid(g)
    if rows is None:
        h = depthwise_conv(h, conv_w)
    else:
        bsz, seqlen, ch = h.shape
        h = depthwise_conv(h.reshape(bsz * rows, GRID_W, ch), conv_w).reshape(bsz, seqlen, ch)
    return jax.nn.silu(layer_norm(h + conv_b, ln_g, ln_b))


def l2norm(t):
    return t * lax.rsqrt(jnp.sum(t * t, axis=-1, keepdims=True) + EPS)


def gdn_prepare(qkv_raw, ab_raw, sconv_w, a_log, dt_bias):
    bsz, seqlen, _ = qkv_raw.shape
    qkv = jax.nn.silu(depthwise_conv(qkv_raw, sconv_w)).astype(jnp.float32)
    q, k, v = jnp.split(qkv, 3, axis=-1)
    q = l2norm(q.reshape(bsz, seqlen, GDN_HEADS, GDN_DK)) * (GDN_DK ** -0.5)
    k = l2norm(k.reshape(bsz, seqlen, GDN_HEADS, GDN_DK))
    v = v.reshape(bsz, seqlen, GDN_HEADS, GDN_DV)
    ab = ab_raw.astype(jnp.float32).reshape(bsz, seqlen, 2, 2, GDN_HEADS)
    log_a = -jnp.exp(a_log.astype(jnp.float32)) * jax.nn.softplus(ab[:, :, :, 0] + dt_bias.astype(jnp.float32))
    beta = jax.nn.sigmoid(ab[:, :, :, 1])
    return q, k, v, log_a, beta


def gated_delta_rule(q, k, v, log_a, beta, s0, with_out):
    bsz, seqlen, nh, _ = q.shape
    n = seqlen // GDN_CHUNK

    def to_chunks(t):
        t = t.reshape((bsz, n, GDN_CHUNK, nh) + t.shape[3:])
        return jnp.moveaxis(jnp.moveaxis(t, 1, 0), 3, 2)

    q, k, v, log_a, beta = (to_chunks(t) for t in (q, k, v, log_a, beta))
    g = jnp.cumsum(log_a, axis=-1)
    idx = jnp.arange(GDN_CHUNK)
    lower = idx[:, None] >= idx[None, :]
    strict = idx[:, None] > idx[None, :]
    gdiff = g[..., :, None] - g[..., None, :]
    decay = jnp.where(lower, jnp.exp(jnp.where(lower, gdiff, 0.0)), 0.0)
    kb = k * beta[..., None]
    a_mat = jnp.where(strict, jnp.einsum('nbhid,nbhjd->nbhij', kb, k) * decay, 0.0)
    eye = jnp.eye(GDN_CHUNK, dtype=a_mat.dtype)
    t_mat = lax.linalg.triangular_solve(a_mat + eye, jnp.broadcast_to(eye, a_mat.shape),
                                        left_side=True, lower=True, unit_diagonal=True)
    u = jnp.einsum('nbhij,nbhjd->nbhid', t_mat, v * beta[..., None])
    w = jnp.einsum('nbhij,nbhjd->nbhid', t_mat, kb * jnp.exp(g)[..., None])
    g_last = g[..., -1]
    k_tail = k * jnp.exp(g_last[..., None] - g)[..., None]

    def step(s, inp):
        w_i, u_i, kt_i, gl_i = inp[:4]
        v_new = u_i - jnp.einsum('bhck,bhkv->bhcv', w_i, s)
        s_next = s * jnp.exp(gl_i)[..., None, None] + jnp.einsum('bhck,bhcv->bhkv', kt_i, v_new)
        if not with_out:
            return s_next, None
        qd_i, qk_i = inp[4:]
        o = jnp.einsum('bhck,bhkv->bhcv', qd_i, s) + jnp.einsum('bhij,bhjv->bhiv', qk_i, v_new)
        return s_next, o

    xs = (w, u, k_tail, g_last)
    if with_out:
        xs = xs + (q * jnp.exp(g)[..., None], jnp.einsum('nbhid,nbhjd->nbhij', q, k) * decay)
    s_final, o = lax.scan(step, s0, xs)
    if not with_out:
        return s_final, None
    o = jnp.swapaxes(jnp.moveaxis(o, 0, 1), 2, 3).reshape(bsz, seqlen, nh, GDN_DV)
    return s_final, o


def bidir_gdn(prep_c, prep_l, with_ctx_out):
    qc, kc, vc, lac, bec = prep_c
    ql, kl, vl, lal, bel = prep_l
    s_zero = jnp.zeros((qc.shape[0], GDN_HEADS, GDN_DK, GDN_DV), jnp.float32)
    fl = lambda t: jnp.flip(t, axis=1)
    s_cf, o_cf = gated_delta_rule(qc, kc, vc, lac[:, :, 0], bec[:, :, 0], s_zero, with_ctx_out)
    _, o_lf = gated_delta_rule(ql, kl, vl, lal[:, :, 0], bel[:, :, 0], s_cf, True)
    s_cb, o_cb = gated_delta_rule(fl(qc), fl(kc), fl(vc), fl(lac[:, :, 1]), fl(bec[:, :, 1]), s_zero, with_ctx_out)
    _, o_lb = gated_delta_rule(fl(ql), fl(kl), fl(vl), fl(lal[:, :, 1]), fl(bel[:, :, 1]), s_cb, True)
    o_l = o_lf + fl(o_lb)
    o_c = o_cf + fl(o_cb) if with_ctx_out else None
    return o_c, o_l


def gdn_output(o, gate, g):
    o = o * lax.rsqrt(jnp.mean(o * o, axis=-1, keepdims=True) + EPS) * g.astype(jnp.float32)
    o = o * jax.nn.silu(gate.astype(jnp.float32).reshape(o.shape))
    return o.reshape(o.shape[0], o.shape[1], GDN_DIM)


def even_mixer(hc, hl, w_in, conv_w, conv_b, cln_g, cln_b, sconv_w, a_log, dt_bias, onorm_g, w_out,
               rows, with_ctx_out):
    cuts = [2 * CONV_DIM, 2 * CONV_DIM + 3 * GDN_DIM, 2 * CONV_DIM + 4 * GDN_DIM]
    glu_c, qkv_c, gate_c, ab_c = jnp.split(hc @ w_in, cuts, axis=-1)
    glu_l, qkv_l, gate_l, ab_l = jnp.split(hl @ w_in, cuts, axis=-1)
    prep_c = gdn_prepare(qkv_c, ab_c, sconv_w, a_log, dt_bias)
    prep_l = gdn_prepare(qkv_l, ab_l, sconv_w, a_log, dt_bias)
    o_c, o_l = bidir_gdn(prep_c, prep_l, with_ctx_out)

    def merge(glu, o, gate, r):
        conv = conformer_conv(glu, conv_w, conv_b, cln_g, cln_b, r)
        gdn = gdn_output(o, gate, onorm_g).astype(conv.dtype)
        return jnp.concatenate([conv, gdn], axis=-1) @ w_out

    yl = merge(glu_l, o_l, gate_l, rows)
    yc = merge(glu_c, o_c, gate_c, None) if with_ctx_out else None
    return yc, yl


def s5_discretize(lam_re, lam_im, log_step, b_re, b_im):
    lam_re = jnp.minimum(lam_re, -1e-4)
    dt = jnp.exp(log_step)[:, None]
    mag = jnp.exp(lam_re * dt)
    ab_re = mag * jnp.cos(lam_im * dt)
    ab_im = mag * jnp.sin(lam_im * dt)
    den = lam_re * lam_re + lam_im * lam_im
    num_re = ab_re - 1.0
    coef_re = (num_re * lam_re + ab_im * lam_im) / den
    coef_im = (ab_im * lam_re - num_re * lam_im) / den
    bb_re = coef_re[..., None] * b_re - coef_im[..., None] * b_im
    bb_im = coef_re[..., None] * b_im + coef_im[..., None] * b_re
    return ab_re, ab_im, bb_re, bb_im


def s5_input(u, bb_re, bb_im):
    return (jnp.einsum('lbgh,gph->lbgp', u, bb_re), jnp.einsum('lbgh,gph->lbgp', u, bb_im))


def _ssm_combine(e1, e2):
    a1r, a1i, b1r, b1i = e1
    a2r, a2i, b2r, b2i = e2
    return (a2r * a1r - a2i * a1i, a2r * a1i + a2i * a1r,
            a2r * b1r - a2i * b1i + b2r, a2r * b1i + a2i * b1r + b2i)


def s5_scan(ab_re, ab_im, bu_re, bu_im, reverse):
    shape = (bu_re.shape[0], 1) + ab_re.shape
    a_re = jnp.broadcast_to(ab_re, shape)
    a_im = jnp.broadcast_to(ab_im, shape)
    _, _, x_re, x_im = lax.associative_scan(_ssm_combine, (a_re, a_im, bu_re, bu_im), reverse=reverse, axis=0)
    return x_re, x_im


def s5_readout(x_re, x_im, c_re, c_im):
    return jnp.einsum('lbgp,ghp->lbgh', x_re, c_re) - jnp.einsum('lbgp,ghp->lbgh', x_im, c_im)


def s5_mixer(hc, hl, w_in, lam_re, lam_im, log_step, b_re, b_im, c_re, c_im, d_skip, w_out, with_ctx_out):
    f32 = jnp.float32

    def to_groups(h):
        u = (h @ w_in).astype(f32)
        bsz, seqlen, _ = u.shape
        return jnp.swapaxes(u, 0, 1).reshape(seqlen, bsz, S5_GROUPS, S5_GROUP_CH)

    uc, ul = to_groups(hc), to_groups(hl)
    d = d_skip.astype(f32).reshape(S5_GROUPS, S5_GROUP_CH)
    yl = d * ul
    yc = d * uc if with_ctx_out else None
    for direction in range(2):
        rev = direction == 1
        ab_re, ab_im, bb_re, bb_im = s5_discretize(lam_re[direction].astype(f32), lam_im[direction].astype(f32),
                                                   log_step[direction].astype(f32), b_re[direction].astype(f32),
                                                   b_im[direction].astype(f32))
        bc_re, bc_im = s5_input(uc, bb_re, bb_im)
        xc_re, xc_im = s5_scan(ab_re, ab_im, bc_re, bc_im, rev)
        end = 0 if rev else -1
        start = -1 if rev else 0
        h0_re, h0_im = xc_re[end], xc_im[end]
        bl_re, bl_im = s5_input(ul, bb_re, bb_im)
        bl_re = bl_re.at[start].add(ab_re * h0_re - ab_im * h0_im)
        bl_im = bl_im.at[start].add(ab_re * h0_im + ab_im * h0_re)
        xl_re, xl_im = s5_scan(ab_re, ab_im, bl_re, bl_im, rev)
        cr, ci = c_re[direction].astype(f32), c_im[direction].astype(f32)
        yl = yl + s5_readout(xl_re, xl_im, cr, ci)
        if with_ctx_out:
            yc = yc + s5_readout(xc_re, xc_im, cr, ci)

    def glu_out(y):
        seqlen, bsz = y.shape[:2]
        z = jax.nn.gelu(jnp.swapaxes(y.reshape(seqlen, bsz, S5_DIM), 0, 1)).astype(hl.dtype)
        a, g = jnp.split(z @ w_out, 2, axis=-1)
        return a * jax.nn.sigmoid(g)

    return (glu_out(yc) if with_ctx_out else None), glu_out(yl)


def setup_inputs(seed: int = 0) -> dict:
    key = jax.random.key(seed)
    ks = iter(jax.random.split(key, 32))
    f32 = jnp.float32
    nrm = lambda shape, scale: jax.random.normal(next(ks), shape, f32) * scale
    ne = (DEPTH + 1) // 2
    no = DEPTH // 2
    D = D_MODEL
    inp = {}
    inp['x'] = nrm((BATCH, SEQ, D), 1.0)
    inp['c'] = nrm((BATCH, D), 1.0)
    inp['ctx'] = nrm((BATCH, CTX_LEN, D), 1.0)
    inp['c_ctx'] = nrm((D,), 1.0)
    inp['w_mod'] = nrm((DEPTH, D, N_MOD * D), 0.5 * D ** -0.5)
    inp['b_mod'] = nrm((DEPTH, N_MOD * D), 0.01)
    inp['norm_g'] = 1.0 + nrm((DEPTH, 3, D), 0.02)
    inp['ffn_w13'] = nrm((DEPTH, 2, D, 2 * D_FF), D ** -0.5)
    inp['ffn_w2'] = nrm((DEPTH, 2, D_FF, D), D_FF ** -0.5)
    inp['final_g'] = 1.0 + nrm((D,), 0.02)
    inp['e_w_in'] = nrm((ne, D, EVEN_IN), D ** -0.5)
    inp['e_conv_w'] = nrm((ne, CONV_WIDTH, CONV_DIM), CONV_WIDTH ** -0.5)
    inp['e_conv_b'] = nrm((ne, CONV_DIM), 0.01)
    inp['e_cln_g'] = 1.0 + nrm((ne, CONV_DIM), 0.02)
    inp['e_cln_b'] = nrm((ne, CONV_DIM), 0.01)
    inp['e_sconv_w'] = nrm((ne, SHORT_CONV, 3 * GDN_DIM), SHORT_CONV ** -0.5)
    inp['e_a_log'] = jnp.log(jax.random.uniform(next(ks), (ne, 2, GDN_HEADS), f32, 1.0, 16.0))
    dt = jnp.exp(jax.random.uniform(next(ks), (ne, 2, GDN_HEADS), f32, math.log(1e-3), math.log(1e-1)))
    inp['e_dt_bias'] = dt + jnp.log(-jnp.expm1(-dt))
    inp['e_onorm_g'] = 1.0 + nrm((ne, GDN_DV), 0.02)
    inp['e_w_out'] = nrm((ne, EVEN_MIX, D), EVEN_MIX ** -0.5)
    inp['o_w_in'] = nrm((no, D, S5_DIM), D ** -0.5)
    inp['o_lam_re'] = -0.5 + nrm((no, 2, S5_GROUPS, S5_STATE), 0.01)
    inp['o_lam_im'] = jnp.pi * jnp.arange(S5_STATE, dtype=f32) + nrm((no, 2, S5_GROUPS, S5_STATE), 0.01)
    inp['o_log_step'] = jax.random.uniform(next(ks), (no, 2, S5_GROUPS), f32, math.log(1e-3), math.log(1e-1))
    inp['o_b_re'] = nrm((no, 2, S5_GROUPS, S5_STATE, S5_GROUP_CH), (2 * S5_GROUP_CH) ** -0.5)
    inp['o_b_im'] = nrm((no, 2, S5_GROUPS, S5_STATE, S5_GROUP_CH), (2 * S5_GROUP_CH) ** -0.5)
    inp['o_c_re'] = nrm((no, 2, S5_GROUPS, S5_GROUP_CH, S5_STATE), (2 * S5_STATE) ** -0.5)
    inp['o_c_im'] = nrm((no, 2, S5_GROUPS, S5_GROUP_CH, S5_STATE), (2 * S5_STATE) ** -0.5)
    inp['o_d'] = nrm((no, S5_DIM), 1.0)
    inp['o_w_out'] = nrm((no, S5_DIM, 2 * D), S5_DIM ** -0.5)
    return inp


def reference(x, c, ctx, c_ctx, w_mod, b_mod, norm_g, ffn_w13, ffn_w2, final_g,
              e_w_in, e_conv_w, e_conv_b, e_cln_g, e_cln_b, e_sconv_w, e_a_log, e_dt_bias, e_onorm_g, e_w_out,
              o_w_in, o_lam_re, o_lam_im, o_log_step, o_b_re, o_b_im, o_c_re, o_c_im, o_d, o_w_out):
    rows = x.shape[1] // GRID_W
    for l in range(DEPTH):
        last = l == DEPTH - 1
        j = l // 2
        ml = jnp.split((jax.nn.silu(c) @ w_mod[l] + b_mod[l])[:, None, :], N_MOD, axis=-1)
        mc = jnp.split(jax.nn.silu(c_ctx) @ w_mod[l] + b_mod[l], N_MOD, axis=-1)
        x = x + 0.5 * ml[2] * swiglu(modulate(x, norm_g[l, 0], ml[0], ml[1]), ffn_w13[l, 0], ffn_w2[l, 0])
        ctx = ctx + 0.5 * mc[2] * swiglu(modulate(ctx, norm_g[l, 0], mc[0], mc[1]), ffn_w13[l, 0], ffn_w2[l, 0])
        hl = modulate(x, norm_g[l, 1], ml[3], ml[4])
        hc = modulate(ctx, norm_g[l, 1], mc[3], mc[4])
        if l % 2 == 0:
            yc, yl = even_mixer(hc, hl, e_w_in[j], e_conv_w[j], e_conv_b[j], e_cln_g[j], e_cln_b[j],
                                e_sconv_w[j], e_a_log[j], e_dt_bias[j], e_onorm_g[j], e_w_out[j],
                                rows, not last)
        else:
            yc, yl = s5_mixer(hc, hl, o_w_in[j], o_lam_re[j], o_lam_im[j], o_log_step[j], o_b_re[j], o_b_im[j],
                              o_c_re[j], o_c_im[j], o_d[j], o_w_out[j], not last)
        x = x + ml[5] * yl
        x = x + 0.5 * ml[8] * swiglu(modulate(x, norm_g[l, 2], ml[6], ml[7]), ffn_w13[l, 1], ffn_w2[l, 1])
        if not last:
            ctx = ctx + mc[5] * yc
            ctx = ctx + 0.5 * mc[8] * swiglu(modulate(ctx, norm_g[l, 2], mc[6], mc[7]), ffn_w13[l, 1], ffn_w2[l, 1])
    return rms_norm(x, final_g)
```

```python
import numpy as np
from contextlib import ExitStack
import concourse.bass as bass
import concourse.mybir as mybir
from concourse.bass_utils import run_bass_kernel_spmd

F32 = mybir.dt.float32
BF16 = mybir.dt.bfloat16
AF = mybir.ActivationFunctionType
ALU = mybir.AluOpType

D = 1024
KC = 8
NCTX = 256
NLAT = 4096
T = NCTX + NLAT
DFF = 2816
NI = DFF // 128
DEPTH = 4
EPS = 1e-6
TILES = [(0, 256)] + [(256 + 512 * j, 512) for j in range(8)]
SUPER = [[0, 1, 2], [3, 4, 5], [6, 7, 8]]


class Buf:
    __slots__ = ("name", "w", "r")

    def __init__(self, name=""):
        self.name = name
        self.w = None
        self.r = {}


class Prog:
    ENG = ("pe", "act", "dve", "pool", "sp")

    def __init__(self, nc, n_dma_sems=48):
        self.nc = nc
        self.q = {k: [] for k in self.ENG}
        self.cnt = {k: 0 for k in self.ENG}
        self.seen = {k: {} for k in self.ENG}
        self.n_dma_sems = n_dma_sems
        self.dma_k = 0
        self.dma_last = {}
        self.n_ops = 0

    def _deps(self, eng, reads, writes, extra=()):
        need = {}

        def add(tok):
            if tok is None:
                return
            k, v = tok
            if need.get(k, 0) < v:
                need[k] = v

        for b in reads:
            add(b.w)
        for b in writes:
            add(b.w)
            for k, v in b.r.items():
                add((k, v))
        for t in extra:
            add(t)
        seen = self.seen[eng]
        waits = []
        for k, v in need.items():
            if k == "pe" and eng == "pe":
                continue
            if seen.get(k, 0) >= v:
                continue
            seen[k] = v
            waits.append((k, v))
        return waits

    def _commit(self, tok, reads, writes):
        k, v = tok
        for b in reads:
            if b.r.get(k, 0) < v:
                b.r[k] = v
        for b in writes:
            b.w = tok
            b.r = {}

    def op(self, eng, fn, reads=(), writes=()):
        waits = self._deps(eng, reads, writes)
        self.cnt[eng] += 1
        tok = (eng, self.cnt[eng])
        self.q[eng].append((waits, fn, tok, 1))
        self._commit(tok, reads, writes)
        self.n_ops += 1
        return tok

    def dma(self, eng, out_ap, in_ap, reads=(), writes=(), **kw):
        s = self.dma_k % self.n_dma_sems
        self.dma_k += 1
        prev = self.dma_last.get(s, 0)
        extra = [(("d", s), prev)] if prev else []
        waits = self._deps(eng, reads, writes, extra)
        val = prev + 16
        self.dma_last[s] = val
        tok = (("d", s), val)

        def fn(e, out_ap=out_ap, in_ap=in_ap, kw=kw):
            return e.dma_start(out=out_ap, in_=in_ap, **kw)

        self.q[eng].append((waits, fn, tok, 16))
        self._commit(tok, reads, writes)
        self.n_ops += 1
        return tok

    def barrier(self):
        toks = [(k, self.cnt[k]) for k in ("pe", "act", "dve", "pool") if self.cnt[k]]
        toks += [(("d", s), v) for s, v in self.dma_last.items()]
        for eng in self.ENG:
            waits = []
            seen = self.seen[eng]
            for k, v in toks:
                if seen.get(k, 0) >= v:
                    continue
                seen[k] = v
                waits.append((k, v))
            if waits:
                self.q[eng].append((waits, None, None, 0))

    def emit(self):
        nc = self.nc
        with ExitStack() as es:
            sems = {}
            for k in ("pe", "act", "dve", "pool"):
                sems[k] = es.enter_context(nc.semaphore("s_" + k))
            for s in range(min(self.n_dma_sems, max(1, self.dma_k))):
                sems[("d", s)] = es.enter_context(nc.semaphore("s_d%d" % s))
            fin = [(("d", s), v) for s, v in self.dma_last.items()]
            self.q["sp"].append((fin, None, None, 0))
            block = es.enter_context(nc.Block())

            def run(name, e):
                for waits, fn, tok, inc in self.q[name]:
                    for k, v in waits:
                        e.wait_ge(sems[k], v)
                    if fn is not None:
                        ins = fn(e)
                        ins.then_inc(sems[tok[0]], inc)

            @block.sync
            def _(e):
                run("sp", e)

            @block.tensor
            def _(e):
                run("pe", e)

            @block.scalar
            def _(e):
                run("act", e)

            @block.vector
            def _(e):
                run("dve", e)

            @block.gpsimd
            def _(e):
                run("pool", e)


class Ring:
    def __init__(self, tiles):
        self.tiles = tiles
        self.bufs = [Buf() for _ in tiles]
        self.i = 0

    def next(self):
        t, b = self.tiles[self.i], self.bufs[self.i]
        self.i = (self.i + 1) % len(self.tiles)
        return t, b


class K:
    def __init__(self, stages=None, layers=None, dump=False):
        self.stages = stages
        self.layers = list(range(DEPTH)) if layers is None else layers
        self.dump = dump
        nc = self.nc = bass.Bass("TRN2", target_bir_lowering=False)
        self.P = Prog(nc)
        self.es = ExitStack()
        self.din = {}
        self.n_alloc = 0

    def dram_in(self, name, shape):
        ap = self.nc.dram_tensor(name, list(shape), F32, kind="ExternalInput").ap()
        self.din[name] = ap
        return ap

    def sb(self, shape, dtype=F32, es=None, name=None):
        self.n_alloc += 1
        nm = name or ("t%d" % self.n_alloc)
        return (es or self.es).enter_context(self.nc.sbuf_tensor(nm, list(shape), dtype))

    def ring(self, n, shape, dtype=F32, es=None):
        return Ring([self.sb(shape, dtype, es) for _ in range(n)])

    def psum(self):
        return self.ps_ring.next()

    def load_cast(self, dst, b_dst, src, stg):
        F = dst.shape[-1]
        st, b_st = stg.next()
        self.P.dma("sp", st[:, :F], src, writes=[b_st])
        self.cp("pool", (dst, b_dst), (st[:, :F], b_st))

    def build(self):
        nc, P = self.nc, self.P
        self.xin = self.dram_in("xin", [D, T])
        self.cvec = self.dram_in("cvec", [128, KC, 2])
        self.wmod = self.dram_in("wmod", [DEPTH, 18, 128, KC * 512])
        self.bmod = self.dram_in("bmod", [DEPTH, 128, 72])
        self.ng2 = self.dram_in("ng2", [DEPTH, 128, 3 * KC * 2])
        self.fg = self.dram_in("fg", [128, KC])
        if self.stages is None or "ffn" in self.stages:
            self.w13 = self.dram_in("w13", [DEPTH, 2, NI, 128, KC * 256])
            self.w2 = self.dram_in("w2", [DEPTH, 2, KC, 128, NI * 128])
        self.out = nc.dram_tensor("out", [D, NLAT], F32, kind="ExternalOutput").ap()
        self.xT = nc.dram_tensor("xT_scratch", [D, T], F32, kind="Internal").ap()
        self.xT_v = self.xT.rearrange("(kc p) t -> p kc t", p=128)
        self.xin_v = self.xin.rearrange("(kc p) t -> p kc t", p=128)
        self.out_v = self.out.rearrange("(kc p) t -> p kc t", p=128)
        self.xbuf = [[Buf("x%d_%d" % (kc, t)) for t in range(len(TILES))] for kc in range(KC)]

        self.ps_ring = Ring([self.es.enter_context(nc.psum_tensor("ps%d" % i, [128, 512], F32))
                             for i in range(8)])
        self.ones_bf = self.sb([128, 128], BF16)
        self.b_const = Buf("const")
        P.op("dve", lambda e: e.memset(self.ones_bf[:], 1.0), writes=[self.b_const])
        self.eps_t = self.sb([128, 1])
        P.op("dve", lambda e: e.memset(self.eps_t[:], EPS), writes=[self.b_const])
        self.sc_bf = self.sb([128, KC, 2], BF16)
        self.b_sc = Buf()
        cv = self.sb([128, KC, 2])
        b_cv = Buf()
        P.dma("sp", cv[:], self.cvec, writes=[b_cv])
        P.op("act", lambda e: e.activation(out=self.sc_bf[:], in_=cv[:], func=AF.Silu),
             reads=[b_cv], writes=[self.b_sc])
        self.fg_t = self.sb([128, KC])
        self.b_fg = Buf()
        P.dma("sp", self.fg_t[:], self.fg, writes=[self.b_fg])
        self.modv = self.sb([128, 9, KC, 2])
        self.gs = self.sb([128, 3, KC, 2])
        self.geff = self.sb([128, 3, KC, 2])
        self.b_mod = Buf("mod")

        self.decl_mixer_inputs()
        first = True
        for l in self.layers:
            self.mod_stage(l)
            self.ffn_stage(l, 0, src_in=first)
            first = False
            self.mixer_stage(l)
            self.ffn_stage(l, 1)
        if self.dump:
            self.dump_stage()
        else:
            self.final_stage()
        P.emit()
        self.es.close()
        return nc

    def mod_stage(self, l):
        nc, P = self.nc, self.P
        es = ExitStack()
        wr = self.ring(2, [128, KC, 512], BF16, es)
        wstg = self.ring(2, [128, KC * 512], F32, es)
        bm = self.sb([128, 72], F32, es)
        ng = self.sb([128, 3, KC, 2], F32, es)
        b_bm, b_ng = Buf(), Buf()
        P.dma("sp", bm[:], self.bmod[l], writes=[b_bm])
        P.dma("sp", ng[:].rearrange("p a k s -> p (a k s)"), self.ng2[l], writes=[b_ng])
        ps, b_ps = self.psum()
        for blk in range(18):
            wt, b_w = wr.next()
            self.load_cast(wt[:].rearrange("p k c -> p (k c)"), b_w, self.wmod[l, blk], wstg)
            for fc in range(4):
                c = blk * 4 + fc
                for kc in range(KC):
                    P.op("pe", lambda e, wt=wt, fc=fc, kc=kc, c=c: e.matmul(
                        ps[:, 2 * c:2 * c + 2], wt[:, kc, fc * 128:(fc + 1) * 128], self.sc_bf[:, kc, :],
                        start=(kc == 0), stop=(kc == KC - 1)),
                        reads=[b_w, self.b_sc], writes=[b_ps])
        modv = self.modv
        for s in range(2):
            P.op("dve", lambda e, s=s: e.tensor_tensor(
                out=modv[:, :, :, s].rearrange("p m k -> p (m k)"),
                in0=ps[:, 0:144].rearrange("p (c s) -> p c s", s=2)[:, :, s],
                in1=bm[:], op=ALU.add),
                reads=[b_ps, b_bm], writes=[self.b_mod])
        for i in range(3):
            P.op("dve", lambda e, i=i: e.scalar_tensor_tensor(
                out=self.gs[:, i].rearrange("p k s -> p (k s)"),
                in0=modv[:, 3 * i + 1].rearrange("p k s -> p (k s)"), scalar=1.0,
                in1=ng[:, i].rearrange("p k s -> p (k s)"), op0=ALU.add, op1=ALU.mult),
                reads=[self.b_mod, b_ng], writes=[self.b_mod])
            gsc = 0.5 if i != 1 else 1.0
            P.op("dve", lambda e, i=i, gsc=gsc: e.tensor_scalar(
                out=self.geff[:, i].rearrange("p k s -> p (k s)"),
                in0=modv[:, 3 * i + 2].rearrange("p k s -> p (k s)"),
                scalar1=gsc, scalar2=None, op0=ALU.mult),
                reads=[self.b_mod], writes=[self.b_mod])
        P.barrier()
        es.close()

    def modulate(self, xt, b_x, n, i_norm, s, h_ap_fn, b_h, rs):
        P = self.P
        sq, b_sq = rs["sq"].next()
        P.op("act", lambda e: e.activation(out=sq[:, :, :n], in_=xt[:, :, :n], func=AF.Square),
             reads=[b_x], writes=[b_sq])
        ps, b_ps = self.psum()
        for kc in range(KC):
            P.op("pe", lambda e, kc=kc: e.matmul(ps[:, :n], self.ones_bf[:], sq[:, kc, :n],
                                                 start=(kc == 0), stop=(kc == KC - 1)),
                 reads=[b_sq, self.b_const], writes=[b_ps])
        rstd, b_r = rs["rstd"].next()
        P.op("act", lambda e: e.activation(out=rstd[:, :n], in_=ps[:, :n], func=AF.Sqrt,
                                           scale=1.0 / D, bias=self.eps_t[:, 0:1]),
             reads=[b_ps, self.b_const], writes=[b_r])
        P.op("dve", lambda e: e.reciprocal(out=rstd[:, :n], in_=rstd[:, :n]), reads=[b_r], writes=[b_r])
        for kc in range(KC):
            tmp, b_t = rs["tmp"].next()
            P.op("dve", lambda e, kc=kc, tmp=tmp: e.tensor_tensor(
                out=tmp[:, :n], in0=xt[:, kc, :n], in1=rstd[:, :n], op=ALU.mult),
                reads=[b_x, b_r], writes=[b_t])
            if i_norm is None:
                P.op("act", lambda e, kc=kc, tmp=tmp: e.activation(
                    out=h_ap_fn(kc), in_=tmp[:, :n], func=AF.Identity,
                    scale=self.fg_t[:, kc:kc + 1]),
                    reads=[b_t, self.b_fg], writes=[b_h])
            else:
                P.op("act", lambda e, kc=kc, tmp=tmp: e.activation(
                    out=h_ap_fn(kc), in_=tmp[:, :n], func=AF.Identity,
                    scale=self.gs[:, i_norm, kc, s:s + 1], bias=self.modv[:, 3 * i_norm, kc, s:s + 1]),
                    reads=[b_t, self.b_mod], writes=[b_h])

    def mk_scratch(self, es):
        return {"sq": self.ring(2, [128, KC, 512], BF16, es),
                "rstd": self.ring(2, [128, 512], F32, es),
                "tmp": self.ring(3, [128, 512], F32, es)}

    def load_x(self, xr, t, src_in=False):
        s0, n = TILES[t]
        xt, b_x = xr.next()
        src = self.xin_v if src_in else self.xT_v
        self.P.dma("sp", xt[:, :, :n], src[:, :, s0:s0 + n],
                   reads=([] if src_in else [self.xbuf[kc][t] for kc in range(KC)]), writes=[b_x])
        return xt, b_x

    def ffn_stage(self, l, f, src_in=False):
        if self.stages is not None and "ffn" not in self.stages:
            if src_in:
                self.copy_in()
            return
        nc, P = self.nc, self.P
        i_norm = 0 if f == 0 else 2
        es = ExitStack()
        rs = self.mk_scratch(es)
        xr = self.ring(1, [128, KC, 512], F32, es)
        wstg = self.ring(2, [128, NI * 128], F32, es)
        TS = 1536
        h = self.sb([128, KC, TS], BF16, es)
        act = self.sb([128, NI, TS], BF16, es)
        w13r = self.ring(3, [128, KC, 256], BF16, es)
        w2r = self.ring(2, [128, NI, 128], BF16, es)
        sar = self.ring(2, [128, 512], F32, es)
        xor_ = self.ring(3, [128, 512], F32, es)
        xnr = self.ring(3, [128, 512], F32, es)
        b_h = [Buf() for _ in range(3)]
        b_act = [[Buf() for _ in range(3)] for _ in range(NI)]
        src_v = self.xin_v if src_in else self.xT_v
        for st in SUPER:
            offs = []
            off = 0
            for ti, t in enumerate(st):
                s0, n = TILES[t]
                s = 1 if t == 0 else 0
                xt, b_x = self.load_x(xr, t, src_in)
                self.modulate(xt, b_x, n, i_norm, s,
                              (lambda kc, off=off, n=n: h[:, kc, off:off + n]), b_h[ti], rs)
                offs.append(off)
                off += n
            for i in range(NI):
                wt, b_w = w13r.next()
                self.load_cast(wt[:].rearrange("p k c -> p (k c)"), b_w, self.w13[l, f, i], wstg)
                for ti, t in enumerate(st):
                    s0, n = TILES[t]
                    off = offs[ti]
                    pa, b_pa = self.psum()
                    pb, b_pb = self.psum()
                    for (pp, b_pp, c0) in ((pa, b_pa, 0), (pb, b_pb, 128)):
                        for kc in range(KC):
                            P.op("pe", lambda e, pp=pp, c0=c0, kc=kc, wt=wt, off=off, n=n: e.matmul(
                                pp[:, :n], wt[:, kc, c0:c0 + 128], h[:, kc, off:off + n],
                                start=(kc == 0), stop=(kc == KC - 1)),
                                reads=[b_w, b_h[ti]], writes=[b_pp])
                    sa, b_sa = sar.next()
                    P.op("act", lambda e, sa=sa, pa=pa, n=n: e.activation(
                        out=sa[:, :n], in_=pa[:, :n], func=AF.Silu), reads=[b_pa], writes=[b_sa])
                    P.op("dve", lambda e, sa=sa, pb=pb, n=n, i=i, off=off: e.tensor_tensor(
                        out=act[:, i, off:off + n], in0=sa[:, :n], in1=pb[:, :n], op=ALU.mult),
                        reads=[b_sa, b_pb], writes=[b_act[i][ti]])
            for oc in range(KC):
                wt, b_w = w2r.next()
                self.load_cast(wt[:].rearrange("p i c -> p (i c)"), b_w, self.w2[l, f, oc], wstg)
                for ti, t in enumerate(st):
                    s0, n = TILES[t]
                    s = 1 if t == 0 else 0
                    off = offs[ti]
                    po, b_po = self.psum()
                    for i in range(NI):
                        P.op("pe", lambda e, po=po, i=i, wt=wt, off=off, n=n: e.matmul(
                            po[:, :n], wt[:, i, :], act[:, i, off:off + n],
                            start=(i == 0), stop=(i == NI - 1)),
                            reads=[b_w, b_act[i][ti]], writes=[b_po])
                    xo, b_xo = xor_.next()
                    P.dma("sp", xo[:, :n], src_v[:, oc, s0:s0 + n],
                          reads=([] if src_in else [self.xbuf[oc][t]]), writes=[b_xo])
                    xn, b_xn = xnr.next()
                    P.op("dve", lambda e, xn=xn, po=po, xo=xo, n=n, oc=oc, s=s: e.scalar_tensor_tensor(
                        out=xn[:, :n], in0=po[:, :n], scalar=self.geff[:, i_norm, oc, s:s + 1],
                        in1=xo[:, :n], op0=ALU.mult, op1=ALU.add),
                        reads=[b_po, b_xo, self.b_mod], writes=[b_xn])
                    P.dma("sp", self.xT_v[:, oc, s0:s0 + n], xn[:, :n], reads=[b_xn],
                          writes=[self.xbuf[oc][t]])
        P.barrier()
        es.close()

    def copy_in(self):
        P = self.P
        es = ExitStack()
        xr = self.ring(2, [128, KC, 512], F32, es)
        for t, (s0, n) in enumerate(TILES):
            xt, b_x = xr.next()
            P.dma("sp", xt[:, :, :n], self.xin_v[:, :, s0:s0 + n], writes=[b_x])
            P.dma("sp", self.xT_v[:, :, s0:s0 + n], xt[:, :, :n], reads=[b_x],
                  writes=[self.xbuf[kc][t] for kc in range(KC)])
        P.barrier()
        es.close()

    def mixer_stage(self, l):
        if self.stages is not None and "mix" not in self.stages:
            return
        if l % 2 == 1:
            self.s5_stage(l)
        else:
            self.even_stage(l)

    def dump_stage(self):
        P = self.P
        es = ExitStack()
        self.xdump = self.nc.dram_tensor("xdump", [D, T], F32, kind="ExternalOutput").ap()
        xd_v = self.xdump.rearrange("(kc p) t -> p kc t", p=128)
        xr = self.ring(2, [128, KC, 512], F32, es)
        for t in range(len(TILES)):
            s0, n = TILES[t]
            xt, b_x = self.load_x(xr, t)
            P.dma("sp", xd_v[:, :, s0:s0 + n], xt[:, :, :n], reads=[b_x])
        P.barrier()
        es.close()

    def tt(self, eng, out, a, b, op):
        self.P.op(eng, lambda e: e.tensor_tensor(out=out[0], in0=a[0], in1=b[0], op=op),
                  reads=[a[1], b[1]], writes=[out[1]])

    def ts(self, eng, out, a, s1, op0, s2=None, op1=None, sb=()):
        s1a = s1[0] if isinstance(s1, tuple) else s1
        s2a = s2[0] if isinstance(s2, tuple) else s2
        rd = [a[1]] + [x[1] for x in (s1, s2) if isinstance(x, tuple)]
        if op1 is None:
            self.P.op(eng, lambda e: e.tensor_scalar(out=out[0], in0=a[0], scalar1=s1a, scalar2=None, op0=op0),
                      reads=rd, writes=[out[1]])
        else:
            self.P.op(eng, lambda e: e.tensor_scalar(out=out[0], in0=a[0], scalar1=s1a, scalar2=s2a,
                                                     op0=op0, op1=op1), reads=rd, writes=[out[1]])

    def stt(self, out, a, sc, b, op0, op1):
        sca = sc[0] if isinstance(sc, tuple) else sc
        rd = [a[1], b[1]] + ([sc[1]] if isinstance(sc, tuple) else [])
        self.P.op("dve", lambda e: e.scalar_tensor_tensor(out=out[0], in0=a[0], scalar=sca, in1=b[0],
                                                          op0=op0, op1=op1), reads=rd, writes=[out[1]])

    def actf(self, out, a, func, scale=1.0, bias=None):
        sca = scale[0] if isinstance(scale, tuple) else scale
        rd = [a[1]] + [x[1] for x in (scale, bias) if isinstance(x, tuple)]
        if bias is None:
            self.P.op("act", lambda e: e.activation(out=out[0], in_=a[0], func=func, scale=sca),
                      reads=rd, writes=[out[1]])
        else:
            ba = bias[0] if isinstance(bias, tuple) else bias
            self.P.op("act", lambda e: e.activation(out=out[0], in_=a[0], func=func, scale=sca, bias=ba),
                      reads=rd, writes=[out[1]])

    def cp(self, eng, out, a):
        if eng == "act":
            self.P.op("act", lambda e: e.activation(out=out[0], in_=a[0], func=AF.Identity),
                      reads=[a[1]], writes=[out[1]])
        else:
            self.P.op(eng, lambda e: e.tensor_copy(out=out[0], in_=a[0]), reads=[a[1]], writes=[out[1]])

    def mm(self, ps, lhsT, rhs, start, stop):
        self.P.op("pe", lambda e: e.matmul(ps[0], lhsT[0], rhs[0], start=start, stop=stop),
                  reads=[lhsT[1], rhs[1]], writes=[ps[1]])


    def decl_mixer_inputs(self):
        if self.stages is not None and "mix" not in self.stages:
            return
        if any(l % 2 == 1 for l in self.layers):
            self.s5_win = self.dram_in("s5_win", [2, 8, 128, KC * 128])
            self.s5_lamA = self.dram_in("s5_lamA", [2, 8, 128, 3 * 512])
            self.s5_bA = self.dram_in("s5_bA", [2, 8, 128, 2 * 512])
            self.s5_lamB = self.dram_in("s5_lamB", [2, 128, 3 * 32])
            self.s5_cB = self.dram_in("s5_cB", [2, 16, 128, 2 * 2 * 128])
            self.s5_bB = self.dram_in("s5_bB", [2, 16, 128, 2 * 2 * 128])
            self.s5_dd = self.dram_in("s5_dd", [2, 8, 128, 128])
            self.s5_wout = self.dram_in("s5_wout", [2, 8, 128, 2048])
        if any(l % 2 == 0 for l in self.layers):
            self.decl_even_inputs()

    def discretize(self, lre, lim, lst, F, es, tag):
        PI = float(np.pi)

        def new(dtype=F32):
            return (self.sb([128, F], dtype, es)[:], Buf())

        cache = getattr(self, "_disc_cache", None)
        if cache is None:
            cache = self._disc_cache = {}
        key = (tag, F, id(es))
        if key not in cache:
            cache[key] = [new() for _ in range(9)] + [new(mybir.dt.int32)]
        A, B, C, Dd, E_, F_, G, H, I, KI = cache[key]
        dt, lr = A, B
        self.actf(dt, lst, AF.Exp)
        self.ts("dve", lr, lre, -1e-4, ALU.min)
        self.tt("dve", C, lr, dt, ALU.mult)
        mag = Dd
        self.actf(mag, C, AF.Exp)
        th = C
        self.tt("dve", th, lim, dt, ALU.mult)
        outs = []
        for sh, o in ((0.0, I), (PI / 2, A)):
            w, kf, r, m = E_, F_, G, H
            self.ts("dve", w, th, sh, ALU.add)
            self.ts("dve", KI, w, 1.0 / (2 * PI), ALU.mult)
            self.cp("dve", kf, KI)
            self.stt(r, kf, -2 * PI, w, ALU.mult, ALU.add)
            self.ts("dve", m, r, PI, ALU.is_gt, -2 * PI, ALU.mult)
            self.tt("dve", r, r, m, ALU.add)
            self.ts("dve", r, r, PI, ALU.min, -PI, ALU.max)
            self.actf(o, r, AF.Sin)
            outs.append(o)
        sn, cs = outs
        are, aim = E_, F_
        self.tt("dve", are, mag, cs, ALU.mult)
        self.tt("dve", aim, mag, sn, ALU.mult)
        self.tt("dve", G, lr, lr, ALU.mult)
        self.tt("dve", H, lim, lim, ALU.mult)
        self.tt("dve", H, H, G, ALU.add)
        self.P.op("dve", lambda e: e.reciprocal(out=H[0], in_=H[0]), reads=[H[1]], writes=[H[1]])
        nre = Dd
        self.ts("dve", nre, are, -1.0, ALU.add)
        cre, cim = I, A
        self.tt("dve", G, nre, lr, ALU.mult)
        self.tt("dve", C, aim, lim, ALU.mult)
        self.tt("dve", G, G, C, ALU.add)
        self.tt("dve", cre, G, H, ALU.mult)
        self.tt("dve", G, aim, lr, ALU.mult)
        self.tt("dve", C, nre, lim, ALU.mult)
        self.tt("dve", G, G, C, ALU.subtract)
        self.tt("dve", cim, G, H, ALU.mult)
        return {"are": are, "aim": aim, "cre": cre, "cim": cim, "t1": G, "t2": C}

    def cmul(self, orr, oi, ar, ai, br, bi, t1, t2):
        self.tt("dve", t1, ar, br, ALU.mult)
        self.tt("dve", t2, ai, bi, ALU.mult)
        self.tt("dve", orr, t1, t2, ALU.subtract)
        self.tt("dve", t1, ar, bi, ALU.mult)
        self.tt("dve", t2, ai, br, ALU.mult)
        self.tt("dve", oi, t1, t2, ALU.add)

    def s5_stage(self, l):
        nc, P = self.nc, self.P
        j = l // 2
        NCH = T // 8
        HALF = NCH // 2
        es = ExitStack()
        U = self.sb([128, 8, T], BF16, es)
        b_U = [Buf() for _ in range(8)]
        Uc = [U[:, tq, :].rearrange("p (c i) -> p c i", i=8) for tq in range(8)]

        es0 = ExitStack()
        rs = self.mk_scratch(es0)
        xr = self.ring(2, [128, KC, 512], F32, es0)
        hr = self.ring(2, [128, KC, 512], BF16, es0)
        win = self.sb([128, 8, KC, 128], BF16, es0)
        b_win = Buf()
        wstg0 = self.ring(2, [128, KC * 128], F32, es0)
        for tq in range(8):
            self.load_cast(win[:, tq].rearrange("p k c -> p (k c)"), b_win, self.s5_win[j, tq], wstg0)
        for t, (s0, n) in enumerate(TILES):
            sidx = 1 if t == 0 else 0
            xt, b_x = self.load_x(xr, t)
            h, b_h = hr.next()
            self.modulate(xt, b_x, n, 1, sidx, (lambda kc, h=h, n=n: h[:, kc, :n]), b_h, rs)
            for tq in range(8):
                ps, b_ps = self.psum()
                for kc in range(KC):
                    self.mm((ps[:, :n], b_ps), (win[:, tq, kc, :], b_win), (h[:, kc, :n], b_h),
                            kc == 0, kc == KC - 1)
                self.cp("act" if tq % 2 == 0 else "dve", (U[:, tq, s0:s0 + n], b_U[tq]), (ps[:, :n], b_ps))
        P.barrier()
        es0.close()

        esB = ExitStack()
        lamB = self.sb([128, 3, 32], F32, esB)
        b_lamB = Buf()
        P.dma("sp", lamB[:].rearrange("p a f -> p (a f)"), self.s5_lamB[j], writes=[b_lamB])
        dB = self.discretize((lamB[:, 0, :], b_lamB), (lamB[:, 1, :], b_lamB), (lamB[:, 2, :], b_lamB), 32, esB, "B")
        PW = self.sb([128, 9, 2, 32], F32, esB)
        NPW = self.sb([128, 9, 2, 32], F32, esB)
        b_PW = Buf()
        P.op("dve", lambda e: e.memset(PW[:, 0, 0, :], 1.0), writes=[b_PW])
        P.op("dve", lambda e: e.memset(PW[:, 0, 1, :], 0.0), writes=[b_PW])
        self.cp("dve", (PW[:, 1, 0, :], b_PW), dB["are"])
        self.cp("dve", (PW[:, 1, 1, :], b_PW), dB["aim"])
        for m in range(2, 9):
            self.cmul((PW[:, m, 0, :], b_PW), (PW[:, m, 1, :], b_PW), (PW[:, m - 1, 0, :], b_PW),
                      (PW[:, m - 1, 1, :], b_PW), dB["are"], dB["aim"], dB["t1"], dB["t2"])
        self.ts("dve", (NPW[:].rearrange("p m r f -> p (m r f)"), b_PW),
                (PW[:].rearrange("p m r f -> p (m r f)"), b_PW), -1.0, ALU.mult)
        ARAR = self.sb([128, 2, 32], F32, esB)
        NAIAI = self.sb([128, 2, 32], F32, esB)
        b_A2 = Buf()
        for d in range(2):
            src_r = PW[:, 8, 0, :].rearrange("p (q d) -> p q d", d=2)[:, :, d]
            src_i = PW[:, 8, 1, :].rearrange("p (q d) -> p q d", d=2)[:, :, d]
            nsrc_i = NPW[:, 8, 1, :].rearrange("p (q d) -> p q d", d=2)[:, :, d]
            self.cp("dve", (ARAR[:, d, 0:16], b_A2), (src_r, b_PW))
            self.cp("dve", (ARAR[:, d, 16:32], b_A2), (src_r, b_PW))
            self.cp("dve", (NAIAI[:, d, 0:16], b_A2), (nsrc_i, b_PW))
            self.cp("dve", (NAIAI[:, d, 16:32], b_A2), (src_i, b_PW))

        E = self.sb([128, 2, 32, NCH], BF16, esB)
        b_E = [Buf(), Buf()]
        es1 = ExitStack()
        rawA = self.ring(2, [128, 3, 256], F32, es1)
        rawb = self.ring(2, [128, 2, 256], F32, es1)
        EBr = self.ring(2, [128, 8, 2, 256], BF16, es1)
        cur = [self.sb([128, 2, 256], F32, es1) for _ in range(2)]
        b_cur = [Buf(), Buf()]
        lamA_v = self.s5_lamA.rearrange("j q p (a f) -> j q p a f", a=3)
        bA_v = self.s5_bA.rearrange("j q p (a f) -> j q p a f", a=2)
        for tq in range(8):
            for pr in range(2):
                Pp = 2 * tq + pr
                es_t = es1
                la, b_la = rawA.next()
                rb, b_rb = rawb.next()
                P.dma("sp", la[:], lamA_v[j, tq, :, :, pr * 256:(pr + 1) * 256], writes=[b_la])
                P.dma("sp", rb[:], bA_v[j, tq, :, :, pr * 256:(pr + 1) * 256], writes=[b_rb])
                dA = self.discretize((la[:, 0, :], b_la), (la[:, 1, :], b_la), (la[:, 2, :], b_la), 256, es_t, "A")
                EB, b_EB = EBr.next()
                self.cmul((cur[0][:, 0, :], b_cur[0]), (cur[0][:, 1, :], b_cur[0]), dA["cre"], dA["cim"],
                          (rb[:, 0, :], b_rb), (rb[:, 1, :], b_rb), dA["t1"], dA["t2"])
                for k in range(8):
                    c0, c1 = cur[k % 2], cur[(k + 1) % 2]
                    self.cp("act", (EB[:, k].rearrange("p r f -> p (r f)"), b_EB),
                            (c0[:].rearrange("p r f -> p (r f)"), b_cur[k % 2]))
                    if k < 7:
                        self.cmul((c1[:, 0, :], b_cur[(k + 1) % 2]), (c1[:, 1, :], b_cur[(k + 1) % 2]),
                                  dA["are"], dA["aim"], (c0[:, 0, :], b_cur[k % 2]), (c0[:, 1, :], b_cur[k % 2]),
                                  dA["t1"], dA["t2"])
                for d in range(2):
                    for ri in range(2):
                        for hf in range(2):
                            ps, b_ps = self.psum()
                            for ip in range(8):
                                k = 7 - ip if d == 0 else ip
                                col = d * 128
                                self.mm((ps[:, :HALF], b_ps), (EB[:, k, ri, col:col + 128], b_EB),
                                        (Uc[tq][:, hf * HALF:(hf + 1) * HALF, ip], b_U[tq]), ip == 0, ip == 7)
                            self.cp("act" if (ri + hf) % 2 == 0 else "dve",
                                    (E[:, d, ri * 16 + Pp, hf * HALF:(hf + 1) * HALF], b_E[d]), (ps[:, :HALF], b_ps))
        P.barrier()
        es1.close()

        es2 = ExitStack()
        SS = [[self.sb([128, 48], F32, es2) for _ in range(2)] for _ in range(2)]
        b_SS = [[Buf(), Buf()], [Buf(), Buf()]]
        T1 = [self.ring(2, [128, 32], F32, es2) for _ in range(2)]
        T2 = [self.ring(2, [128, 32], F32, es2) for _ in range(2)]
        for d in range(2):
            for k in range(2):
                P.op("dve", lambda e, d=d, k=k: e.memset(SS[d][k][:], 0.0), writes=[b_SS[d][k]])
        order = [list(range(NCH)), list(range(31, -1, -1)) + list(range(NCH - 1, 31, -1))]
        for step in range(NCH):
            for d in range(2):
                c = order[d][step]
                cur, nxt = SS[d][step % 2], SS[d][(step + 1) % 2]
                b_cur, b_nxt = b_SS[d][step % 2], b_SS[d][(step + 1) % 2]
                t1, b_t1 = T1[d].next()
                t2, b_t2 = T2[d].next()
                ec = (E[:, d, :, c], b_E[d])
                self.tt("dve", (t1[:], b_t1), (ARAR[:, d, :], b_A2), (cur[:, 0:32], b_cur), ALU.mult)
                self.tt("dve", (t2[:], b_t2), (NAIAI[:, d, :], b_A2), (cur[:, 16:48], b_cur), ALU.mult)
                self.tt("dve", (t1[:], b_t1), (t1[:], b_t1), (t2[:], b_t2), ALU.add)
                self.tt("dve", (nxt[:, 0:32], b_nxt), (t1[:], b_t1), ec, ALU.add)
                self.cp("act", ec, (cur[:, 0:32], b_cur))
                self.cp("act", (nxt[:, 32:48], b_nxt), (nxt[:, 0:16], b_nxt))
        P.barrier()
        es2.close()

        es3 = ExitStack()
        craw = self.ring(2, [128, 2, 2, 128], F32, es3)
        braw = self.ring(2, [128, 2, 2, 128], F32, es3)
        CAr = self.ring(2, [128, 2, 9, 2, 128], BF16, es3)
        BBr = self.ring(2, [128, 2, 2, 128], BF16, es3)
        Ktr = self.ring(2, [128, 15, 128], BF16, es3)
        ddr = self.ring(2, [128, 128], F32, es3)
        zst = self.ring(1, [128, T], BF16, es3)
        tmpc = self.ring(8, [128, 128], F32, es3)
        for tq in range(8):
            CA = []; BB = []
            for pr in range(2):
                Pp = 2 * tq + pr
                cr, b_cr = craw.next()
                br_, b_br = braw.next()
                P.dma("sp", cr[:].rearrange("p a d f -> p (a d f)"), self.s5_cB[j, Pp], writes=[b_cr])
                P.dma("sp", br_[:].rearrange("p a d f -> p (a d f)"), self.s5_bB[j, Pp], writes=[b_br])
                ca, b_ca = CAr.next()
                bb, b_bb = BBr.next()
                for d in range(2):
                    col = Pp * 2 + d
                    cre = (dB["cre"][0][:, col:col + 1], dB["cre"][1])
                    cim = (dB["cim"][0][:, col:col + 1], dB["cim"][1])
                    tA, b_tA = tmpc.next()
                    self.actf((tA[:], b_tA), (br_[:, 0, d, :], b_br), AF.Identity, cre)
                    ncim, b_nc = tmpc.next()
                    self.ts("dve", (ncim[:, 0:1], b_nc), cim, -1.0, ALU.mult)
                    self.stt((bb[:, d, 0, :], b_bb), (br_[:, 1, d, :], b_br), (ncim[:, 0:1], b_nc), (tA[:], b_tA),
                             ALU.mult, ALU.add)
                    tB, b_tB = tmpc.next()
                    self.actf((tB[:], b_tB), (br_[:, 1, d, :], b_br), AF.Identity, cre)
                    self.stt((bb[:, d, 1, :], b_bb), (br_[:, 0, d, :], b_br), cim, (tB[:], b_tB), ALU.mult, ALU.add)
                    for m in range(9):
                        are_m = (PW[:, m, 0, col:col + 1], b_PW)
                        aim_m = (PW[:, m, 1, col:col + 1], b_PW)
                        nare_m = (NPW[:, m, 0, col:col + 1], b_PW)
                        naim_m = (NPW[:, m, 1, col:col + 1], b_PW)
                        tC, b_tC = tmpc.next()
                        self.actf((tC[:], b_tC), (cr[:, 0, d, :], b_cr), AF.Identity, are_m)
                        self.stt((ca[:, d, m, 0, :], b_ca), (cr[:, 1, d, :], b_cr), naim_m, (tC[:], b_tC),
                                 ALU.mult, ALU.add)
                        tD, b_tD = tmpc.next()
                        self.actf((tD[:], b_tD), (cr[:, 0, d, :], b_cr), AF.Identity, naim_m)
                        self.stt((ca[:, d, m, 1, :], b_ca), (cr[:, 1, d, :], b_cr), nare_m, (tD[:], b_tD),
                                 ALU.mult, ALU.add)
                CA.append((ca, b_ca)); BB.append((bb, b_bb))
            Kt, b_Kt = Ktr.next()
            dd, b_dd = ddr.next()
            P.dma("sp", dd[:], self.s5_dd[j, tq], writes=[b_dd])
            for d in range(2):
                for m in range(1, 8):
                    ps, b_ps = self.psum()
                    n_mm = 0
                    for pr in range(2):
                        for ri in range(2):
                            self.mm((ps[:, :128], b_ps), (BB[pr][0][:, d, ri, :], BB[pr][1]),
                                    (CA[pr][0][:, d, m, ri, :], CA[pr][1]), n_mm == 0, n_mm == 3)
                            n_mm += 1
                    mi = 7 + m if d == 0 else 7 - m
                    self.cp("act", (Kt[:, mi, :], b_Kt), (ps[:, :128], b_ps))
            ps, b_ps = self.psum()
            n_mm = 0
            for d in range(2):
                for pr in range(2):
                    for ri in range(2):
                        self.mm((ps[:, :128], b_ps), (BB[pr][0][:, d, ri, :], BB[pr][1]),
                                (CA[pr][0][:, d, 0, ri, :], CA[pr][1]), n_mm == 0, n_mm == 7)
                        n_mm += 1
            self.tt("dve", (Kt[:, 7, :], b_Kt), (ps[:, :128], b_ps), (dd[:], b_dd), ALU.add)
            z, b_z = zst.next()
            zc = z[:].rearrange("p (c i) -> p c i", i=8)
            for i in range(8):
                for hf in range(2):
                    ps, b_ps = self.psum()
                    n_mm = 0
                    for ip in range(8):
                        self.mm((ps[:, :HALF], b_ps), (Kt[:, 7 + (i - ip), :], b_Kt),
                                (Uc[tq][:, hf * HALF:(hf + 1) * HALF, ip], b_U[tq]), n_mm == 0, False)
                        n_mm += 1
                    for pr in range(2):
                        Pp = 2 * tq + pr
                        for d in range(2):
                            m = i + 1 if d == 0 else 8 - i
                            for ri in range(2):
                                n_mm += 1
                                self.mm((ps[:, :HALF], b_ps), (CA[pr][0][:, d, m, ri, :], CA[pr][1]),
                                        (E[:, d, ri * 16 + Pp, hf * HALF:(hf + 1) * HALF], b_E[d]),
                                        False, n_mm == 16)
                    self.actf((zc[:, hf * HALF:(hf + 1) * HALF, i], b_z), (ps[:, :HALF], b_ps), AF.Gelu_apprx_tanh)
            self.cp("pool", (U[:, tq, :], b_U[tq]), (z[:], b_z))
        P.barrier()
        es3.close()
        esB.close()

        es4 = ExitStack()
        wo = self.sb([128, 8, 2048], BF16, es4)
        b_wo = Buf()
        wstg4 = self.ring(2, [128, 2048], F32, es4)
        for tq in range(8):
            self.load_cast(wo[:, tq, :], b_wo, self.s5_wout[j, tq], wstg4)
        sgr = self.ring(2, [128, 512], F32, es4)
        yr = self.ring(2, [128, 512], F32, es4)
        xor_ = self.ring(3, [128, 512], F32, es4)
        xnr = self.ring(3, [128, 512], F32, es4)
        for t, (s0, n) in enumerate(TILES):
            sidx = 1 if t == 0 else 0
            for mo in range(8):
                pa, b_pa = self.psum()
                pg, b_pg = self.psum()
                for (pp, b_pp, c0) in ((pa, b_pa, mo * 128), (pg, b_pg, 1024 + mo * 128)):
                    for tq in range(8):
                        self.mm((pp[:, :n], b_pp), (wo[:, tq, c0:c0 + 128], b_wo), (U[:, tq, s0:s0 + n], b_U[tq]),
                                tq == 0, tq == 7)
                sg, b_sg = sgr.next()
                self.actf((sg[:, :n], b_sg), (pg[:, :n], b_pg), AF.Sigmoid)
                y, b_y = yr.next()
                self.tt("dve", (y[:, :n], b_y), (sg[:, :n], b_sg), (pa[:, :n], b_pa), ALU.mult)
                self.residual_out(y, b_y, mo, t, xor_, xnr)
        P.barrier()
        es4.close()
        es.close()

    def residual_out(self, y, b_y, oc, t, xor_, xnr):
        P = self.P
        s0, n = TILES[t]
        sidx = 1 if t == 0 else 0
        xo, b_xo = xor_.next()
        P.dma("sp", xo[:, :n], self.xT_v[:, oc, s0:s0 + n], reads=[self.xbuf[oc][t]], writes=[b_xo])
        xn, b_xn = xnr.next()
        self.stt((xn[:, :n], b_xn), (y[:, :n], b_y), (self.geff[:, 1, oc, sidx:sidx + 1], self.b_mod),
                 (xo[:, :n], b_xo), ALU.mult, ALU.add)
        P.dma("sp", self.xT_v[:, oc, s0:s0 + n], xn[:, :n], reads=[b_xn], writes=[self.xbuf[oc][t]])


    def decl_even_inputs(self):
        nc = self.nc
        self.ev_win = self.dram_in("ev_win", [2, 128, KC, 3088])
        self.ev_convw = self.dram_in("ev_convw", [2, 128, 4, 31])
        self.ev_cvec = self.dram_in("ev_cvec", [2, 128, 3, 4])
        self.ev_sconv = self.dram_in("ev_sconv", [2, 128, 12, 5])
        self.ev_dh = self.dram_in("ev_dh", [2, 64, 2, 8])
        self.ev_og = self.dram_in("ev_og", [2, 128, 1])
        self.ev_wout = self.dram_in("ev_wout", [2, 128, KC, 1024])
        self.ev_const = self.dram_in("ev_const", [128, 6, 128])
        def scr(name, n):
            return nc.dram_tensor(name, [n, 128, T], F32, kind="Internal").ap()
        self.d_qkvraw = scr("d_qkvraw", 12)
        self.d_qkvf = scr("d_qkvf", 12)
        self.d_gate = scr("d_gate", 4)
        self.d_conv = scr("d_conv", 4)
        self.d_ot = [scr("d_otf", 4), scr("d_otb", 4)]
        nt = len(TILES)
        self.bq_raw = [Buf() for _ in range(nt)]
        self.bq_f = [Buf() for _ in range(nt)]
        self.bq_gate = [Buf() for _ in range(nt)]
        self.bq_conv = [Buf() for _ in range(nt)]
        self.bq_ot = [[Buf() for _ in range(nt)] for _ in range(2)]

    def even_stage(self, l):
        P = self.P
        j = l // 2
        esL = ExitStack()
        cst = self.sb([128, 6, 128], F32, esL)
        b_cst = Buf()
        P.dma("sp", cst[:].rearrange("p a f -> p (a f)"), self.ev_const.rearrange("p a f -> p (a f)"), writes=[b_cst])
        self.cst, self.b_cst = cst, b_cst
        self.LA = self.sb([64, 68, 8], F32, esL)
        self.BETA = self.sb([64, 68, 8], F32, esL)
        self.b_lab = Buf()
        import os
        ph = os.environ.get("EVPH", "0a,0b,1,2,3").split(",")
        if "0a" in ph:
            self.ev_e0a(l, j)
        if "0b" in ph:
            self.ev_e0b(l, j)
        if "1" in ph:
            self.ev_e1(l, j)
        if "2" in ph:
            self.ev_e2(l, j, esL)
        if "3" in ph:
            self.ev_e3(l, j)
        P.barrier()
        esL.close()

    def ev_e0a(self, l, j):
        P = self.P
        es = ExitStack()
        rs = self.mk_scratch(es)
        xr = self.ring(2, [128, KC, 512], F32, es)
        hr = self.ring(2, [128, KC, 512], BF16, es)
        NW = 2064
        w = self.sb([128, KC, NW], BF16, es)
        b_w = Buf()
        wstg = self.ring(2, [128, NW], F32, es)
        for kc in range(KC):
            self.load_cast(w[:, kc, :], b_w, self.ev_win[j, :, kc, 1024:3088], wstg)
        dh = self.sb([64, 2, 8], F32, es)
        b_dh = Buf()
        P.dma("sp", dh[:].rearrange("p a f -> p (a f)"), self.ev_dh[j].rearrange("p a f -> p (a f)"), writes=[b_dh])
        nA = self.sb([64, 8], F32, es)
        self.actf((nA[:], b_dh), (dh[:, 0, :], b_dh), AF.Exp)
        self.ts("dve", (nA[:], b_dh), (nA[:], b_dh), -1.0, ALU.mult)
        stg = self.ring(4, [128, 512], F32, es)
        xa = self.sb([64, 8, 8], F32, es)
        b_xa = Buf()
        for t, (s0, n) in enumerate(TILES):
            sidx = 1 if t == 0 else 0
            xt, b_x = self.load_x(xr, t)
            h, b_h = hr.next()
            self.modulate(xt, b_x, n, 1, sidx, (lambda kc, h=h, n=n: h[:, kc, :n]), b_h, rs)
            for fc in range(16):
                ps, b_ps = self.psum()
                for kc in range(KC):
                    self.mm((ps[:, :n], b_ps), (w[:, kc, fc * 128:(fc + 1) * 128], b_w), (h[:, kc, :n], b_h),
                            kc == 0, kc == KC - 1)
                st, b_st = stg.next()
                if fc < 12:
                    self.cp("act" if fc % 2 == 0 else "dve", (st[:, :n], b_st), (ps[:, :n], b_ps))
                    P.dma("sp", self.d_qkvraw[fc, :, s0:s0 + n], st[:, :n], reads=[b_st], writes=[self.bq_raw[t]])
                else:
                    self.actf((st[:, :n], b_st), (ps[:, :n], b_ps), AF.Silu)
                    P.dma("sp", self.d_gate[fc - 12, :, s0:s0 + n], st[:, :n], reads=[b_st], writes=[self.bq_gate[t]])
            nchk = n // 64
            c0 = s0 // 64
            ps, b_ps = self.psum()
            for ci in range(nchk):
                for kc in range(KC):
                    self.mm((ps[0:64, ci * 16:(ci + 1) * 16], b_ps), (h[:, kc, ci * 64:(ci + 1) * 64], b_h),
                            (w[:, kc, 2048:2064], b_w), kc == 0, kc == KC - 1)
            ab = ps[0:64, 0:nchk * 16].rearrange("p (c d a h) -> p c d a h", d=2, a=2, h=4)
            for d in range(2):
                self.tt("dve", (xa[:, :nchk, d * 4:(d + 1) * 4], b_xa), (ab[:, :, d, 0, :], b_ps),
                        (dh[:, 1, d * 4:(d + 1) * 4].unsqueeze(1).to_broadcast([64, nchk, 4]), b_dh), ALU.add)
            self.actf((xa[:, :nchk, :], b_xa), (xa[:, :nchk, :], b_xa), AF.Exp)
            self.actf((xa[:, :nchk, :], b_xa), (xa[:, :nchk, :], b_xa), AF.Ln, 1.0, 1.0)
            self.tt("dve", (self.LA[:, c0:c0 + nchk, :], self.b_lab), (xa[:, :nchk, :], b_xa),
                    (nA[:].unsqueeze(1).to_broadcast([64, nchk, 8]), b_dh), ALU.mult)
            for d in range(2):
                self.actf((self.BETA[:, c0:c0 + nchk, d * 4:(d + 1) * 4], self.b_lab), (ab[:, :, d, 1, :], b_ps),
                          AF.Sigmoid)
        P.barrier()
        es.close()

    def ev_e0b(self, l, j):
        P = self.P
        es = ExitStack()
        rs = self.mk_scratch(es)
        xr = self.ring(2, [128, KC, 512], F32, es)
        hr = self.ring(2, [128, KC, 512], BF16, es)
        w = self.sb([128, KC, 1024], BF16, es)
        b_w = Buf()
        wstg = self.ring(2, [128, 1024], F32, es)
        for kc in range(KC):
            self.load_cast(w[:, kc, :], b_w, self.ev_win[j, :, kc, 0:1024], wstg)
        cw = self.sb([128, 4, 31], F32, es)
        cv = self.sb([128, 3, 4], F32, es)
        b_cw = Buf()
        P.dma("sp", cw[:].rearrange("p a f -> p (a f)"), self.ev_convw[j].rearrange("p a f -> p (a f)"), writes=[b_cw])
        P.dma("sp", cv[:].rearrange("p a f -> p (a f)"), self.ev_cvec[j].rearrange("p a f -> p (a f)"), writes=[b_cw])
        DG = self.sb([128, 4, 31, 128], BF16, es)
        b_DG = Buf()
        for cc in range(4):
            for k in range(31):
                self.ts("pool" if k % 2 else "dve", (DG[:, cc, k, :], b_DG), (self.cst[:, 0, :], self.b_cst),
                        (cw[:, cc, k:k + 1], b_cw), ALU.mult)
        sgr = self.ring(2, [128, 512], F32, es)
        cinr = self.ring(2, [128, 4, 752], BF16, es)
        hcr = self.ring(2, [128, 4, 512], F32, es)
        hbr = self.ring(2, [128, 2, 4, 512], BF16, es)
        mr = self.ring(6, [128, 512], F32, es)
        yr = self.ring(3, [128, 512], F32, es)
        for t, (s0, n) in enumerate(TILES):
            sidx = 1 if t == 0 else 0
            xt, b_x = self.load_x(xr, t)
            h, b_h = hr.next()
            self.modulate(xt, b_x, n, 1, sidx, (lambda kc, h=h, n=n: h[:, kc, :n]), b_h, rs)
            cin, b_cin = cinr.next()
            W = 256 if t == 0 else 64
            R = n // W
            WP = W + 30
            P.op("pool", lambda e, cin=cin: e.memset(cin[:].rearrange("p a f -> p (a f)"), 0.0), writes=[b_cin])
            for cc in range(4):
                pa, b_pa = self.psum()
                pg, b_pg = self.psum()
                for (pp, b_pp, c0) in ((pa, b_pa, cc * 128), (pg, b_pg, 512 + cc * 128)):
                    for kc in range(KC):
                        self.mm((pp[:, :n], b_pp), (w[:, kc, c0:c0 + 128], b_w), (h[:, kc, :n], b_h),
                                kc == 0, kc == KC - 1)
                sg, b_sg = sgr.next()
                self.actf((sg[:, :n], b_sg), (pg[:, :n], b_pg), AF.Sigmoid)
                civ = cin[:, cc, 0:R * WP].rearrange("p (r w) -> p r w", w=WP)[:, :, 15:15 + W]
                self.tt("dve", (civ, b_cin), (sg[:, :n].rearrange("p (r w) -> p r w", w=W), b_sg),
                        (pa[:, :n].rearrange("p (r w) -> p r w", w=W), b_pa), ALU.mult)
            hc, b_hc = hcr.next()
            hb, b_hb = hbr.next()
            if t == 0:
                groups = [(15, 256, 0, 1)]
            else:
                groups = [(15, 3 * WP + W, 0, 4), (4 * WP + 15, 3 * WP + W, 4, 4)]
            for cc in range(4):
                for (a0, N, r0, nr) in groups:
                    ps, b_ps = self.psum()
                    taps = [15] + [k for k in range(31) if k != 15]
                    for ti, k in enumerate(taps):
                        sft = k - 15
                        self.mm((ps[:, 0:N], b_ps), (DG[:, cc, k, :], b_DG), (cin[:, cc, a0 + sft:a0 + sft + N], b_cin),
                                ti == 0, ti == 30)
                    if t == 0:
                        src = ps[:, 0:256]
                        dst = hc[:, cc, 0:256]
                    else:
                        src = ps[:, 0:nr * WP].rearrange("p (r w) -> p r w", w=WP)[:, :, 0:W]
                        dst = hc[:, cc, r0 * W:(r0 + nr) * W].rearrange("p (r w) -> p r w", w=W)
                    self.actf((dst, b_hc), (src, b_ps), AF.Identity, 1.0, (cv[:, 0, cc:cc + 1], b_cw))
            self.cp("pool", (hb[:, 0, :, :n], b_hb), (hc[:, :, :n], b_hc))
            self.P.op("act", lambda e, hb=hb, hc=hc, n=n: e.activation(out=hb[:, 1, :, :n], in_=hc[:, :, :n],
                                                                      func=AF.Square), reads=[b_hc], writes=[b_hb])
            p1, b_p1 = self.psum()
            p2, b_p2 = self.psum()
            for (pp, b_pp, a) in ((p1, b_p1, 0), (p2, b_p2, 1)):
                for cc in range(4):
                    self.mm((pp[:, :n], b_pp), (self.ones_bf[:], self.b_const), (hb[:, a, cc, :n], b_hb),
                            cc == 0, cc == 3)
            mean, b_mean = mr.next()
            m2, b_m2 = mr.next()
            var, b_var = mr.next()
            self.ts("dve", (mean[:, :n], b_mean), (p1[:, :n], b_p1), 1.0 / 512, ALU.mult)
            self.tt("dve", (m2[:, :n], b_m2), (mean[:, :n], b_mean), (mean[:, :n], b_mean), ALU.mult)
            self.stt((var[:, :n], b_var), (p2[:, :n], b_p2), 1.0 / 512, (m2[:, :n], b_m2), ALU.mult, ALU.subtract)
            self.ts("dve", (var[:, :n], b_var), (var[:, :n], b_var), 0.0, ALU.max)
            self.actf((var[:, :n], b_var), (var[:, :n], b_var), AF.Sqrt, 1.0, (self.eps_t[:, 0:1], self.b_const))
            self.P.op("dve", lambda e, var=var, n=n: e.reciprocal(out=var[:, :n], in_=var[:, :n]),
                      reads=[b_var], writes=[b_var])
            for cc in range(4):
                y, b_y = yr.next()
                self.tt("dve", (y[:, :n], b_y), (hc[:, cc, :n], b_hc), (mean[:, :n], b_mean), ALU.subtract)
                self.tt("dve", (y[:, :n], b_y), (y[:, :n], b_y), (var[:, :n], b_var), ALU.mult)
                self.actf((y[:, :n], b_y), (y[:, :n], b_y), AF.Silu, (cv[:, 1, cc:cc + 1], b_cw),
                          (cv[:, 2, cc:cc + 1], b_cw))
                P.dma("sp", self.d_conv[cc, :, s0:s0 + n], y[:, :n], reads=[b_y], writes=[self.bq_conv[t]])
        P.barrier()
        es.close()

    def ev_e1(self, l, j):
        P = self.P
        es = ExitStack()
        sw = self.sb([128, 12, 5], F32, es)
        b_sw = Buf()
        P.dma("sp", sw[:].rearrange("p a f -> p (a f)"), self.ev_sconv[j].rearrange("p a f -> p (a f)"), writes=[b_sw])
        Rr = self.ring(2, [128, 12, 516], F32, es)
        accr = self.ring(2, [128, 12, 512], F32, es)
        prodr = self.ring(10, [128, 512], F32, es)
        b_accc = [[Buf() for _ in range(12)] for _ in range(2)]
        sqr = self.ring(2, [128, 8, 512], BF16, es)
        rnr = self.ring(3, [128, 512], F32, es)
        nt = len(TILES)
        for t, (s0, n) in enumerate(TILES):
            seg0, seg1 = (0, NCTX) if t == 0 else (NCTX, T)
            lo, hi = max(seg0, s0 - 2), min(seg1, s0 + n + 2)
            Rt, b_R = Rr.next()
            P.op("pool", lambda e, Rt=Rt: e.memset(Rt[:].rearrange("p a f -> p (a f)"), 0.0), writes=[b_R])
            nb = [self.bq_raw[tt_] for tt_ in (t - 1, t, t + 1) if 0 <= tt_ < nt]
            P.dma("sp", Rt[:, :, lo - (s0 - 2):hi - (s0 - 2)],
                  self.d_qkvraw.rearrange("a p t -> p a t")[:, :, lo:hi], reads=nb, writes=[b_R])
            acc, b_acc = accr.next()
            self.P.op("dve", lambda e, acc=acc: e.memset(acc[:, 0, 0:1], 0.0), writes=[b_acc])
            for cc in range(12):
                prods = []
                for k in range(5):
                    pk, b_pk = prodr.next()
                    self.actf((pk[:, :n], b_pk), (Rt[:, cc, k:k + n], b_R), AF.Identity, (sw[:, cc, k:k + 1], b_sw))
                    prods.append((pk, b_pk))
                b_ac = b_accc[accr.i][cc]
                self.P.op("dve", lambda e, acc=acc, cc=cc, n=n, p0=prods[0][0], p1=prods[1][0]: e.tensor_tensor(
                    out=acc[:, cc, :n], in0=p0[:, :n], in1=p1[:, :n], op=ALU.add),
                    reads=[prods[0][1], prods[1][1], b_acc], writes=[b_ac])
                for k in range(2, 5):
                    self.tt("dve", (acc[:, cc, :n], b_ac), (acc[:, cc, :n], b_ac), (prods[k][0][:, :n], prods[k][1]),
                            ALU.add)
            self.P.op("dve", lambda e, acc=acc: e.tensor_copy(out=acc[:, 0, 0:1], in_=acc[:, 0, 0:1]),
                      reads=[b_accc[accr.i][cc] for cc in range(12)], writes=[b_acc])
            self.P.op("act", lambda e, acc=acc, n=n: e.activation(out=acc[:, :, :n], in_=acc[:, :, :n], func=AF.Silu),
                      reads=[b_acc], writes=[b_acc])
            sq, b_sq = sqr.next()
            self.P.op("act", lambda e, acc=acc, sq=sq, n=n: e.activation(out=sq[:, :, :n], in_=acc[:, 0:8, :n],
                                                                      func=AF.Square), reads=[b_acc], writes=[b_sq])
            for cc in range(8):
                ps, b_ps = self.psum()
                self.mm((ps[:, :n], b_ps), (self.ones_bf[:], self.b_const), (sq[:, cc, :n], b_sq), True, True)
                rn, b_rn = rnr.next()
                self.actf((rn[:, :n], b_rn), (ps[:, :n], b_ps), AF.Sqrt, 1.0, (self.eps_t[:, 0:1], self.b_const))
                self.P.op("dve", lambda e, rn=rn, n=n: e.reciprocal(out=rn[:, :n], in_=rn[:, :n]),
                          reads=[b_rn], writes=[b_rn])
                if cc < 4:
                    self.stt((acc[:, cc, :n], b_acc), (acc[:, cc, :n], b_acc), 128.0 ** -0.5, (rn[:, :n], b_rn),
                             ALU.mult, ALU.mult)
                else:
                    self.tt("dve", (acc[:, cc, :n], b_acc), (acc[:, cc, :n], b_acc), (rn[:, :n], b_rn), ALU.mult)
            P.dma("sp", self.d_qkvf.rearrange("a p t -> p a t")[:, :, s0:s0 + n], acc[:, :, :n], reads=[b_acc],
                  writes=[self.bq_f[t]])
        P.barrier()
        es.close()


    def ev_e2(self, l, j, esL):
        P = self.P
        es = ExitStack()
        cst, b_c = self.cst, self.b_cst
        I64 = (cst[0:64, 0, 0:64], b_c)
        ID128 = (cst[:, 0, :], b_c)
        ones64 = (cst[0:64, 5, 0:64], b_c)
        ones64x128 = (cst[0:64, 5, :], b_c)
        UM = [(cst[0:64, 1, 0:64], b_c), (cst[0:64, 2, 0:64], b_c)]
        SM = [(cst[0:64, 3, 0:64], b_c), (cst[0:64, 4, 0:64], b_c)]

        def bc_h(ap):
            return ap.unsqueeze(1).to_broadcast([64, 4, 64])

        GTK = self.sb([64, 68, 8], F32, es)
        EGT = self.sb([64, 68, 8], F32, es)
        BG = self.sb([64, 68, 8], F32, es)
        ETAIL = self.sb([64, 68, 8], F32, es)
        NBETA = self.sb([64, 68, 8], F32, es)
        EGL = self.sb([128, 68, 8], F32, es)
        b_tab = Buf()
        LAb = (self.LA, self.b_lab)
        for d in range(2):
            hd = slice(d * 4, d * 4 + 4)
            ps, b_ps = self.psum()
            self.mm((ps[0:64, 0:272], b_ps), UM[d], (self.LA[:, :, hd], self.b_lab), True, True)
            self.cp("dve", (GTK[:, :, hd], b_tab), (ps[0:64, 0:272].rearrange("p (c h) -> p c h", h=4), b_ps))
            ps2, b_ps2 = self.psum()
            self.mm((ps2[:, 0:272], b_ps2), ones64x128, (self.LA[:, :, hd], self.b_lab), True, True)
            self.actf((EGL[:, :, hd], b_tab), (ps2[:, 0:272].rearrange("p (c h) -> p c h", h=4), b_ps2), AF.Exp)
            self.tt("dve", (ETAIL[:, :, hd], b_tab), (ps2[0:64, 0:272].rearrange("p (c h) -> p c h", h=4), b_ps2),
                    (GTK[:, :, hd], b_tab), ALU.subtract)
        fl = lambda tl: tl[:].rearrange("p c h -> p (c h)")
        self.ts("dve", (fl(ETAIL), b_tab), (fl(ETAIL), b_tab), 0.0, ALU.min)
        self.actf((fl(ETAIL), b_tab), (fl(ETAIL), b_tab), AF.Exp)
        self.actf((fl(EGT), b_tab), (fl(GTK), b_tab), AF.Exp)
        self.tt("dve", (fl(BG), b_tab), (fl(self.BETA), self.b_lab), (fl(EGT), b_tab), ALU.mult)
        self.ts("dve", (fl(NBETA), b_tab), (fl(self.BETA), self.b_lab), -1.0, ALU.mult)

        def R(n, shape, dt=F32):
            return self.ring(n, shape, dt, es)
        qkvr = R(4, [128, 12, 64]); qkbr = R(4, [128, 8, 64], BF16)
        L1r = R(4, [64, 4, 64]); L1nr = R(4, [64, 4, 64])
        egr = R(4, [128, 4, 64])
        dsr = R(4, [64, 4, 64]); dir_ = R(4, [64, 4, 64])
        ktr = R(4, [64, 4, 128], BF16); kbgr = R(4, [64, 4, 128]); vbr = R(4, [64, 4, 128], BF16)
        pmr = R(6, [64, 4, 64]); ptr = R(6, [64, 4, 64]); xtr = R(6, [64, 4, 64])
        qktr = R(4, [64, 4, 64], BF16); ttbr = R(4, [64, 4, 64], BF16)
        nwr = R(4, [128, 4, 64], BF16); qdr = R(4, [128, 4, 64], BF16)
        vnr = R(4, [64, 4, 128], BF16)
        S = [self.sb([128, 4, 128], F32, es) for _ in range(2)]
        Sb = [self.sb([128, 4, 128], BF16, es) for _ in range(2)]
        b_S = [Buf(), Buf()]; b_Sb = [Buf(), Buf()]
        OTs = [R(2, [128, 4, 512]), R(2, [128, 4, 512])]
        for d in range(2):
            P.op("dve", lambda e, d=d: e.memset(S[d][:].rearrange("p h v -> p (h v)"), 0.0), writes=[b_S[d]])
            P.op("pool", lambda e, d=d: e.memset(Sb[d][:].rearrange("p h v -> p (h v)"), 0.0), writes=[b_Sb[d]])
        qkvf_v = self.d_qkvf.rearrange("a p t -> p a t")

        def tile_of_chunk(c):
            return 0 if c < 4 else 1 + (c - 4) // 8

        import os as _os
        PPN = float(_os.environ.get("PPN", "99"))

        class _Stop(Exception):
            pass

        def sec(k):
            if k > PPN:
                raise _Stop()

        def prepass(c, d):
            hd = slice(d * 4, d * 4 + 4)
            t = tile_of_chunk(c)
            qkv, b_q = qkvr.next()
            P.dma("sp", qkv[:], qkvf_v[:, :, c * 64:(c + 1) * 64], reads=[self.bq_f[t]], writes=[b_q])
            qkb, b_qb = qkbr.next()
            self.cp("pool", (qkb[:], b_qb), (qkv[:, 0:8, :], b_q))
            L1, b_L1 = L1r.next()
            self.tt("dve", (L1[:], b_L1), (bc_h(UM[d][0]), b_c),
                    (self.LA[:, c, hd].unsqueeze(2).to_broadcast([64, 4, 64]), self.b_lab), ALU.mult)
            sec(2)
            ps, b_ps = self.psum()
            self.mm((ps[:, 0:256], b_ps), ones64x128, (L1[:].rearrange("p h i -> p (h i)"), b_L1), True, True)
            eg, b_eg = egr.next()
            self.actf((eg[:].rearrange("p h i -> p (h i)"), b_eg), (ps[:, 0:256], b_ps), AF.Exp)
            qd, b_qd = qdr.next()
            self.tt("dve", (qd[:], b_qd), (qkv[:, 0:4, :], b_q), (eg[:], b_eg), ALU.mult)
            sec(3)
            gp, b_gp = L1nr.next()
            self.tt("dve", (gp[:], b_gp), (ps[0:64, 0:256].rearrange("p (h i) -> p h i", h=4), b_ps),
                    (GTK[:, c, hd].unsqueeze(2).to_broadcast([64, 4, 64]), b_tab), ALU.subtract)
            ds, b_ds = dsr.next(); di, b_di = dir_.next()
            self.ts("dve", (ds[:], b_ds), (gp[:], b_gp), 0.0, ALU.max)
            self.actf((ds[:], b_ds), (ds[:], b_ds), AF.Exp, -1.0)
            self.tt("dve", (ds[:], b_ds), (ds[:], b_ds), (bc_h(SM[d][0]), b_c), ALU.mult)
            self.ts("dve", (di[:], b_di), (gp[:], b_gp), 0.0, ALU.min)
            self.actf((di[:], b_di), (di[:], b_di), AF.Exp)
            self.tt("pool", (di[:], b_di), (di[:], b_di), (bc_h(UM[d][0]), b_c), ALU.mult)
            sec(4)
            psk, b_psk = self.psum(); psv, b_psv = self.psum()
            for h in range(4):
                P.op("pe", lambda e, h=h, psk=psk, qkv=qkv: e.transpose(psk[0:64, h * 128:(h + 1) * 128],
                                                                      qkv[:, 4 + h, :], cst[:, 0, :]),
                     reads=[b_q, b_c], writes=[b_psk])
                P.op("pe", lambda e, h=h, psv=psv, qkv=qkv: e.transpose(psv[0:64, h * 128:(h + 1) * 128],
                                                                      qkv[:, 8 + h, :], cst[:, 0, :]),
                     reads=[b_q, b_c], writes=[b_psv])
            K3 = psk[0:64, :].rearrange("p (h k) -> p h k", h=4)
            V3 = psv[0:64, :].rearrange("p (h k) -> p h k", h=4)

            def bc_k(tl):
                return tl[:, c, hd].unsqueeze(2).to_broadcast([64, 4, 128])
            kt, b_kt = ktr.next(); kbg, b_kbg = kbgr.next(); vb, b_vb = vbr.next()
            self.tt("dve", (kt[:], b_kt), (K3, b_psk), (bc_k(ETAIL), b_tab), ALU.mult)
            self.tt("dve", (kbg[:], b_kbg), (K3, b_psk), (bc_k(BG), b_tab), ALU.mult)
            self.tt("dve", (vb[:], b_vb), (V3, b_psv), (bc_k(self.BETA), self.b_lab), ALU.mult)
            sec(5)
            pskk, b_pskk = self.psum(); psqk, b_psqk = self.psum()
            for h in range(4):
                self.mm((pskk[0:64, h * 64:(h + 1) * 64], b_pskk), (qkb[:, 4 + h, :], b_qb), (qkb[:, 4 + h, :], b_qb),
                        True, True)
                self.mm((psqk[0:64, h * 64:(h + 1) * 64], b_psqk), (qkb[:, 4 + h, :], b_qb), (qkb[:, h, :], b_qb),
                        True, True)
            Pm, b_Pm = pmr.next()
            self.tt("dve", (Pm[:], b_Pm), (pskk[0:64, 0:256].rearrange("p (h i) -> p h i", h=4), b_pskk),
                    (ds[:], b_ds), ALU.mult)
            self.tt("dve", (Pm[:], b_Pm), (Pm[:], b_Pm),
                    (NBETA[:, c, hd].unsqueeze(2).to_broadcast([64, 4, 64]), b_tab), ALU.mult)
            qkt, b_qkt = qktr.next()
            self.tt("dve", (qkt[:], b_qkt), (psqk[0:64, 0:256].rearrange("p (h i) -> p h i", h=4), b_psqk),
                    (di[:], b_di), ALU.mult)
            sec(6)
            pst, b_pst = self.psum()
            for h in range(4):
                self.mm((pst[0:64, h * 64:(h + 1) * 64], b_pst), (Pm[:, h, :], b_Pm), I64, True, True)
            sec(6.3)
            PT, b_PT = ptr.next()
            self.cp("act", (PT[:].rearrange("p h i -> p (h i)"), b_PT), (pst[0:64, 0:256], b_pst))
            sec(6.6)
            XT, b_XT = xtr.next()
            self.tt("dve", (XT[:], b_XT), (PT[:], b_PT), (bc_h(I64[0]), b_c), ALU.add)
            sec(7)
            for lvl in range(1, 6):
                psP, b_psP = self.psum()
                for h in range(4):
                    self.mm((psP[0:64, h * 64:(h + 1) * 64], b_psP), (PT[:, h, :], b_PT), (Pm[:, h, :], b_Pm), True, True)
                if lvl < 5:
                    psT, b_psT = self.psum()
                    for h in range(4):
                        self.mm((psT[0:64, h * 64:(h + 1) * 64], b_psT), (Pm[:, h, :], b_Pm), (PT[:, h, :], b_PT),
                                True, True)
                Pm2, b_Pm2 = pmr.next()
                self.cp("act", (Pm2[:].rearrange("p h i -> p (h i)"), b_Pm2), (psP[0:64, 0:256], b_psP))
                if lvl < 5:
                    PT2, b_PT2 = ptr.next()
                    self.cp("dve", (PT2[:].rearrange("p h i -> p (h i)"), b_PT2), (psT[0:64, 0:256], b_psT))
                psX, b_psX = self.psum()
                for h in range(4):
                    self.mm((psX[0:64, h * 64:(h + 1) * 64], b_psX), (Pm2[:, h, :], b_Pm2), (XT[:, h, :], b_XT), True, True)
                XT2, b_XT2 = xtr.next()
                self.tt("dve", (XT2[:].rearrange("p h i -> p (h i)"), b_XT2), (XT[:].rearrange("p h i -> p (h i)"), b_XT),
                        (psX[0:64, 0:256], b_psX), ALU.add)
                Pm, b_Pm = Pm2, b_Pm2
                if lvl < 5:
                    PT, b_PT = PT2, b_PT2
                XT, b_XT = XT2, b_XT2
            sec(8)
            ttb, b_ttb = ttbr.next()
            self.cp("act", (ttb[:].rearrange("p h i -> p (h i)"), b_ttb), (XT[:].rearrange("p h i -> p (h i)"), b_XT))
            psw, b_psw = self.psum()
            for h in range(4):
                self.mm((psw[:, h * 64:(h + 1) * 64], b_psw), (kbg[:, h, :], b_kbg), (XT[:, h, :], b_XT), True, True)
            nw, b_nw = nwr.next()
            self.actf((nw[:].rearrange("p h i -> p (h i)"), b_nw), (psw[:, 0:256], b_psw), AF.Identity, -1.0)
            return dict(ttb=(ttb, b_ttb), vb=(vb, b_vb), nw=(nw, b_nw), qd=(qd, b_qd), qkt=(qkt, b_qkt), kt=(kt, b_kt))

        cur_ot = [None, None]

        def step(c, d, pp):
            t = tile_of_chunk(c)
            s0, n = TILES[t]
            col = c * 64 - s0
            ttb, vb, nw, qd, qkt, kt = pp["ttb"], pp["vb"], pp["nw"], pp["qd"], pp["qkt"], pp["kt"]
            psv, b_psv = self.psum()
            for h in range(4):
                self.mm((psv[0:64, h * 128:(h + 1) * 128], b_psv), (ttb[0][:, h, :], ttb[1]), (vb[0][:, h, :], vb[1]),
                        True, False)
                self.mm((psv[0:64, h * 128:(h + 1) * 128], b_psv), (nw[0][:, h, :], nw[1]), (Sb[d][:, h, :], b_Sb[d]),
                        False, True)
            vn, b_vn = vnr.next()
            self.cp("act", (vn[:].rearrange("p h v -> p (h v)"), b_vn), (psv[0:64, :], b_psv))
            pso, b_pso = self.psum()
            for h in range(4):
                self.mm((pso[:, h * 64:(h + 1) * 64], b_pso), (Sb[d][:, h, :], b_Sb[d]), (qd[0][:, h, :], qd[1]),
                        True, False)
                self.mm((pso[:, h * 64:(h + 1) * 64], b_pso), (vn[:, h, :], b_vn), (qkt[0][:, h, :], qkt[1]),
                        False, True)
            first = (col == 0) if d == 0 else (col == n - 64)
            last = (col == n - 64) if d == 0 else (col == 0)
            if first:
                cur_ot[d] = OTs[d].next()
            ot, b_ot = cur_ot[d]
            self.cp("dve", (ot[:, :, col:col + 64], b_ot), (pso[:, 0:256].rearrange("p (h i) -> p h i", h=4), b_pso))
            if last:
                P.dma("sp", self.d_ot[d].rearrange("a p t -> p a t")[:, :, s0:s0 + n], ot[:, :, :n], reads=[b_ot],
                      writes=[self.bq_ot[d][t]])
            pss, b_pss = self.psum()
            for h in range(4):
                self.mm((pss[:, h * 128:(h + 1) * 128], b_pss), (kt[0][:, h, :], kt[1]), (vn[:, h, :], b_vn), True, True)
            for h in range(4):
                self.stt((S[d][:, h, :], b_S[d]), (S[d][:, h, :], b_S[d]), (EGL[:, c, d * 4 + h:d * 4 + h + 1], b_tab),
                         (pss[:, h * 128:(h + 1) * 128], b_pss), ALU.mult, ALU.add)
            self.cp("pool", (Sb[d][:].rearrange("p h v -> p (h v)"), b_Sb[d]), (S[d][:].rearrange("p h v -> p (h v)"), b_S[d]))

        order = [list(range(68)), [3, 2, 1, 0] + list(range(67, 3, -1))]
        import os
        ev2 = os.environ.get("EV2", "all")
        if ev2 == "tab":
            P.barrier(); es.close(); return
        if ev2 == "pre1":
            try:
                prepass(0, 0)
            except _Stop:
                pass
            P.barrier(); es.close(); return
        if ev2 == "pre":
            for sidx in range(68):
                prepass(order[0][sidx], 0); prepass(order[1][sidx], 1)
            P.barrier(); es.close(); return
        NST = int(os.environ.get("EV2N", "68"))
        pend = [prepass(order[0][0], 0), prepass(order[1][0], 1)]
        for sidx in range(NST):
            nxt = [None, None]
            if sidx + 1 < NST:
                nxt = [prepass(order[0][sidx + 1], 0), prepass(order[1][sidx + 1], 1)]
            step(order[0][sidx], 0, pend[0])
            step(order[1][sidx], 1, pend[1])
            pend = nxt
        P.barrier()
        es.close()

    def ev_e3(self, l, j):
        P = self.P
        es = ExitStack()
        wo = self.sb([128, KC, 1024], BF16, es)
        b_wo = Buf()
        wstg = self.ring(2, [128, 1024], F32, es)
        for kc in range(KC):
            self.load_cast(wo[:, kc, :], b_wo, self.ev_wout[j, :, kc, :], wstg)
        og = self.sb([128, 1], F32, es)
        b_og = Buf()
        P.dma("sp", og[:], self.ev_og[j], writes=[b_og])
        ofr = self.ring(2, [128, 4, 512], F32, es)
        obr = self.ring(2, [128, 4, 512], F32, es)
        gtr = self.ring(2, [128, 4, 512], F32, es)
        cvr = self.ring(2, [128, 4, 512], F32, es)
        sqr = self.ring(2, [128, 4, 512], BF16, es)
        mixr = self.ring(2, [128, 8, 512], BF16, es)
        rnr = self.ring(3, [128, 512], F32, es)
        xor_ = self.ring(3, [128, 512], F32, es)
        xnr = self.ring(3, [128, 512], F32, es)
        def issue_loads(t):
            s0, n = TILES[t]
            of, b_of = ofr.next(); ob, b_ob = obr.next(); gt, b_gt = gtr.next(); cv, b_cv = cvr.next()
            P.dma("sp", of[:, :, :n], self.d_ot[0].rearrange("a p t -> p a t")[:, :, s0:s0 + n],
                  reads=[self.bq_ot[0][t]], writes=[b_of])
            P.dma("sp", ob[:, :, :n], self.d_ot[1].rearrange("a p t -> p a t")[:, :, s0:s0 + n],
                  reads=[self.bq_ot[1][t]], writes=[b_ob])
            P.dma("sp", gt[:, :, :n], self.d_gate.rearrange("a p t -> p a t")[:, :, s0:s0 + n],
                  reads=[self.bq_gate[t]], writes=[b_gt])
            P.dma("sp", cv[:, :, :n], self.d_conv.rearrange("a p t -> p a t")[:, :, s0:s0 + n],
                  reads=[self.bq_conv[t]], writes=[b_cv])
            return (of, b_of, ob, b_ob, gt, b_gt, cv, b_cv)

        pending = issue_loads(0)
        for t, (s0, n) in enumerate(TILES):
            of, b_of, ob, b_ob, gt, b_gt, cv, b_cv = pending
            if t + 1 < len(TILES):
                pending = issue_loads(t + 1)
            self.tt("pool", (of[:, :, :n], b_of), (of[:, :, :n], b_of), (ob[:, :, :n], b_ob), ALU.add)
            sq, b_sq = sqr.next()
            self.P.op("act", lambda e, sq=sq, of=of, n=n: e.activation(out=sq[:, :, :n], in_=of[:, :, :n], func=AF.Square),
                      reads=[b_of], writes=[b_sq])
            mix, b_mix = mixr.next()
            self.cp("pool", (mix[:, 0:4, :n], b_mix), (cv[:, :, :n], b_cv))
            for h in range(4):
                ps, b_ps = self.psum()
                self.mm((ps[:, :n], b_ps), (self.ones_bf[:], self.b_const), (sq[:, h, :n], b_sq), True, True)
                rn, b_rn = rnr.next()
                self.actf((rn[:, :n], b_rn), (ps[:, :n], b_ps), AF.Sqrt, 1.0 / 128, (self.eps_t[:, 0:1], self.b_const))
                self.P.op("dve", lambda e, rn=rn, n=n: e.reciprocal(out=rn[:, :n], in_=rn[:, :n]),
                          reads=[b_rn], writes=[b_rn])
                self.tt("dve", (rn[:, :n], b_rn), (rn[:, :n], b_rn), (of[:, h, :n], b_of), ALU.mult)
                self.stt((mix[:, 4 + h, :n], b_mix), (rn[:, :n], b_rn), (og[:, 0:1], b_og), (gt[:, h, :n], b_gt),
                         ALU.mult, ALU.mult)
            for oc in range(KC):
                ps, b_ps = self.psum()
                for kc in range(KC):
                    self.mm((ps[:, :n], b_ps), (wo[:, kc, oc * 128:(oc + 1) * 128], b_wo), (mix[:, kc, :n], b_mix),
                            kc == 0, kc == KC - 1)
                self.residual_out(ps, b_ps, oc, t, xor_, xnr)
        P.barrier()
        es.close()

    def final_stage(self):
        P = self.P
        es = ExitStack()
        rs = self.mk_scratch(es)
        xr = self.ring(2, [128, KC, 512], F32, es)
        yr = self.ring(2, [128, KC, 512], F32, es)
        for t in range(1, len(TILES)):
            s0, n = TILES[t]
            xt, b_x = self.load_x(xr, t)
            y, b_y = yr.next()
            self.modulate(xt, b_x, n, None, 0, (lambda kc, y=y, n=n: y[:, kc, :n]), b_y, rs)
            P.dma("sp", self.out_v[:, :, s0 - NCTX:s0 - NCTX + n], y[:, :, :n], reads=[b_y])
        P.barrier()
        es.close()


def fm(v):
    return np.ascontiguousarray(v.reshape(KC, 128).T)


def prep_shared(inp, need=None):
    sh = {}
    w_mod = inp["w_mod"]
    sh["wmod"] = np.ascontiguousarray(
        w_mod.reshape(DEPTH, KC, 128, 18, 512).transpose(0, 3, 2, 1, 4)).reshape(DEPTH, 18, 128, KC * 512)
    sh["bmod"] = np.ascontiguousarray(inp["b_mod"].reshape(DEPTH, 72, 128).transpose(0, 2, 1))
    ng = inp["norm_g"].reshape(DEPTH, 3, KC, 128).transpose(0, 3, 1, 2)
    sh["ng2"] = np.ascontiguousarray(np.repeat(ng[..., None], 2, axis=-1)).reshape(DEPTH, 128, 48)
    sh["fg"] = fm(inp["final_g"])
    if need is not None and "s5_win" in need:
        sh.update(prep_s5(inp))
    if need is not None and "ev_win" in need:
        sh.update(prep_even(inp))
    if need is not None and "w13" not in need:
        return sh
    w13 = inp["ffn_w13"]
    a = w13[..., :DFF].reshape(DEPTH, 2, KC, 128, NI, 128)
    b = w13[..., DFF:].reshape(DEPTH, 2, KC, 128, NI, 128)
    ab = np.stack([a, b], axis=5)
    sh["w13"] = np.ascontiguousarray(ab.transpose(0, 1, 4, 3, 2, 5, 6)).reshape(DEPTH, 2, NI, 128, KC * 256)
    w2 = inp["ffn_w2"].reshape(DEPTH, 2, NI, 128, KC, 128)
    sh["w2"] = np.ascontiguousarray(w2.transpose(0, 1, 4, 3, 2, 5)).reshape(DEPTH, 2, KC, 128, NI * 128)
    return sh


def prep_s5(inp):
    o = {}
    w_in = inp["o_w_in"]; lre = inp["o_lam_re"]; lim = inp["o_lam_im"]; lst = inp["o_log_step"]
    bre = inp["o_b_re"]; bim = inp["o_b_im"]; cre = inp["o_c_re"]; cim = inp["o_c_im"]
    od = inp["o_d"]; wout = inp["o_w_out"]
    NJ = w_in.shape[0]
    win_s = np.zeros((NJ, 8, 128, KC, 128), np.float32)
    wout_s = np.zeros((NJ, 8, 128, 2048), np.float32)
    dd = np.zeros((NJ, 8, 128, 128), np.float32)
    lamA = np.zeros((NJ, 8, 128, 3, 2, 2, 2, 64), np.float32)
    bA = np.zeros((NJ, 8, 128, 2, 2, 2, 2, 64), np.float32)
    lamB = np.zeros((NJ, 2, 64, 3, 16, 2), np.float32)
    cB = np.zeros((NJ, 16, 2, 64, 2, 2, 4, 32), np.float32)
    bB = np.zeros((NJ, 16, 2, 64, 2, 2, 4, 32), np.float32)
    for j in range(NJ):
        wk = w_in[j].reshape(KC, 128, 512)
        for tq in range(8):
            for sl in range(4):
                g = 4 * tq + sl
                r0 = 32 * sl
                win_s[j, tq, :, :, r0:r0 + 16] = wk[:, :, 16 * g:16 * g + 16].transpose(1, 0, 2)
                wout_s[j, tq, r0:r0 + 16, :] = wout[j, 16 * g:16 * g + 16, :]
                dd[j, tq, np.arange(r0, r0 + 16), np.arange(r0, r0 + 16)] = od[j, 16 * g:16 * g + 16]
                pr, s2 = sl // 2, sl % 2
                for d in range(2):
                    lamA[j, tq, :, 0, pr, d, s2, :] = lre[j, d, g][None, :]
                    lamA[j, tq, :, 1, pr, d, s2, :] = lim[j, d, g][None, :]
                    lamA[j, tq, :, 2, pr, d, s2, :] = lst[j, d, g]
                    bA[j, tq, r0:r0 + 16, 0, pr, d, s2, :] = bre[j, d, g].T
                    bA[j, tq, r0:r0 + 16, 1, pr, d, s2, :] = bim[j, d, g].T
        for Pp in range(16):
            for s2 in range(2):
                g = 2 * Pp + s2
                sl = g % 4
                for d in range(2):
                    lamB[j, s2, :, 0, Pp, d] = lre[j, d, g]
                    lamB[j, s2, :, 1, Pp, d] = lim[j, d, g]
                    lamB[j, s2, :, 2, Pp, d] = lst[j, d, g]
                    cB[j, Pp, s2, :, 0, d, sl, 0:16] = cre[j, d, g].T
                    cB[j, Pp, s2, :, 1, d, sl, 0:16] = cim[j, d, g].T
                    bB[j, Pp, s2, :, 0, d, sl, 0:16] = bre[j, d, g]
                    bB[j, Pp, s2, :, 1, d, sl, 0:16] = bim[j, d, g]
    o["s5_win"] = win_s.reshape(NJ, 8, 128, KC * 128)
    o["s5_wout"] = wout_s
    o["s5_dd"] = dd
    o["s5_lamA"] = lamA.reshape(NJ, 8, 128, 3 * 512)
    o["s5_bA"] = bA.reshape(NJ, 8, 128, 2 * 512)
    o["s5_lamB"] = lamB.reshape(NJ, 128, 3 * 32)
    o["s5_cB"] = cB.reshape(NJ, 16, 128, 2 * 2 * 128)
    o["s5_bB"] = bB.reshape(NJ, 16, 128, 2 * 2 * 128)
    return o


def prep_even(inp):
    o = {}
    NJ = inp["e_w_in"].shape[0]
    o["ev_win"] = np.ascontiguousarray(inp["e_w_in"].reshape(NJ, KC, 128, 3088).transpose(0, 2, 1, 3))
    o["ev_convw"] = np.ascontiguousarray(inp["e_conv_w"].reshape(NJ, 31, 4, 128).transpose(0, 3, 2, 1))
    cv = np.stack([inp["e_conv_b"], inp["e_cln_g"], inp["e_cln_b"]], axis=1)
    o["ev_cvec"] = np.ascontiguousarray(cv.reshape(NJ, 3, 4, 128).transpose(0, 3, 1, 2))
    o["ev_sconv"] = np.ascontiguousarray(inp["e_sconv_w"].reshape(NJ, 5, 12, 128).transpose(0, 3, 2, 1))
    dh = np.stack([inp["e_a_log"].reshape(NJ, 8), inp["e_dt_bias"].reshape(NJ, 8)], axis=1)
    o["ev_dh"] = np.ascontiguousarray(np.broadcast_to(dh[:, None], (NJ, 64, 2, 8)))
    o["ev_og"] = np.ascontiguousarray(inp["e_onorm_g"].reshape(NJ, 128, 1))
    o["ev_wout"] = np.ascontiguousarray(inp["e_w_out"].reshape(NJ, KC, 128, 1024).transpose(0, 2, 1, 3))
    c = np.zeros((128, 6, 128), np.float32)
    c[:, 0, :] = np.eye(128)
    a = np.arange(64)
    c[:64, 1, :64] = (a[:, None] <= a[None, :])
    c[:64, 2, :64] = (a[:, None] >= a[None, :])
    c[:64, 3, :64] = (a[:, None] > a[None, :])
    c[:64, 4, :64] = (a[:, None] < a[None, :])
    c[:, 5, :] = 1.0
    o["ev_const"] = c
    return o


def prep_core(inp, b):
    m = {}
    xa = np.concatenate([inp["ctx"][b], inp["x"][b]], axis=0)
    m["xin"] = np.ascontiguousarray(xa.T)
    cv = np.stack([fm(inp["c"][b]), fm(inp["c_ctx"])], axis=-1)
    m["cvec"] = np.ascontiguousarray(cv)
    return m


_CACHE = {}


def kernel(**inputs):
    inp = {k: np.asarray(v, dtype=np.float32) for k, v in inputs.items()}
    B = _CACHE.get("ncores", inp["x"].shape[0])
    if "nc" not in _CACHE:
        k = K()
        _CACHE["nc"] = k.build()
        _CACHE["need"] = set(k.din.keys())
    nc = _CACHE["nc"]
    need = _CACHE["need"]
    sh = prep_shared(inp, need)
    xover = _CACHE.get("xin_override")
    in_maps = []
    for b in range(B):
        m = dict(sh)
        m.update(prep_core(inp, b))
        if xover is not None:
            m["xin"] = xover
        in_maps.append({k: v for k, v in m.items() if k in need})
    _CACHE["in_maps"] = in_maps
    res = run_bass_kernel_spmd(nc, in_maps, core_ids=list(range(B)))
    _CACHE["res"] = res
    if "out" not in res.results[0]:
        return None
    out = np.stack([np.ascontiguousarray(r["out"].T) for r in res.results], axis=0)
    return out.astype(np.float32)
```

```python
import numpy as np
from contextlib import ExitStack
import concourse.bass as bass
import concourse.mybir as mybir
from concourse.bass_utils import run_bass_kernel_spmd

F32 = mybir.dt.float32
BF16 = mybir.dt.bfloat16
AF = mybir.ActivationFunctionType
ALU = mybir.AluOpType

D = 1024
KC = 8
NCTX = 256
NLAT = 4096
T = NCTX + NLAT
DFF = 2816
NI = DFF // 128
DEPTH = 4
EPS = 1e-6
TILES = [(0, 256)] + [(256 + 512 * j, 512) for j in range(8)]
SUPER = [[0, 1, 2], [3, 4, 5], [6, 7, 8]]


class Buf:
    __slots__ = ("name", "w", "r")

    def __init__(self, name=""):
        self.name = name
        self.w = None
        self.r = {}


class Prog:
    ENG = ("pe", "act", "dve", "pool", "sp")

    def __init__(self, nc, n_dma_sems=48):
        self.nc = nc
        self.q = {k: [] for k in self.ENG}
        self.cnt = {k: 0 for k in self.ENG}
        self.seen = {k: {} for k in self.ENG}
        self.n_dma_sems = n_dma_sems
        self.dma_k = 0
        self.dma_last = {}
        self.n_ops = 0

    def _deps(self, eng, reads, writes, extra=()):
        need = {}

        def add(tok):
            if tok is None:
                return
            k, v = tok
            if need.get(k, 0) < v:
                need[k] = v

        for b in reads:
            add(b.w)
        for b in writes:
            add(b.w)
            for k, v in b.r.items():
                add((k, v))
        for t in extra:
            add(t)
        seen = self.seen[eng]
        waits = []
        for k, v in need.items():
            if k == "pe" and eng == "pe":
                continue
            if seen.get(k, 0) >= v:
                continue
            seen[k] = v
            waits.append((k, v))
        return waits

    def _commit(self, tok, reads, writes):
        k, v = tok
        for b in reads:
            if b.r.get(k, 0) < v:
                b.r[k] = v
        for b in writes:
            b.w = tok
            b.r = {}

    def op(self, eng, fn, reads=(), writes=()):
        waits = self._deps(eng, reads, writes)
        self.cnt[eng] += 1
        tok = (eng, self.cnt[eng])
        self.q[eng].append((waits, fn, tok, 1))
        self._commit(tok, reads, writes)
        self.n_ops += 1
        return tok

    def dma(self, eng, out_ap, in_ap, reads=(), writes=(), **kw):
        s = self.dma_k % self.n_dma_sems
        self.dma_k += 1
        prev = self.dma_last.get(s, 0)
        extra = [(("d", s), prev)] if prev else []
        waits = self._deps(eng, reads, writes, extra)
        val = prev + 16
        self.dma_last[s] = val
        tok = (("d", s), val)

        def fn(e, out_ap=out_ap, in_ap=in_ap, kw=kw):
            return e.dma_start(out=out_ap, in_=in_ap, **kw)

        self.q[eng].append((waits, fn, tok, 16))
        self._commit(tok, reads, writes)
        self.n_ops += 1
        return tok

    def barrier(self):
        toks = [(k, self.cnt[k]) for k in ("pe", "act", "dve", "pool") if self.cnt[k]]
        toks += [(("d", s), v) for s, v in self.dma_last.items()]
        for eng in self.ENG:
            waits = []
            seen = self.seen[eng]
            for k, v in toks:
                if seen.get(k, 0) >= v:
                    continue
                seen[k] = v
                waits.append((k, v))
            if waits:
                self.q[eng].append((waits, None, None, 0))

    def emit(self):
        nc = self.nc
        with ExitStack() as es:
            sems = {}
            for k in ("pe", "act", "dve", "pool"):
                sems[k] = es.enter_context(nc.semaphore("s_" + k))
            for s in range(min(self.n_dma_sems, max(1, self.dma_k))):
                sems[("d", s)] = es.enter_context(nc.semaphore("s_d%d" % s))
            fin = [(("d", s), v) for s, v in self.dma_last.items()]
            self.q["sp"].append((fin, None, None, 0))
            block = es.enter_context(nc.Block())

            def run(name, e):
                for waits, fn, tok, inc in self.q[name]:
                    for k, v in waits:
                        e.wait_ge(sems[k], v)
                    if fn is not None:
                        ins = fn(e)
                        ins.then_inc(sems[tok[0]], inc)

            @block.sync
            def _(e):
                run("sp", e)

            @block.tensor
            def _(e):
                run("pe", e)

            @block.scalar
            def _(e):
                run("act", e)

            @block.vector
            def _(e):
                run("dve", e)

            @block.gpsimd
            def _(e):
                run("pool", e)


class Ring:
    def __init__(self, tiles):
        self.tiles = tiles
        self.bufs = [Buf() for _ in tiles]
        self.i = 0

    def next(self):
        t, b = self.tiles[self.i], self.bufs[self.i]
        self.i = (self.i + 1) % len(self.tiles)
        return t, b


class K:
    def __init__(self, stages=None, layers=None, dump=False):
        self.stages = stages
        self.layers = list(range(DEPTH)) if layers is None else layers
        self.dump = dump
        nc = self.nc = bass.Bass("TRN2", target_bir_lowering=False)
        self.P = Prog(nc)
        self.es = ExitStack()
        self.din = {}
        self.n_alloc = 0

    def dram_in(self, name, shape):
        ap = self.nc.dram_tensor(name, list(shape), F32, kind="ExternalInput").ap()
        self.din[name] = ap
        return ap

    def sb(self, shape, dtype=F32, es=None, name=None):
        self.n_alloc += 1
        nm = name or ("t%d" % self.n_alloc)
        return (es or self.es).enter_context(self.nc.sbuf_tensor(nm, list(shape), dtype))

    def ring(self, n, shape, dtype=F32, es=None):
        return Ring([self.sb(shape, dtype, es) for _ in range(n)])

    def psum(self):
        return self.ps_ring.next()

    def load_cast(self, dst, b_dst, src, stg):
        F = dst.shape[-1]
        st, b_st = stg.next()
        self.P.dma("sp", st[:, :F], src, writes=[b_st])
        self.cp("pool", (dst, b_dst), (st[:, :F], b_st))

    def build(self):
        nc, P = self.nc, self.P
        self.xin = self.dram_in("xin", [D, T])
        self.cvec = self.dram_in("cvec", [128, KC, 2])
        self.wmod = self.dram_in("wmod", [DEPTH, 18, 128, KC * 512])
        self.bmod = self.dram_in("bmod", [DEPTH, 128, 72])
        self.ng2 = self.dram_in("ng2", [DEPTH, 128, 3 * KC * 2])
        self.fg = self.dram_in("fg", [128, KC])
        if self.stages is None or "ffn" in self.stages:
            self.w13 = self.dram_in("w13", [DEPTH, 2, NI, 128, KC * 256])
            self.w2 = self.dram_in("w2", [DEPTH, 2, KC, 128, NI * 128])
        self.out = nc.dram_tensor("out", [D, NLAT], F32, kind="ExternalOutput").ap()
        self.xT = nc.dram_tensor("xT_scratch", [D, T], F32, kind="Internal").ap()
        self.xT_v = self.xT.rearrange("(kc p) t -> p kc t", p=128)
        self.xin_v = self.xin.rearrange("(kc p) t -> p kc t", p=128)
        self.out_v = self.out.rearrange("(kc p) t -> p kc t", p=128)
        self.xbuf = [[Buf("x%d_%d" % (kc, t)) for t in range(len(TILES))] for kc in range(KC)]

        self.ps_ring = Ring([self.es.enter_context(nc.psum_tensor("ps%d" % i, [128, 512], F32))
                             for i in range(8)])
        self.ones_bf = self.sb([128, 128], BF16)
        self.b_const = Buf("const")
        P.op("dve", lambda e: e.memset(self.ones_bf[:], 1.0), writes=[self.b_const])
        self.eps_t = self.sb([128, 1])
        P.op("dve", lambda e: e.memset(self.eps_t[:], EPS), writes=[self.b_const])
        self.sc_bf = self.sb([128, KC, 2], BF16)
        self.b_sc = Buf()
        cv = self.sb([128, KC, 2])
        b_cv = Buf()
        P.dma("sp", cv[:], self.cvec, writes=[b_cv])
        P.op("act", lambda e: e.activation(out=self.sc_bf[:], in_=cv[:], func=AF.Silu),
             reads=[b_cv], writes=[self.b_sc])
        self.fg_t = self.sb([128, KC])
        self.b_fg = Buf()
        P.dma("sp", self.fg_t[:], self.fg, writes=[self.b_fg])
        self.modv = self.sb([128, 9, KC, 2])
        self.gs = self.sb([128, 3, KC, 2])
        self.geff = self.sb([128, 3, KC, 2])
        self.b_mod = Buf("mod")

        self.decl_mixer_inputs()
        first = True
        for l in self.layers:
            self.mod_stage(l)
            self.ffn_stage(l, 0, src_in=first)
            first = False
            self.mixer_stage(l)
            self.ffn_stage(l, 1)
        if self.dump:
            self.dump_stage()
        else:
            self.final_stage()
        P.emit()
        self.es.close()
        return nc

    def mod_stage(self, l):
        nc, P = self.nc, self.P
        es = ExitStack()
        wr = self.ring(2, [128, KC, 512], BF16, es)
        wstg = self.ring(2, [128, KC * 512], F32, es)
        bm = self.sb([128, 72], F32, es)
        ng = self.sb([128, 3, KC, 2], F32, es)
        b_bm, b_ng = Buf(), Buf()
        P.dma("sp", bm[:], self.bmod[l], writes=[b_bm])
        P.dma("sp", ng[:].rearrange("p a k s -> p (a k s)"), self.ng2[l], writes=[b_ng])
        ps, b_ps = self.psum()
        for blk in range(18):
            wt, b_w = wr.next()
            self.load_cast(wt[:].rearrange("p k c -> p (k c)"), b_w, self.wmod[l, blk], wstg)
            for fc in range(4):
                c = blk * 4 + fc
                for kc in range(KC):
                    P.op("pe", lambda e, wt=wt, fc=fc, kc=kc, c=c: e.matmul(
                        ps[:, 2 * c:2 * c + 2], wt[:, kc, fc * 128:(fc + 1) * 128], self.sc_bf[:, kc, :],
                        start=(kc == 0), stop=(kc == KC - 1)),
                        reads=[b_w, self.b_sc], writes=[b_ps])
        modv = self.modv
        for s in range(2):
            P.op("dve", lambda e, s=s: e.tensor_tensor(
                out=modv[:, :, :, s].rearrange("p m k -> p (m k)"),
                in0=ps[:, 0:144].rearrange("p (c s) -> p c s", s=2)[:, :, s],
                in1=bm[:], op=ALU.add),
                reads=[b_ps, b_bm], writes=[self.b_mod])
        for i in range(3):
            P.op("dve", lambda e, i=i: e.scalar_tensor_tensor(
                out=self.gs[:, i].rearrange("p k s -> p (k s)"),
                in0=modv[:, 3 * i + 1].rearrange("p k s -> p (k s)"), scalar=1.0,
                in1=ng[:, i].rearrange("p k s -> p (k s)"), op0=ALU.add, op1=ALU.mult),
                reads=[self.b_mod, b_ng], writes=[self.b_mod])
            gsc = 0.5 if i != 1 else 1.0
            P.op("dve", lambda e, i=i, gsc=gsc: e.tensor_scalar(
                out=self.geff[:, i].rearrange("p k s -> p (k s)"),
                in0=modv[:, 3 * i + 2].rearrange("p k s -> p (k s)"),
                scalar1=gsc, scalar2=None, op0=ALU.mult),
                reads=[self.b_mod], writes=[self.b_mod])
        P.barrier()
        es.close()

    def modulate(self, xt, b_x, n, i_norm, s, h_ap_fn, b_h, rs):
        P = self.P
        sq, b_sq = rs["sq"].next()
        P.op("act", lambda e: e.activation(out=sq[:, :, :n], in_=xt[:, :, :n], func=AF.Square),
             reads=[b_x], writes=[b_sq])
        ps, b_ps = self.psum()
        for kc in range(KC):
            P.op("pe", lambda e, kc=kc: e.matmul(ps[:, :n], self.ones_bf[:], sq[:, kc, :n],
                                                 start=(kc == 0), stop=(kc == KC - 1)),
                 reads=[b_sq, self.b_const], writes=[b_ps])
        rstd, b_r = rs["rstd"].next()
        P.op("act", lambda e: e.activation(out=rstd[:, :n], in_=ps[:, :n], func=AF.Sqrt,
                                           scale=1.0 / D, bias=self.eps_t[:, 0:1]),
             reads=[b_ps, self.b_const], writes=[b_r])
        P.op("dve", lambda e: e.reciprocal(out=rstd[:, :n], in_=rstd[:, :n]), reads=[b_r], writes=[b_r])
        for kc in range(KC):
            tmp, b_t = rs["tmp"].next()
            P.op("dve", lambda e, kc=kc, tmp=tmp: e.tensor_tensor(
                out=tmp[:, :n], in0=xt[:, kc, :n], in1=rstd[:, :n], op=ALU.mult),
                reads=[b_x, b_r], writes=[b_t])
            if i_norm is None:
                P.op("act", lambda e, kc=kc, tmp=tmp: e.activation(
                    out=h_ap_fn(kc), in_=tmp[:, :n], func=AF.Identity,
                    scale=self.fg_t[:, kc:kc + 1]),
                    reads=[b_t, self.b_fg], writes=[b_h])
            else:
                P.op("act", lambda e, kc=kc, tmp=tmp: e.activation(
                    out=h_ap_fn(kc), in_=tmp[:, :n], func=AF.Identity,
                    scale=self.gs[:, i_norm, kc, s:s + 1], bias=self.modv[:, 3 * i_norm, kc, s:s + 1]),
                    reads=[b_t, self.b_mod], writes=[b_h])

    def mk_scratch(self, es):
        return {"sq": self.ring(2, [128, KC, 512], BF16, es),
                "rstd": self.ring(2, [128, 512], F32, es),
                "tmp": self.ring(3, [128, 512], F32, es)}

    def load_x(self, xr, t, src_in=False):
        s0, n = TILES[t]
        xt, b_x = xr.next()
        src = self.xin_v if src_in else self.xT_v
        self.P.dma("sp", xt[:, :, :n], src[:, :, s0:s0 + n],
                   reads=([] if src_in else [self.xbuf[kc][t] for kc in range(KC)]), writes=[b_x])
        return xt, b_x

    def ffn_stage(self, l, f, src_in=False):
        if self.stages is not None and "ffn" not in self.stages:
            if src_in:
                self.copy_in()
            return
        nc, P = self.nc, self.P
        i_norm = 0 if f == 0 else 2
        es = ExitStack()
        rs = self.mk_scratch(es)
        xr = self.ring(1, [128, KC, 512], F32, es)
        wstg = self.ring(2, [128, NI * 128], F32, es)
        TS = 1536
        h = self.sb([128, KC, TS], BF16, es)
        act = self.sb([128, NI, TS], BF16, es)
        w13r = self.ring(3, [128, KC, 256], BF16, es)
        w2r = self.ring(2, [128, NI, 128], BF16, es)
        sar = self.ring(2, [128, 512], F32, es)
        xor_ = self.ring(3, [128, 512], F32, es)
        xnr = self.ring(3, [128, 512], F32, es)
        b_h = [Buf() for _ in range(3)]
        b_act = [[Buf() for _ in range(3)] for _ in range(NI)]
        src_v = self.xin_v if src_in else self.xT_v
        for st in SUPER:
            offs = []
            off = 0
            for ti, t in enumerate(st):
                s0, n = TILES[t]
                s = 1 if t == 0 else 0
                xt, b_x = self.load_x(xr, t, src_in)
                self.modulate(xt, b_x, n, i_norm, s,
                              (lambda kc, off=off, n=n: h[:, kc, off:off + n]), b_h[ti], rs)
                offs.append(off)
                off += n
            for i in range(NI):
                wt, b_w = w13r.next()
                self.load_cast(wt[:].rearrange("p k c -> p (k c)"), b_w, self.w13[l, f, i], wstg)
                for ti, t in enumerate(st):
                    s0, n = TILES[t]
                    off = offs[ti]
                    pa, b_pa = self.psum()
                    pb, b_pb = self.psum()
                    for (pp, b_pp, c0) in ((pa, b_pa, 0), (pb, b_pb, 128)):
                        for kc in range(KC):
                            P.op("pe", lambda e, pp=pp, c0=c0, kc=kc, wt=wt, off=off, n=n: e.matmul(
                                pp[:, :n], wt[:, kc, c0:c0 + 128], h[:, kc, off:off + n],
                                start=(kc == 0), stop=(kc == KC - 1)),
                                reads=[b_w, b_h[ti]], writes=[b_pp])
                    sa, b_sa = sar.next()
                    P.op("act", lambda e, sa=sa, pa=pa, n=n: e.activation(
                        out=sa[:, :n], in_=pa[:, :n], func=AF.Silu), reads=[b_pa], writes=[b_sa])
                    P.op("dve", lambda e, sa=sa, pb=pb, n=n, i=i, off=off: e.tensor_tensor(
                        out=act[:, i, off:off + n], in0=sa[:, :n], in1=pb[:, :n], op=ALU.mult),
                        reads=[b_sa, b_pb], writes=[b_act[i][ti]])
            for oc in range(KC):
                wt, b_w = w2r.next()
                self.load_cast(wt[:].rearrange("p i c -> p (i c)"), b_w, self.w2[l, f, oc], wstg)
                for ti, t in enumerate(st):
                    s0, n = TILES[t]
                    s = 1 if t == 0 else 0
                    off = offs[ti]
                    po, b_po = self.psum()
                    for i in range(NI):
                        P.op("pe", lambda e, po=po, i=i, wt=wt, off=off, n=n: e.matmul(
                            po[:, :n], wt[:, i, :], act[:, i, off:off + n],
                            start=(i == 0), stop=(i == NI - 1)),
                            reads=[b_w, b_act[i][ti]], writes=[b_po])
                    xo, b_xo = xor_.next()
                    P.dma("sp", xo[:, :n], src_v[:, oc, s0:s0 + n],
                          reads=([] if src_in else [self.xbuf[oc][t]]), writes=[b_xo])
                    xn, b_xn = xnr.next()
                    P.op("dve", lambda e, xn=xn, po=po, xo=xo, n=n, oc=oc, s=s: e.scalar_tensor_tensor(
                        out=xn[:, :n], in0=po[:, :n], scalar=self.geff[:, i_norm, oc, s:s + 1],
                        in1=xo[:, :n], op0=ALU.mult, op1=ALU.add),
                        reads=[b_po, b_xo, self.b_mod], writes=[b_xn])
                    P.dma("sp", self.xT_v[:, oc, s0:s0 + n], xn[:, :n], reads=[b_xn],
                          writes=[self.xbuf[oc][t]])
        P.barrier()
        es.close()

    def copy_in(self):
        P = self.P
        es = ExitStack()
        xr = self.ring(2, [128, KC, 512], F32, es)
        for t, (s0, n) in enumerate(TILES):
            xt, b_x = xr.next()
            P.dma("sp", xt[:, :, :n], self.xin_v[:, :, s0:s0 + n], writes=[b_x])
            P.dma("sp", self.xT_v[:, :, s0:s0 + n], xt[:, :, :n], reads=[b_x],
                  writes=[self.xbuf[kc][t] for kc in range(KC)])
        P.barrier()
        es.close()

    def mixer_stage(self, l):
        if self.stages is not None and "mix" not in self.stages:
            return
        if l % 2 == 1:
            self.s5_stage(l)
        else:
            self.even_stage(l)

    def dump_stage(self):
        P = self.P
        es = ExitStack()
        self.xdump = self.nc.dram_tensor("xdump", [D, T], F32, kind="ExternalOutput").ap()
        xd_v = self.xdump.rearrange("(kc p) t -> p kc t", p=128)
        xr = self.ring(2, [128, KC, 512], F32, es)
        for t in range(len(TILES)):
            s0, n = TILES[t]
            xt, b_x = self.load_x(xr, t)
            P.dma("sp", xd_v[:, :, s0:s0 + n], xt[:, :, :n], reads=[b_x])
        P.barrier()
        es.close()

    def tt(self, eng, out, a, b, op):
        self.P.op(eng, lambda e: e.tensor_tensor(out=out[0], in0=a[0], in1=b[0], op=op),
                  reads=[a[1], b[1]], writes=[out[1]])

    def ts(self, eng, out, a, s1, op0, s2=None, op1=None, sb=()):
        s1a = s1[0] if isinstance(s1, tuple) else s1
        s2a = s2[0] if isinstance(s2, tuple) else s2
        rd = [a[1]] + [x[1] for x in (s1, s2) if isinstance(x, tuple)]
        if op1 is None:
            self.P.op(eng, lambda e: e.tensor_scalar(out=out[0], in0=a[0], scalar1=s1a, scalar2=None, op0=op0),
                      reads=rd, writes=[out[1]])
        else:
            self.P.op(eng, lambda e: e.tensor_scalar(out=out[0], in0=a[0], scalar1=s1a, scalar2=s2a,
                                                     op0=op0, op1=op1), reads=rd, writes=[out[1]])

    def stt(self, out, a, sc, b, op0, op1):
        sca = sc[0] if isinstance(sc, tuple) else sc
        rd = [a[1], b[1]] + ([sc[1]] if isinstance(sc, tuple) else [])
        self.P.op("dve", lambda e: e.scalar_tensor_tensor(out=out[0], in0=a[0], scalar=sca, in1=b[0],
                                                          op0=op0, op1=op1), reads=rd, writes=[out[1]])

    def actf(self, out, a, func, scale=1.0, bias=None):
        sca = scale[0] if isinstance(scale, tuple) else scale
        rd = [a[1]] + [x[1] for x in (scale, bias) if isinstance(x, tuple)]
        if bias is None:
            self.P.op("act", lambda e: e.activation(out=out[0], in_=a[0], func=func, scale=sca),
                      reads=rd, writes=[out[1]])
        else:
            ba = bias[0] if isinstance(bias, tuple) else bias
            self.P.op("act", lambda e: e.activation(out=out[0], in_=a[0], func=func, scale=sca, bias=ba),
                      reads=rd, writes=[out[1]])

    def cp(self, eng, out, a):
        if eng == "act":
            self.P.op("act", lambda e: e.activation(out=out[0], in_=a[0], func=AF.Identity),
                      reads=[a[1]], writes=[out[1]])
        else:
            self.P.op(eng, lambda e: e.tensor_copy(out=out[0], in_=a[0]), reads=[a[1]], writes=[out[1]])

    def mm(self, ps, lhsT, rhs, start, stop):
        self.P.op("pe", lambda e: e.matmul(ps[0], lhsT[0], rhs[0], start=start, stop=stop),
                  reads=[lhsT[1], rhs[1]], writes=[ps[1]])


    def decl_mixer_inputs(self):
        if self.stages is not None and "mix" not in self.stages:
            return
        if any(l % 2 == 1 for l in self.layers):
            self.s5_win = self.dram_in("s5_win", [2, 8, 128, KC * 128])
            self.s5_lamA = self.dram_in("s5_lamA", [2, 8, 128, 3 * 512])
            self.s5_bA = self.dram_in("s5_bA", [2, 8, 128, 2 * 512])
            self.s5_lamB = self.dram_in("s5_lamB", [2, 128, 3 * 32])
            self.s5_cB = self.dram_in("s5_cB", [2, 16, 128, 2 * 2 * 128])
            self.s5_bB = self.dram_in("s5_bB", [2, 16, 128, 2 * 2 * 128])
            self.s5_dd = self.dram_in("s5_dd", [2, 8, 128, 128])
            self.s5_wout = self.dram_in("s5_wout", [2, 8, 128, 2048])
        if any(l % 2 == 0 for l in self.layers):
            self.decl_even_inputs()

    def discretize(self, lre, lim, lst, F, es, tag):
        PI = float(np.pi)

        def new(dtype=F32):
            return (self.sb([128, F], dtype, es)[:], Buf())

        cache = getattr(self, "_disc_cache", None)
        if cache is None:
            cache = self._disc_cache = {}
        key = (tag, F, id(es))
        if key not in cache:
            cache[key] = [new() for _ in range(9)] + [new(mybir.dt.int32)]
        A, B, C, Dd, E_, F_, G, H, I, KI = cache[key]
        dt, lr = A, B
        self.actf(dt, lst, AF.Exp)
        self.ts("dve", lr, lre, -1e-4, ALU.min)
        self.tt("dve", C, lr, dt, ALU.mult)
        mag = Dd
        self.actf(mag, C, AF.Exp)
        th = C
        self.tt("dve", th, lim, dt, ALU.mult)
        outs = []
        for sh, o in ((0.0, I), (PI / 2, A)):
            w, kf, r, m = E_, F_, G, H
            self.ts("dve", w, th, sh, ALU.add)
            self.ts("dve", KI, w, 1.0 / (2 * PI), ALU.mult)
            self.cp("dve", kf, KI)
            self.stt(r, kf, -2 * PI, w, ALU.mult, ALU.add)
            self.ts("dve", m, r, PI, ALU.is_gt, -2 * PI, ALU.mult)
            self.tt("dve", r, r, m, ALU.add)
            self.ts("dve", r, r, PI, ALU.min, -PI, ALU.max)
            self.actf(o, r, AF.Sin)
            outs.append(o)
        sn, cs = outs
        are, aim = E_, F_
        self.tt("dve", are, mag, cs, ALU.mult)
        self.tt("dve", aim, mag, sn, ALU.mult)
        self.tt("dve", G, lr, lr, ALU.mult)
        self.tt("dve", H, lim, lim, ALU.mult)
        self.tt("dve", H, H, G, ALU.add)
        self.P.op("dve", lambda e: e.reciprocal(out=H[0], in_=H[0]), reads=[H[1]], writes=[H[1]])
        nre = Dd
        self.ts("dve", nre, are, -1.0, ALU.add)
        cre, cim = I, A
        self.tt("dve", G, nre, lr, ALU.mult)
        self.tt("dve", C, aim, lim, ALU.mult)
        self.tt("dve", G, G, C, ALU.add)
        self.tt("dve", cre, G, H, ALU.mult)
        self.tt("dve", G, aim, lr, ALU.mult)
        self.tt("dve", C, nre, lim, ALU.mult)
        self.tt("dve", G, G, C, ALU.subtract)
        self.tt("dve", cim, G, H, ALU.mult)
        return {"are": are, "aim": aim, "cre": cre, "cim": cim, "t1": G, "t2": C}

    def cmul(self, orr, oi, ar, ai, br, bi, t1, t2):
        self.tt("dve", t1, ar, br, ALU.mult)
        self.tt("dve", t2, ai, bi, ALU.mult)
        self.tt("dve", orr, t1, t2, ALU.subtract)
        self.tt("dve", t1, ar, bi, ALU.mult)
        self.tt("dve", t2, ai, br, ALU.mult)
        self.tt("dve", oi, t1, t2, ALU.add)

    def s5_stage(self, l):
        nc, P = self.nc, self.P
        j = l // 2
        NCH = T // 8
        HALF = NCH // 2
        es = ExitStack()
        U = self.sb([128, 8, T], BF16, es)
        b_U = [Buf() for _ in range(8)]
        Up = [U[:, tq, :].rearrange("p (i c) -> p i c", i=8) for tq in range(8)]

        es0 = ExitStack()
        rs = self.mk_scratch(es0)
        xr = self.ring(2, [128, KC, 512], F32, es0)
        hr = self.ring(2, [128, KC, 512], BF16, es0)
        win = self.sb([128, 8, KC, 128], BF16, es0)
        b_win = Buf()
        wstg0 = self.ring(2, [128, KC * 128], F32, es0)
        for tq in range(8):
            self.load_cast(win[:, tq].rearrange("p k c -> p (k c)"), b_win, self.s5_win[j, tq], wstg0)
        for t, (s0, n) in enumerate(TILES):
            sidx = 1 if t == 0 else 0
            xt, b_x = self.load_x(xr, t)
            h, b_h = hr.next()
            self.modulate(xt, b_x, n, 1, sidx, (lambda kc, h=h, n=n: h[:, kc, :n]), b_h, rs)
            for tq in range(8):
                ps, b_ps = self.psum()
                for kc in range(KC):
                    self.mm((ps[:, :n], b_ps), (win[:, tq, kc, :], b_win), (h[:, kc, :n], b_h),
                            kc == 0, kc == KC - 1)
                c0, nc_ = s0 // 8, n // 8
                self.cp("act" if tq % 2 == 0 else "dve",
                        (Up[tq][:, :, c0:c0 + nc_].rearrange("p i c -> p c i"), b_U[tq]),
                        (ps[:, :n].rearrange("p (c i) -> p c i", i=8), b_ps))
        P.barrier()
        es0.close()

        esB = ExitStack()
        lamB = self.sb([128, 3, 32], F32, esB)
        b_lamB = Buf()
        P.dma("sp", lamB[:].rearrange("p a f -> p (a f)"), self.s5_lamB[j], writes=[b_lamB])
        dB = self.discretize((lamB[:, 0, :], b_lamB), (lamB[:, 1, :], b_lamB), (lamB[:, 2, :], b_lamB), 32, esB, "B")
        PW = self.sb([128, 9, 2, 32], F32, esB)
        NPW = self.sb([128, 9, 2, 32], F32, esB)
        b_PW = Buf()
        P.op("dve", lambda e: e.memset(PW[:, 0, 0, :], 1.0), writes=[b_PW])
        P.op("dve", lambda e: e.memset(PW[:, 0, 1, :], 0.0), writes=[b_PW])
        self.cp("dve", (PW[:, 1, 0, :], b_PW), dB["are"])
        self.cp("dve", (PW[:, 1, 1, :], b_PW), dB["aim"])
        for m in range(2, 9):
            self.cmul((PW[:, m, 0, :], b_PW), (PW[:, m, 1, :], b_PW), (PW[:, m - 1, 0, :], b_PW),
                      (PW[:, m - 1, 1, :], b_PW), dB["are"], dB["aim"], dB["t1"], dB["t2"])
        self.ts("dve", (NPW[:].rearrange("p m r f -> p (m r f)"), b_PW),
                (PW[:].rearrange("p m r f -> p (m r f)"), b_PW), -1.0, ALU.mult)
        ARAR = self.sb([128, 2, 32], F32, esB)
        NAIAI = self.sb([128, 2, 32], F32, esB)
        b_A2 = Buf()
        for d in range(2):
            src_r = PW[:, 8, 0, :].rearrange("p (q d) -> p q d", d=2)[:, :, d]
            src_i = PW[:, 8, 1, :].rearrange("p (q d) -> p q d", d=2)[:, :, d]
            nsrc_i = NPW[:, 8, 1, :].rearrange("p (q d) -> p q d", d=2)[:, :, d]
            self.cp("dve", (ARAR[:, d, 0:16], b_A2), (src_r, b_PW))
            self.cp("dve", (ARAR[:, d, 16:32], b_A2), (src_r, b_PW))
            self.cp("dve", (NAIAI[:, d, 0:16], b_A2), (nsrc_i, b_PW))
            self.cp("dve", (NAIAI[:, d, 16:32], b_A2), (src_i, b_PW))

        E = self.sb([128, 2, 32, NCH], BF16, esB)
        b_E = [Buf(), Buf()]
        es1 = ExitStack()
        rawA = self.ring(2, [128, 3, 256], F32, es1)
        rawb = self.ring(2, [128, 2, 256], F32, es1)
        EBr = self.ring(2, [128, 8, 2, 256], BF16, es1)
        cur = [self.sb([128, 2, 256], F32, es1) for _ in range(2)]
        b_cur = [Buf(), Buf()]
        lamA_v = self.s5_lamA.rearrange("j q p (a f) -> j q p a f", a=3)
        bA_v = self.s5_bA.rearrange("j q p (a f) -> j q p a f", a=2)
        for tq in range(8):
            for pr in range(2):
                Pp = 2 * tq + pr
                es_t = es1
                la, b_la = rawA.next()
                rb, b_rb = rawb.next()
                P.dma("sp", la[:], lamA_v[j, tq, :, :, pr * 256:(pr + 1) * 256], writes=[b_la])
                P.dma("sp", rb[:], bA_v[j, tq, :, :, pr * 256:(pr + 1) * 256], writes=[b_rb])
                dA = self.discretize((la[:, 0, :], b_la), (la[:, 1, :], b_la), (la[:, 2, :], b_la), 256, es_t, "A")
                EB, b_EB = EBr.next()
                self.cmul((cur[0][:, 0, :], b_cur[0]), (cur[0][:, 1, :], b_cur[0]), dA["cre"], dA["cim"],
                          (rb[:, 0, :], b_rb), (rb[:, 1, :], b_rb), dA["t1"], dA["t2"])
                for k in range(8):
                    c0, c1 = cur[k % 2], cur[(k + 1) % 2]
                    self.cp("act", (EB[:, k].rearrange("p r f -> p (r f)"), b_EB),
                            (c0[:].rearrange("p r f -> p (r f)"), b_cur[k % 2]))
                    if k < 7:
                        self.cmul((c1[:, 0, :], b_cur[(k + 1) % 2]), (c1[:, 1, :], b_cur[(k + 1) % 2]),
                                  dA["are"], dA["aim"], (c0[:, 0, :], b_cur[k % 2]), (c0[:, 1, :], b_cur[k % 2]),
                                  dA["t1"], dA["t2"])
                for d in range(2):
                    for ri in range(2):
                        for hf in range(2):
                            ps, b_ps = self.psum()
                            for ip in range(8):
                                k = 7 - ip if d == 0 else ip
                                col = d * 128
                                self.mm((ps[:, :HALF], b_ps), (EB[:, k, ri, col:col + 128], b_EB),
                                        (Up[tq][:, ip, hf * HALF:(hf + 1) * HALF], b_U[tq]), ip == 0, ip == 7)
                            self.cp("act" if (ri + hf) % 2 == 0 else "dve",
                                    (E[:, d, ri * 16 + Pp, hf * HALF:(hf + 1) * HALF], b_E[d]), (ps[:, :HALF], b_ps))
        P.barrier()
        es1.close()

        es2 = ExitStack()
        SS = [[self.sb([128, 48], F32, es2) for _ in range(2)] for _ in range(2)]
        b_SS = [[Buf(), Buf()], [Buf(), Buf()]]
        T1 = [self.ring(2, [128, 32], F32, es2) for _ in range(2)]
        T2 = [self.ring(2, [128, 32], F32, es2) for _ in range(2)]
        for d in range(2):
            for k in range(2):
                P.op("dve", lambda e, d=d, k=k: e.memset(SS[d][k][:], 0.0), writes=[b_SS[d][k]])
        order = [list(range(NCH)), list(range(31, -1, -1)) + list(range(NCH - 1, 31, -1))]
        for step in range(NCH):
            for d in range(2):
                c = order[d][step]
                cur, nxt = SS[d][step % 2], SS[d][(step + 1) % 2]
                b_cur, b_nxt = b_SS[d][step % 2], b_SS[d][(step + 1) % 2]
                t1, b_t1 = T1[d].next()
                t2, b_t2 = T2[d].next()
                ec = (E[:, d, :, c], b_E[d])
                self.tt("dve", (t1[:], b_t1), (ARAR[:, d, :], b_A2), (cur[:, 0:32], b_cur), ALU.mult)
                self.tt("dve", (t2[:], b_t2), (NAIAI[:, d, :], b_A2), (cur[:, 16:48], b_cur), ALU.mult)
                self.tt("dve", (t1[:], b_t1), (t1[:], b_t1), (t2[:], b_t2), ALU.add)
                self.tt("dve", (nxt[:, 0:32], b_nxt), (t1[:], b_t1), ec, ALU.add)
                self.cp("act", ec, (cur[:, 0:32], b_cur))
                self.cp("act", (nxt[:, 32:48], b_nxt), (nxt[:, 0:16], b_nxt))
        P.barrier()
        es2.close()

        es3 = ExitStack()
        craw = self.ring(2, [128, 2, 2, 128], F32, es3)
        braw = self.ring(2, [128, 2, 2, 128], F32, es3)
        CAr = self.ring(2, [128, 2, 9, 2, 128], BF16, es3)
        BBr = self.ring(2, [128, 2, 2, 128], BF16, es3)
        Ktr = self.ring(2, [128, 15, 128], BF16, es3)
        ddr = self.ring(2, [128, 128], F32, es3)
        zst = self.ring(1, [128, T], BF16, es3)
        tmpc = self.ring(8, [128, 128], F32, es3)
        for tq in range(8):
            CA = []; BB = []
            for pr in range(2):
                Pp = 2 * tq + pr
                cr, b_cr = craw.next()
                br_, b_br = braw.next()
                P.dma("sp", cr[:].rearrange("p a d f -> p (a d f)"), self.s5_cB[j, Pp], writes=[b_cr])
                P.dma("sp", br_[:].rearrange("p a d f -> p (a d f)"), self.s5_bB[j, Pp], writes=[b_br])
                ca, b_ca = CAr.next()
                bb, b_bb = BBr.next()
                for d in range(2):
                    col = Pp * 2 + d
                    cre = (dB["cre"][0][:, col:col + 1], dB["cre"][1])
                    cim = (dB["cim"][0][:, col:col + 1], dB["cim"][1])
                    tA, b_tA = tmpc.next()
                    self.actf((tA[:], b_tA), (br_[:, 0, d, :], b_br), AF.Identity, cre)
                    ncim, b_nc = tmpc.next()
                    self.ts("dve", (ncim[:, 0:1], b_nc), cim, -1.0, ALU.mult)
                    self.stt((bb[:, d, 0, :], b_bb), (br_[:, 1, d, :], b_br), (ncim[:, 0:1], b_nc), (tA[:], b_tA),
                             ALU.mult, ALU.add)
                    tB, b_tB = tmpc.next()
                    self.actf((tB[:], b_tB), (br_[:, 1, d, :], b_br), AF.Identity, cre)
                    self.stt((bb[:, d, 1, :], b_bb), (br_[:, 0, d, :], b_br), cim, (tB[:], b_tB), ALU.mult, ALU.add)
                    for m in range(9):
                        are_m = (PW[:, m, 0, col:col + 1], b_PW)
                        aim_m = (PW[:, m, 1, col:col + 1], b_PW)
                        nare_m = (NPW[:, m, 0, col:col + 1], b_PW)
                        naim_m = (NPW[:, m, 1, col:col + 1], b_PW)
                        tC, b_tC = tmpc.next()
                        self.actf((tC[:], b_tC), (cr[:, 0, d, :], b_cr), AF.Identity, are_m)
                        self.stt((ca[:, d, m, 0, :], b_ca), (cr[:, 1, d, :], b_cr), naim_m, (tC[:], b_tC),
                                 ALU.mult, ALU.add)
                        tD, b_tD = tmpc.next()
                        self.actf((tD[:], b_tD), (cr[:, 0, d, :], b_cr), AF.Identity, naim_m)
                        self.stt((ca[:, d, m, 1, :], b_ca), (cr[:, 1, d, :], b_cr), nare_m, (tD[:], b_tD),
                                 ALU.mult, ALU.add)
                CA.append((ca, b_ca)); BB.append((bb, b_bb))
            Kt, b_Kt = Ktr.next()
            dd, b_dd = ddr.next()
            P.dma("sp", dd[:], self.s5_dd[j, tq], writes=[b_dd])
            for d in range(2):
                for m in range(1, 8):
                    ps, b_ps = self.psum()
                    n_mm = 0
                    for pr in range(2):
                        for ri in range(2):
                            self.mm((ps[:, :128], b_ps), (BB[pr][0][:, d, ri, :], BB[pr][1]),
                                    (CA[pr][0][:, d, m, ri, :], CA[pr][1]), n_mm == 0, n_mm == 3)
                            n_mm += 1
                    mi = 7 + m if d == 0 else 7 - m
                    self.cp("act", (Kt[:, mi, :], b_Kt), (ps[:, :128], b_ps))
            ps, b_ps = self.psum()
            n_mm = 0
            for d in range(2):
                for pr in range(2):
                    for ri in range(2):
                        self.mm((ps[:, :128], b_ps), (BB[pr][0][:, d, ri, :], BB[pr][1]),
                                (CA[pr][0][:, d, 0, ri, :], CA[pr][1]), n_mm == 0, n_mm == 7)
                        n_mm += 1
            self.tt("dve", (Kt[:, 7, :], b_Kt), (ps[:, :128], b_ps), (dd[:], b_dd), ALU.add)
            z, b_z = zst.next()
            zc = z[:].rearrange("p (c i) -> p c i", i=8)
            for i in range(8):
                for hf in range(2):
                    ps, b_ps = self.psum()
                    n_mm = 0
                    for ip in range(8):
                        self.mm((ps[:, :HALF], b_ps), (Kt[:, 7 + (i - ip), :], b_Kt),
                                (Up[tq][:, ip, hf * HALF:(hf + 1) * HALF], b_U[tq]), n_mm == 0, False)
                        n_mm += 1
                    for pr in range(2):
                        Pp = 2 * tq + pr
                        for d in range(2):
                            m = i + 1 if d == 0 else 8 - i
                            for ri in range(2):
                                n_mm += 1
                                self.mm((ps[:, :HALF], b_ps), (CA[pr][0][:, d, m, ri, :], CA[pr][1]),
                                        (E[:, d, ri * 16 + Pp, hf * HALF:(hf + 1) * HALF], b_E[d]),
                                        False, n_mm == 16)
                    self.actf((zc[:, hf * HALF:(hf + 1) * HALF, i], b_z), (ps[:, :HALF], b_ps), AF.Gelu_apprx_tanh)
            self.cp("pool", (U[:, tq, :], b_U[tq]), (z[:], b_z))
        P.barrier()
        es3.close()
        esB.close()

        es4 = ExitStack()
        wo = self.sb([128, 8, 2048], BF16, es4)
        b_wo = Buf()
        wstg4 = self.ring(2, [128, 2048], F32, es4)
        for tq in range(8):
            self.load_cast(wo[:, tq, :], b_wo, self.s5_wout[j, tq], wstg4)
        sgr = self.ring(2, [128, 512], F32, es4)
        yr = self.ring(2, [128, 512], F32, es4)
        xor_ = self.ring(3, [128, 512], F32, es4)
        xnr = self.ring(3, [128, 512], F32, es4)
        for t, (s0, n) in enumerate(TILES):
            sidx = 1 if t == 0 else 0
            for mo in range(8):
                pa, b_pa = self.psum()
                pg, b_pg = self.psum()
                for (pp, b_pp, c0) in ((pa, b_pa, mo * 128), (pg, b_pg, 1024 + mo * 128)):
                    for tq in range(8):
                        self.mm((pp[:, :n], b_pp), (wo[:, tq, c0:c0 + 128], b_wo), (U[:, tq, s0:s0 + n], b_U[tq]),
                                tq == 0, tq == 7)
                sg, b_sg = sgr.next()
                self.actf((sg[:, :n], b_sg), (pg[:, :n], b_pg), AF.Sigmoid)
                y, b_y = yr.next()
                self.tt("dve", (y[:, :n], b_y), (sg[:, :n], b_sg), (pa[:, :n], b_pa), ALU.mult)
                self.residual_out(y, b_y, mo, t, xor_, xnr)
        P.barrier()
        es4.close()
        es.close()

    def residual_out(self, y, b_y, oc, t, xor_, xnr):
        P = self.P
        s0, n = TILES[t]
        sidx = 1 if t == 0 else 0
        xo, b_xo = xor_.next()
        P.dma("sp", xo[:, :n], self.xT_v[:, oc, s0:s0 + n], reads=[self.xbuf[oc][t]], writes=[b_xo])
        xn, b_xn = xnr.next()
        self.stt((xn[:, :n], b_xn), (y[:, :n], b_y), (self.geff[:, 1, oc, sidx:sidx + 1], self.b_mod),
                 (xo[:, :n], b_xo), ALU.mult, ALU.add)
        P.dma("sp", self.xT_v[:, oc, s0:s0 + n], xn[:, :n], reads=[b_xn], writes=[self.xbuf[oc][t]])


    def decl_even_inputs(self):
        nc = self.nc
        self.ev_win = self.dram_in("ev_win", [2, 128, KC, 3088])
        self.ev_convw = self.dram_in("ev_convw", [2, 128, 4, 31])
        self.ev_cvec = self.dram_in("ev_cvec", [2, 128, 3, 4])
        self.ev_sconv = self.dram_in("ev_sconv", [2, 128, 12, 5])
        self.ev_dh = self.dram_in("ev_dh", [2, 64, 2, 8])
        self.ev_og = self.dram_in("ev_og", [2, 128, 1])
        self.ev_wout = self.dram_in("ev_wout", [2, 128, KC, 1024])
        self.ev_const = self.dram_in("ev_const", [128, 6, 128])
        def scr(name, n):
            return nc.dram_tensor(name, [n, 128, T], F32, kind="Internal").ap()
        self.d_qkvraw = scr("d_qkvraw", 12)
        self.d_qkvf = scr("d_qkvf", 12)
        self.d_gate = scr("d_gate", 4)
        self.d_conv = scr("d_conv", 4)
        self.d_ot = [scr("d_otf", 4), scr("d_otb", 4)]
        nt = len(TILES)
        self.bq_raw = [Buf() for _ in range(nt)]
        self.bq_f = [Buf() for _ in range(nt)]
        self.bq_gate = [Buf() for _ in range(nt)]
        self.bq_conv = [Buf() for _ in range(nt)]
        self.bq_ot = [[Buf() for _ in range(nt)] for _ in range(2)]

    def even_stage(self, l):
        P = self.P
        j = l // 2
        esL = ExitStack()
        cst = self.sb([128, 6, 128], F32, esL)
        b_cst = Buf()
        P.dma("sp", cst[:].rearrange("p a f -> p (a f)"), self.ev_const.rearrange("p a f -> p (a f)"), writes=[b_cst])
        self.cst, self.b_cst = cst, b_cst
        self.LA = self.sb([64, 68, 8], F32, esL)
        self.BETA = self.sb([64, 68, 8], F32, esL)
        self.b_lab = Buf()
        import os
        ph = os.environ.get("EVPH", "0a,0b,1,2,3").split(",")
        if "0a" in ph:
            self.ev_e0a(l, j)
        if "0b" in ph:
            self.ev_e0b(l, j)
        if "1" in ph:
            self.ev_e1(l, j)
        if "2" in ph:
            self.ev_e2(l, j, esL)
        if "3" in ph:
            self.ev_e3(l, j)
        P.barrier()
        esL.close()

    def ev_e0a(self, l, j):
        P = self.P
        es = ExitStack()
        rs = self.mk_scratch(es)
        xr = self.ring(2, [128, KC, 512], F32, es)
        hr = self.ring(2, [128, KC, 512], BF16, es)
        NW = 2064
        w = self.sb([128, KC, NW], BF16, es)
        b_w = Buf()
        wstg = self.ring(2, [128, NW], F32, es)
        for kc in range(KC):
            self.load_cast(w[:, kc, :], b_w, self.ev_win[j, :, kc, 1024:3088], wstg)
        dh = self.sb([64, 2, 8], F32, es)
        b_dh = Buf()
        P.dma("sp", dh[:].rearrange("p a f -> p (a f)"), self.ev_dh[j].rearrange("p a f -> p (a f)"), writes=[b_dh])
        nA = self.sb([64, 8], F32, es)
        self.actf((nA[:], b_dh), (dh[:, 0, :], b_dh), AF.Exp)
        self.ts("dve", (nA[:], b_dh), (nA[:], b_dh), -1.0, ALU.mult)
        stg = self.ring(4, [128, 512], F32, es)
        xa = self.sb([64, 8, 8], F32, es)
        b_xa = Buf()
        for t, (s0, n) in enumerate(TILES):
            sidx = 1 if t == 0 else 0
            xt, b_x = self.load_x(xr, t)
            h, b_h = hr.next()
            self.modulate(xt, b_x, n, 1, sidx, (lambda kc, h=h, n=n: h[:, kc, :n]), b_h, rs)
            for fc in range(16):
                ps, b_ps = self.psum()
                for kc in range(KC):
                    self.mm((ps[:, :n], b_ps), (w[:, kc, fc * 128:(fc + 1) * 128], b_w), (h[:, kc, :n], b_h),
                            kc == 0, kc == KC - 1)
                st, b_st = stg.next()
                if fc < 12:
                    self.cp("act" if fc % 2 == 0 else "dve", (st[:, :n], b_st), (ps[:, :n], b_ps))
                    P.dma("sp", self.d_qkvraw[fc, :, s0:s0 + n], st[:, :n], reads=[b_st], writes=[self.bq_raw[t]])
                else:
                    self.actf((st[:, :n], b_st), (ps[:, :n], b_ps), AF.Silu)
                    P.dma("sp", self.d_gate[fc - 12, :, s0:s0 + n], st[:, :n], reads=[b_st], writes=[self.bq_gate[t]])
            nchk = n // 64
            c0 = s0 // 64
            ps, b_ps = self.psum()
            for ci in range(nchk):
                for kc in range(KC):
                    self.mm((ps[0:64, ci * 16:(ci + 1) * 16], b_ps), (h[:, kc, ci * 64:(ci + 1) * 64], b_h),
                            (w[:, kc, 2048:2064], b_w), kc == 0, kc == KC - 1)
            ab = ps[0:64, 0:nchk * 16].rearrange("p (c d a h) -> p c d a h", d=2, a=2, h=4)
            for d in range(2):
                self.tt("dve", (xa[:, :nchk, d * 4:(d + 1) * 4], b_xa), (ab[:, :, d, 0, :], b_ps),
                        (dh[:, 1, d * 4:(d + 1) * 4].unsqueeze(1).to_broadcast([64, nchk, 4]), b_dh), ALU.add)
            self.actf((xa[:, :nchk, :], b_xa), (xa[:, :nchk, :], b_xa), AF.Exp)
            self.actf((xa[:, :nchk, :], b_xa), (xa[:, :nchk, :], b_xa), AF.Ln, 1.0, 1.0)
            self.tt("dve", (self.LA[:, c0:c0 + nchk, :], self.b_lab), (xa[:, :nchk, :], b_xa),
                    (nA[:].unsqueeze(1).to_broadcast([64, nchk, 8]), b_dh), ALU.mult)
            for d in range(2):
                self.actf((self.BETA[:, c0:c0 + nchk, d * 4:(d + 1) * 4], self.b_lab), (ab[:, :, d, 1, :], b_ps),
                          AF.Sigmoid)
        P.barrier()
        es.close()

    def ev_e0b(self, l, j):
        P = self.P
        es = ExitStack()
        rs = self.mk_scratch(es)
        xr = self.ring(2, [128, KC, 512], F32, es)
        hr = self.ring(2, [128, KC, 512], BF16, es)
        w = self.sb([128, KC, 1024], BF16, es)
        b_w = Buf()
        wstg = self.ring(2, [128, 1024], F32, es)
        for kc in range(KC):
            self.load_cast(w[:, kc, :], b_w, self.ev_win[j, :, kc, 0:1024], wstg)
        cw = self.sb([128, 4, 31], F32, es)
        cv = self.sb([128, 3, 4], F32, es)
        b_cw = Buf()
        P.dma("sp", cw[:].rearrange("p a f -> p (a f)"), self.ev_convw[j].rearrange("p a f -> p (a f)"), writes=[b_cw])
        P.dma("sp", cv[:].rearrange("p a f -> p (a f)"), self.ev_cvec[j].rearrange("p a f -> p (a f)"), writes=[b_cw])
        DG = self.sb([128, 4, 31, 128], BF16, es)
        b_DG = Buf()
        for cc in range(4):
            for k in range(31):
                self.ts("dve", (DG[:, cc, k, :], b_DG), (self.cst[:, 0, :], self.b_cst),
                        (cw[:, cc, k:k + 1], b_cw), ALU.mult)
        sgr = self.ring(2, [128, 512], F32, es)
        cinr = self.ring(2, [128, 4, 752], BF16, es)
        hcr = self.ring(2, [128, 4, 512], F32, es)
        hbr = self.ring(2, [128, 2, 4, 512], BF16, es)
        mr = self.ring(6, [128, 512], F32, es)
        yr = self.ring(3, [128, 512], F32, es)
        for t, (s0, n) in enumerate(TILES):
            sidx = 1 if t == 0 else 0
            xt, b_x = self.load_x(xr, t)
            h, b_h = hr.next()
            self.modulate(xt, b_x, n, 1, sidx, (lambda kc, h=h, n=n: h[:, kc, :n]), b_h, rs)
            cin, b_cin = cinr.next()
            W = 256 if t == 0 else 64
            R = n // W
            WP = W + 30
            P.op("pool", lambda e, cin=cin: e.memset(cin[:].rearrange("p a f -> p (a f)"), 0.0), writes=[b_cin])
            for cc in range(4):
                pa, b_pa = self.psum()
                pg, b_pg = self.psum()
                for (pp, b_pp, c0) in ((pa, b_pa, cc * 128), (pg, b_pg, 512 + cc * 128)):
                    for kc in range(KC):
                        self.mm((pp[:, :n], b_pp), (w[:, kc, c0:c0 + 128], b_w), (h[:, kc, :n], b_h),
                                kc == 0, kc == KC - 1)
                sg, b_sg = sgr.next()
                self.actf((sg[:, :n], b_sg), (pg[:, :n], b_pg), AF.Sigmoid)
                civ = cin[:, cc, 0:R * WP].rearrange("p (r w) -> p r w", w=WP)[:, :, 15:15 + W]
                self.tt("dve", (civ, b_cin), (sg[:, :n].rearrange("p (r w) -> p r w", w=W), b_sg),
                        (pa[:, :n].rearrange("p (r w) -> p r w", w=W), b_pa), ALU.mult)
            hc, b_hc = hcr.next()
            hb, b_hb = hbr.next()
            if t == 0:
                groups = [(15, 256, 0, 1)]
            else:
                groups = [(15, 3 * WP + W, 0, 4), (4 * WP + 15, 3 * WP + W, 4, 4)]
            for cc in range(4):
                for (a0, N, r0, nr) in groups:
                    ps, b_ps = self.psum()
                    taps = [15] + [k for k in range(31) if k != 15]
                    for ti, k in enumerate(taps):
                        sft = k - 15
                        self.mm((ps[:, 0:N], b_ps), (DG[:, cc, k, :], b_DG), (cin[:, cc, a0 + sft:a0 + sft + N], b_cin),
                                ti == 0, ti == 30)
                    if t == 0:
                        src = ps[:, 0:256]
                        dst = hc[:, cc, 0:256]
                    else:
                        src = ps[:, 0:nr * WP].rearrange("p (r w) -> p r w", w=WP)[:, :, 0:W]
                        dst = hc[:, cc, r0 * W:(r0 + nr) * W].rearrange("p (r w) -> p r w", w=W)
                    self.actf((dst, b_hc), (src, b_ps), AF.Identity, 1.0, (cv[:, 0, cc:cc + 1], b_cw))
            self.cp("pool", (hb[:, 0, :, :n], b_hb), (hc[:, :, :n], b_hc))
            self.P.op("act", lambda e, hb=hb, hc=hc, n=n: e.activation(out=hb[:, 1, :, :n], in_=hc[:, :, :n],
                                                                      func=AF.Square), reads=[b_hc], writes=[b_hb])
            p1, b_p1 = self.psum()
            p2, b_p2 = self.psum()
            for (pp, b_pp, a) in ((p1, b_p1, 0), (p2, b_p2, 1)):
                for cc in range(4):
                    self.mm((pp[:, :n], b_pp), (self.ones_bf[:], self.b_const), (hb[:, a, cc, :n], b_hb),
                            cc == 0, cc == 3)
            mean, b_mean = mr.next()
            m2, b_m2 = mr.next()
            var, b_var = mr.next()
            self.ts("dve", (mean[:, :n], b_mean), (p1[:, :n], b_p1), 1.0 / 512, ALU.mult)
            self.tt("dve", (m2[:, :n], b_m2), (mean[:, :n], b_mean), (mean[:, :n], b_mean), ALU.mult)
            self.stt((var[:, :n], b_var), (p2[:, :n], b_p2), 1.0 / 512, (m2[:, :n], b_m2), ALU.mult, ALU.subtract)
            self.ts("dve", (var[:, :n], b_var), (var[:, :n], b_var), 0.0, ALU.max)
            self.actf((var[:, :n], b_var), (var[:, :n], b_var), AF.Sqrt, 1.0, (self.eps_t[:, 0:1], self.b_const))
            self.P.op("dve", lambda e, var=var, n=n: e.reciprocal(out=var[:, :n], in_=var[:, :n]),
                      reads=[b_var], writes=[b_var])
            for cc in range(4):
                y, b_y = yr.next()
                self.tt("dve", (y[:, :n], b_y), (hc[:, cc, :n], b_hc), (mean[:, :n], b_mean), ALU.subtract)
                self.tt("dve", (y[:, :n], b_y), (y[:, :n], b_y), (var[:, :n], b_var), ALU.mult)
                self.actf((y[:, :n], b_y), (y[:, :n], b_y), AF.Silu, (cv[:, 1, cc:cc + 1], b_cw),
                          (cv[:, 2, cc:cc + 1], b_cw))
                P.dma("sp", self.d_conv[cc, :, s0:s0 + n], y[:, :n], reads=[b_y], writes=[self.bq_conv[t]])
        P.barrier()
        es.close()

    def ev_e1(self, l, j):
        P = self.P
        es = ExitStack()
        sw = self.sb([128, 12, 5], F32, es)
        b_sw = Buf()
        P.dma("sp", sw[:].rearrange("p a f -> p (a f)"), self.ev_sconv[j].rearrange("p a f -> p (a f)"), writes=[b_sw])
        Rr = self.ring(2, [128, 12, 516], F32, es)
        accr = self.ring(2, [128, 12, 512], F32, es)
        prodr = self.ring(10, [128, 512], F32, es)
        b_accc = [[Buf() for _ in range(12)] for _ in range(2)]
        sqr = self.ring(2, [128, 8, 512], BF16, es)
        rnr = self.ring(3, [128, 512], F32, es)
        nt = len(TILES)
        for t, (s0, n) in enumerate(TILES):
            seg0, seg1 = (0, NCTX) if t == 0 else (NCTX, T)
            lo, hi = max(seg0, s0 - 2), min(seg1, s0 + n + 2)
            Rt, b_R = Rr.next()
            P.op("pool", lambda e, Rt=Rt: e.memset(Rt[:].rearrange("p a f -> p (a f)"), 0.0), writes=[b_R])
            nb = [self.bq_raw[tt_] for tt_ in (t - 1, t, t + 1) if 0 <= tt_ < nt]
            P.dma("sp", Rt[:, :, lo - (s0 - 2):hi - (s0 - 2)],
                  self.d_qkvraw.rearrange("a p t -> p a t")[:, :, lo:hi], reads=nb, writes=[b_R])
            acc, b_acc = accr.next()
            self.P.op("dve", lambda e, acc=acc: e.memset(acc[:, 0, 0:1], 0.0), writes=[b_acc])
            for cc in range(12):
                prods = []
                for k in range(5):
                    pk, b_pk = prodr.next()
                    self.actf((pk[:, :n], b_pk), (Rt[:, cc, k:k + n], b_R), AF.Identity, (sw[:, cc, k:k + 1], b_sw))
                    prods.append((pk, b_pk))
                b_ac = b_accc[accr.i][cc]
                self.P.op("dve", lambda e, acc=acc, cc=cc, n=n, p0=prods[0][0], p1=prods[1][0]: e.tensor_tensor(
                    out=acc[:, cc, :n], in0=p0[:, :n], in1=p1[:, :n], op=ALU.add),
                    reads=[prods[0][1], prods[1][1], b_acc], writes=[b_ac])
                for k in range(2, 5):
                    self.tt("dve", (acc[:, cc, :n], b_ac), (acc[:, cc, :n], b_ac), (prods[k][0][:, :n], prods[k][1]),
                            ALU.add)
            self.P.op("dve", lambda e, acc=acc: e.tensor_copy(out=acc[:, 0, 0:1], in_=acc[:, 0, 0:1]),
                      reads=[b_accc[accr.i][cc] for cc in range(12)], writes=[b_acc])
            self.P.op("act", lambda e, acc=acc, n=n: e.activation(out=acc[:, :, :n], in_=acc[:, :, :n], func=AF.Silu),
                      reads=[b_acc], writes=[b_acc])
            sq, b_sq = sqr.next()
            self.P.op("act", lambda e, acc=acc, sq=sq, n=n: e.activation(out=sq[:, :, :n], in_=acc[:, 0:8, :n],
                                                                      func=AF.Square), reads=[b_acc], writes=[b_sq])
            for cc in range(8):
                ps, b_ps = self.psum()
                self.mm((ps[:, :n], b_ps), (self.ones_bf[:], self.b_const), (sq[:, cc, :n], b_sq), True, True)
                rn, b_rn = rnr.next()
                self.actf((rn[:, :n], b_rn), (ps[:, :n], b_ps), AF.Sqrt, 1.0, (self.eps_t[:, 0:1], self.b_const))
                self.P.op("dve", lambda e, rn=rn, n=n: e.reciprocal(out=rn[:, :n], in_=rn[:, :n]),
                          reads=[b_rn], writes=[b_rn])
                if cc < 4:
                    self.stt((acc[:, cc, :n], b_acc), (acc[:, cc, :n], b_acc), 128.0 ** -0.5, (rn[:, :n], b_rn),
                             ALU.mult, ALU.mult)
                else:
                    self.tt("dve", (acc[:, cc, :n], b_acc), (acc[:, cc, :n], b_acc), (rn[:, :n], b_rn), ALU.mult)
            P.dma("sp", self.d_qkvf.rearrange("a p t -> p a t")[:, :, s0:s0 + n], acc[:, :, :n], reads=[b_acc],
                  writes=[self.bq_f[t]])
        P.barrier()
        es.close()


    def ev_e2(self, l, j, esL):
        P = self.P
        es = ExitStack()
        cst, b_c = self.cst, self.b_cst
        I64 = (cst[0:64, 0, 0:64], b_c)
        ID128 = (cst[:, 0, :], b_c)
        ones64 = (cst[0:64, 5, 0:64], b_c)
        ones64x128 = (cst[0:64, 5, :], b_c)
        UM = [(cst[0:64, 1, 0:64], b_c), (cst[0:64, 2, 0:64], b_c)]
        SM = [(cst[0:64, 3, 0:64], b_c), (cst[0:64, 4, 0:64], b_c)]

        def bc_h(ap):
            return ap.unsqueeze(1).to_broadcast([64, 4, 64])

        GTK = self.sb([64, 68, 8], F32, es)
        EGT = self.sb([64, 68, 8], F32, es)
        BG = self.sb([64, 68, 8], F32, es)
        ETAIL = self.sb([64, 68, 8], F32, es)
        NBETA = self.sb([64, 68, 8], F32, es)
        EGL = self.sb([128, 68, 8], F32, es)
        b_tab = Buf()
        LAb = (self.LA, self.b_lab)
        for d in range(2):
            hd = slice(d * 4, d * 4 + 4)
            ps, b_ps = self.psum()
            self.mm((ps[0:64, 0:272], b_ps), UM[d], (self.LA[:, :, hd], self.b_lab), True, True)
            self.cp("dve", (GTK[:, :, hd], b_tab), (ps[0:64, 0:272].rearrange("p (c h) -> p c h", h=4), b_ps))
            ps2, b_ps2 = self.psum()
            self.mm((ps2[:, 0:272], b_ps2), ones64x128, (self.LA[:, :, hd], self.b_lab), True, True)
            self.actf((EGL[:, :, hd], b_tab), (ps2[:, 0:272].rearrange("p (c h) -> p c h", h=4), b_ps2), AF.Exp)
            self.tt("dve", (ETAIL[:, :, hd], b_tab), (ps2[0:64, 0:272].rearrange("p (c h) -> p c h", h=4), b_ps2),
                    (GTK[:, :, hd], b_tab), ALU.subtract)
        fl = lambda tl: tl[:].rearrange("p c h -> p (c h)")
        self.ts("dve", (fl(ETAIL), b_tab), (fl(ETAIL), b_tab), 0.0, ALU.min)
        self.actf((fl(ETAIL), b_tab), (fl(ETAIL), b_tab), AF.Exp)
        self.actf((fl(EGT), b_tab), (fl(GTK), b_tab), AF.Exp)
        self.tt("dve", (fl(BG), b_tab), (fl(self.BETA), self.b_lab), (fl(EGT), b_tab), ALU.mult)
        self.ts("dve", (fl(NBETA), b_tab), (fl(self.BETA), self.b_lab), -1.0, ALU.mult)

        def R(n, shape, dt=F32):
            return self.ring(n, shape, dt, es)
        qkvr = R(4, [128, 12, 64]); qkbr = R(4, [128, 8, 64], BF16)
        L1r = R(4, [64, 4, 64]); L1nr = R(4, [64, 4, 64])
        egr = R(4, [128, 4, 64])
        dsr = R(4, [64, 4, 64]); dir_ = R(4, [64, 4, 64])
        ktr = R(4, [64, 4, 128], BF16); kbgr = R(4, [64, 4, 128]); vbr = R(4, [64, 4, 128], BF16)
        pmr = R(6, [64, 4, 64]); ptr = R(6, [64, 4, 64]); xtr = R(6, [64, 4, 64])
        qktr = R(4, [64, 4, 64], BF16); ttbr = R(4, [64, 4, 64], BF16)
        nwr = R(4, [128, 4, 64], BF16); qdr = R(4, [128, 4, 64], BF16)
        vnr = R(4, [64, 4, 128], BF16)
        S = [self.sb([128, 4, 128], F32, es) for _ in range(2)]
        Sb = [self.sb([128, 4, 128], BF16, es) for _ in range(2)]
        b_S = [Buf(), Buf()]; b_Sb = [Buf(), Buf()]
        OTs = [R(2, [128, 4, 512]), R(2, [128, 4, 512])]
        for d in range(2):
            P.op("dve", lambda e, d=d: e.memset(S[d][:].rearrange("p h v -> p (h v)"), 0.0), writes=[b_S[d]])
            P.op("pool", lambda e, d=d: e.memset(Sb[d][:].rearrange("p h v -> p (h v)"), 0.0), writes=[b_Sb[d]])
        qkvf_v = self.d_qkvf.rearrange("a p t -> p a t")

        def tile_of_chunk(c):
            return 0 if c < 4 else 1 + (c - 4) // 8

        import os as _os
        PPN = float(_os.environ.get("PPN", "99"))

        class _Stop(Exception):
            pass

        def sec(k):
            if k > PPN:
                raise _Stop()

        def prepass(c, d):
            hd = slice(d * 4, d * 4 + 4)
            t = tile_of_chunk(c)
            qkv, b_q = qkvr.next()
            P.dma("sp", qkv[:], qkvf_v[:, :, c * 64:(c + 1) * 64], reads=[self.bq_f[t]], writes=[b_q])
            qkb, b_qb = qkbr.next()
            self.cp("pool", (qkb[:], b_qb), (qkv[:, 0:8, :], b_q))
            L1, b_L1 = L1r.next()
            self.tt("dve", (L1[:], b_L1), (bc_h(UM[d][0]), b_c),
                    (self.LA[:, c, hd].unsqueeze(2).to_broadcast([64, 4, 64]), self.b_lab), ALU.mult)
            sec(2)
            ps, b_ps = self.psum()
            self.mm((ps[:, 0:256], b_ps), ones64x128, (L1[:].rearrange("p h i -> p (h i)"), b_L1), True, True)
            eg, b_eg = egr.next()
            self.actf((eg[:].rearrange("p h i -> p (h i)"), b_eg), (ps[:, 0:256], b_ps), AF.Exp)
            qd, b_qd = qdr.next()
            self.tt("dve", (qd[:], b_qd), (qkv[:, 0:4, :], b_q), (eg[:], b_eg), ALU.mult)
            sec(3)
            gp, b_gp = L1nr.next()
            self.tt("dve", (gp[:], b_gp), (ps[0:64, 0:256].rearrange("p (h i) -> p h i", h=4), b_ps),
                    (GTK[:, c, hd].unsqueeze(2).to_broadcast([64, 4, 64]), b_tab), ALU.subtract)
            ds, b_ds = dsr.next(); di, b_di = dir_.next()
            self.ts("dve", (ds[:], b_ds), (gp[:], b_gp), 0.0, ALU.max)
            self.actf((ds[:], b_ds), (ds[:], b_ds), AF.Exp, -1.0)
            self.tt("dve", (ds[:], b_ds), (ds[:], b_ds), (bc_h(SM[d][0]), b_c), ALU.mult)
            self.ts("dve", (di[:], b_di), (gp[:], b_gp), 0.0, ALU.min)
            self.actf((di[:], b_di), (di[:], b_di), AF.Exp)
            self.tt("pool", (di[:], b_di), (di[:], b_di), (bc_h(UM[d][0]), b_c), ALU.mult)
            sec(4)
            psk, b_psk = self.psum(); psv, b_psv = self.psum()
            for h in range(4):
                P.op("pe", lambda e, h=h, psk=psk, qkv=qkv: e.transpose(psk[0:64, h * 128:(h + 1) * 128],
                                                                      qkv[:, 4 + h, :], cst[:, 0, :]),
                     reads=[b_q, b_c], writes=[b_psk])
                P.op("pe", lambda e, h=h, psv=psv, qkv=qkv: e.transpose(psv[0:64, h * 128:(h + 1) * 128],
                                                                      qkv[:, 8 + h, :], cst[:, 0, :]),
                     reads=[b_q, b_c], writes=[b_psv])
            K3 = psk[0:64, :].rearrange("p (h k) -> p h k", h=4)
            V3 = psv[0:64, :].rearrange("p (h k) -> p h k", h=4)

            def bc_k(tl):
                return tl[:, c, hd].unsqueeze(2).to_broadcast([64, 4, 128])
            kt, b_kt = ktr.next(); kbg, b_kbg = kbgr.next(); vb, b_vb = vbr.next()
            self.tt("dve", (kt[:], b_kt), (K3, b_psk), (bc_k(ETAIL), b_tab), ALU.mult)
            self.tt("dve", (kbg[:], b_kbg), (K3, b_psk), (bc_k(BG), b_tab), ALU.mult)
            self.tt("dve", (vb[:], b_vb), (V3, b_psv), (bc_k(self.BETA), self.b_lab), ALU.mult)
            sec(5)
            pskk, b_pskk = self.psum(); psqk, b_psqk = self.psum()
            for h in range(4):
                self.mm((pskk[0:64, h * 64:(h + 1) * 64], b_pskk), (qkb[:, 4 + h, :], b_qb), (qkb[:, 4 + h, :], b_qb),
                        True, True)
                self.mm((psqk[0:64, h * 64:(h + 1) * 64], b_psqk), (qkb[:, 4 + h, :], b_qb), (qkb[:, h, :], b_qb),
                        True, True)
            Pm, b_Pm = pmr.next()
            self.tt("dve", (Pm[:], b_Pm), (pskk[0:64, 0:256].rearrange("p (h i) -> p h i", h=4), b_pskk),
                    (ds[:], b_ds), ALU.mult)
            self.tt("dve", (Pm[:], b_Pm), (Pm[:], b_Pm),
                    (NBETA[:, c, hd].unsqueeze(2).to_broadcast([64, 4, 64]), b_tab), ALU.mult)
            qkt, b_qkt = qktr.next()
            self.tt("dve", (qkt[:], b_qkt), (psqk[0:64, 0:256].rearrange("p (h i) -> p h i", h=4), b_psqk),
                    (di[:], b_di), ALU.mult)
            sec(6)
            pst, b_pst = self.psum()
            for h in range(4):
                self.mm((pst[0:64, h * 64:(h + 1) * 64], b_pst), (Pm[:, h, :], b_Pm), I64, True, True)
            sec(6.3)
            PT, b_PT = ptr.next()
            self.cp("act", (PT[:].rearrange("p h i -> p (h i)"), b_PT), (pst[0:64, 0:256], b_pst))
            sec(6.6)
            XT, b_XT = xtr.next()
            self.tt("dve", (XT[:], b_XT), (PT[:], b_PT), (bc_h(I64[0]), b_c), ALU.add)
            sec(7)
            for lvl in range(1, 6):
                psP, b_psP = self.psum()
                for h in range(4):
                    self.mm((psP[0:64, h * 64:(h + 1) * 64], b_psP), (PT[:, h, :], b_PT), (Pm[:, h, :], b_Pm), True, True)
                if lvl < 5:
                    psT, b_psT = self.psum()
                    for h in range(4):
                        self.mm((psT[0:64, h * 64:(h + 1) * 64], b_psT), (Pm[:, h, :], b_Pm), (PT[:, h, :], b_PT),
                                True, True)
                Pm2, b_Pm2 = pmr.next()
                self.cp("act", (Pm2[:].rearrange("p h i -> p (h i)"), b_Pm2), (psP[0:64, 0:256], b_psP))
                if lvl < 5:
                    PT2, b_PT2 = ptr.next()
                    self.cp("dve", (PT2[:].rearrange("p h i -> p (h i)"), b_PT2), (psT[0:64, 0:256], b_psT))
                psX, b_psX = self.psum()
                for h in range(4):
                    self.mm((psX[0:64, h * 64:(h + 1) * 64], b_psX), (Pm2[:, h, :], b_Pm2), (XT[:, h, :], b_XT), True, True)
                XT2, b_XT2 = xtr.next()
                self.tt("dve", (XT2[:].rearrange("p h i -> p (h i)"), b_XT2), (XT[:].rearrange("p h i -> p (h i)"), b_XT),
                        (psX[0:64, 0:256], b_psX), ALU.add)
                Pm, b_Pm = Pm2, b_Pm2
                if lvl < 5:
                    PT, b_PT = PT2, b_PT2
                XT, b_XT = XT2, b_XT2
            sec(8)
            ttb, b_ttb = ttbr.next()
            self.cp("act", (ttb[:].rearrange("p h i -> p (h i)"), b_ttb), (XT[:].rearrange("p h i -> p (h i)"), b_XT))
            psw, b_psw = self.psum()
            for h in range(4):
                self.mm((psw[:, h * 64:(h + 1) * 64], b_psw), (kbg[:, h, :], b_kbg), (XT[:, h, :], b_XT), True, True)
            nw, b_nw = nwr.next()
            self.actf((nw[:].rearrange("p h i -> p (h i)"), b_nw), (psw[:, 0:256], b_psw), AF.Identity, -1.0)
            return dict(ttb=(ttb, b_ttb), vb=(vb, b_vb), nw=(nw, b_nw), qd=(qd, b_qd), qkt=(qkt, b_qkt), kt=(kt, b_kt))

        cur_ot = [None, None]

        def step(c, d, pp):
            t = tile_of_chunk(c)
            s0, n = TILES[t]
            col = c * 64 - s0
            ttb, vb, nw, qd, qkt, kt = pp["ttb"], pp["vb"], pp["nw"], pp["qd"], pp["qkt"], pp["kt"]
            psv, b_psv = self.psum()
            for h in range(4):
                self.mm((psv[0:64, h * 128:(h + 1) * 128], b_psv), (ttb[0][:, h, :], ttb[1]), (vb[0][:, h, :], vb[1]),
                        True, False)
                self.mm((psv[0:64, h * 128:(h + 1) * 128], b_psv), (nw[0][:, h, :], nw[1]), (Sb[d][:, h, :], b_Sb[d]),
                        False, True)
            vn, b_vn = vnr.next()
            self.cp("act", (vn[:].rearrange("p h v -> p (h v)"), b_vn), (psv[0:64, :], b_psv))
            pso, b_pso = self.psum()
            for h in range(4):
                self.mm((pso[:, h * 64:(h + 1) * 64], b_pso), (Sb[d][:, h, :], b_Sb[d]), (qd[0][:, h, :], qd[1]),
                        True, False)
                self.mm((pso[:, h * 64:(h + 1) * 64], b_pso), (vn[:, h, :], b_vn), (qkt[0][:, h, :], qkt[1]),
                        False, True)
            first = (col == 0) if d == 0 else (col == n - 64)
            last = (col == n - 64) if d == 0 else (col == 0)
            if first:
                cur_ot[d] = OTs[d].next()
            ot, b_ot = cur_ot[d]
            self.cp("dve", (ot[:, :, col:col + 64], b_ot), (pso[:, 0:256].rearrange("p (h i) -> p h i", h=4), b_pso))
            if last:
                P.dma("sp", self.d_ot[d].rearrange("a p t -> p a t")[:, :, s0:s0 + n], ot[:, :, :n], reads=[b_ot],
                      writes=[self.bq_ot[d][t]])
            pss, b_pss = self.psum()
            for h in range(4):
                self.mm((pss[:, h * 128:(h + 1) * 128], b_pss), (kt[0][:, h, :], kt[1]), (vn[:, h, :], b_vn), True, True)
            for h in range(4):
                self.stt((S[d][:, h, :], b_S[d]), (S[d][:, h, :], b_S[d]), (EGL[:, c, d * 4 + h:d * 4 + h + 1], b_tab),
                         (pss[:, h * 128:(h + 1) * 128], b_pss), ALU.mult, ALU.add)
            self.cp("pool", (Sb[d][:].rearrange("p h v -> p (h v)"), b_Sb[d]), (S[d][:].rearrange("p h v -> p (h v)"), b_S[d]))

        order = [list(range(68)), [3, 2, 1, 0] + list(range(67, 3, -1))]
        import os
        ev2 = os.environ.get("EV2", "all")
        if ev2 == "tab":
            P.barrier(); es.close(); return
        if ev2 == "pre1":
            try:
                prepass(0, 0)
            except _Stop:
                pass
            P.barrier(); es.close(); return
        if ev2 == "pre":
            for sidx in range(68):
                prepass(order[0][sidx], 0); prepass(order[1][sidx], 1)
            P.barrier(); es.close(); return
        NST = int(os.environ.get("EV2N", "68"))
        pend = [prepass(order[0][0], 0), prepass(order[1][0], 1)]
        for sidx in range(NST):
            nxt = [None, None]
            if sidx + 1 < NST:
                nxt = [prepass(order[0][sidx + 1], 0), prepass(order[1][sidx + 1], 1)]
            step(order[0][sidx], 0, pend[0])
            step(order[1][sidx], 1, pend[1])
            pend = nxt
        P.barrier()
        es.close()

    def ev_e3(self, l, j):
        P = self.P
        es = ExitStack()
        wo = self.sb([128, KC, 1024], BF16, es)
        b_wo = Buf()
        wstg = self.ring(2, [128, 1024], F32, es)
        for kc in range(KC):
            self.load_cast(wo[:, kc, :], b_wo, self.ev_wout[j, :, kc, :], wstg)
        og = self.sb([128, 1], F32, es)
        b_og = Buf()
        P.dma("sp", og[:], self.ev_og[j], writes=[b_og])
        ofr = self.ring(2, [128, 4, 512], F32, es)
        obr = self.ring(2, [128, 4, 512], F32, es)
        gtr = self.ring(2, [128, 4, 512], F32, es)
        cvr = self.ring(2, [128, 4, 512], F32, es)
        sqr = self.ring(2, [128, 4, 512], BF16, es)
        mixr = self.ring(2, [128, 8, 512], BF16, es)
        rnr = self.ring(3, [128, 512], F32, es)
        xor_ = self.ring(3, [128, 512], F32, es)
        xnr = self.ring(3, [128, 512], F32, es)
        def issue_loads(t):
            s0, n = TILES[t]
            of, b_of = ofr.next(); ob, b_ob = obr.next(); gt, b_gt = gtr.next(); cv, b_cv = cvr.next()
            P.dma("sp", of[:, :, :n], self.d_ot[0].rearrange("a p t -> p a t")[:, :, s0:s0 + n],
                  reads=[self.bq_ot[0][t]], writes=[b_of])
            P.dma("sp", ob[:, :, :n], self.d_ot[1].rearrange("a p t -> p a t")[:, :, s0:s0 + n],
                  reads=[self.bq_ot[1][t]], writes=[b_ob])
            P.dma("sp", gt[:, :, :n], self.d_gate.rearrange("a p t -> p a t")[:, :, s0:s0 + n],
                  reads=[self.bq_gate[t]], writes=[b_gt])
            P.dma("sp", cv[:, :, :n], self.d_conv.rearrange("a p t -> p a t")[:, :, s0:s0 + n],
                  reads=[self.bq_conv[t]], writes=[b_cv])
            return (of, b_of, ob, b_ob, gt, b_gt, cv, b_cv)

        pending = issue_loads(0)
        for t, (s0, n) in enumerate(TILES):
            of, b_of, ob, b_ob, gt, b_gt, cv, b_cv = pending
            if t + 1 < len(TILES):
                pending = issue_loads(t + 1)
            self.tt("pool", (of[:, :, :n], b_of), (of[:, :, :n], b_of), (ob[:, :, :n], b_ob), ALU.add)
            sq, b_sq = sqr.next()
            self.P.op("act", lambda e, sq=sq, of=of, n=n: e.activation(out=sq[:, :, :n], in_=of[:, :, :n], func=AF.Square),
                      reads=[b_of], writes=[b_sq])
            mix, b_mix = mixr.next()
            self.cp("pool", (mix[:, 0:4, :n], b_mix), (cv[:, :, :n], b_cv))
            for h in range(4):
                ps, b_ps = self.psum()
                self.mm((ps[:, :n], b_ps), (self.ones_bf[:], self.b_const), (sq[:, h, :n], b_sq), True, True)
                rn, b_rn = rnr.next()
                self.actf((rn[:, :n], b_rn), (ps[:, :n], b_ps), AF.Sqrt, 1.0 / 128, (self.eps_t[:, 0:1], self.b_const))
                self.P.op("dve", lambda e, rn=rn, n=n: e.reciprocal(out=rn[:, :n], in_=rn[:, :n]),
                          reads=[b_rn], writes=[b_rn])
                self.tt("dve", (rn[:, :n], b_rn), (rn[:, :n], b_rn), (of[:, h, :n], b_of), ALU.mult)
                self.stt((mix[:, 4 + h, :n], b_mix), (rn[:, :n], b_rn), (og[:, 0:1], b_og), (gt[:, h, :n], b_gt),
                         ALU.mult, ALU.mult)
            for oc in range(KC):
                ps, b_ps = self.psum()
                for kc in range(KC):
                    self.mm((ps[:, :n], b_ps), (wo[:, kc, oc * 128:(oc + 1) * 128], b_wo), (mix[:, kc, :n], b_mix),
                            kc == 0, kc == KC - 1)
                self.residual_out(ps, b_ps, oc, t, xor_, xnr)
        P.barrier()
        es.close()

    def final_stage(self):
        P = self.P
        es = ExitStack()
        rs = self.mk_scratch(es)
        xr = self.ring(2, [128, KC, 512], F32, es)
        yr = self.ring(2, [128, KC, 512], F32, es)
        for t in range(1, len(TILES)):
            s0, n = TILES[t]
            xt, b_x = self.load_x(xr, t)
            y, b_y = yr.next()
            self.modulate(xt, b_x, n, None, 0, (lambda kc, y=y, n=n: y[:, kc, :n]), b_y, rs)
            P.dma("sp", self.out_v[:, :, s0 - NCTX:s0 - NCTX + n], y[:, :, :n], reads=[b_y])
        P.barrier()
        es.close()


def fm(v):
    return np.ascontiguousarray(v.reshape(KC, 128).T)


def prep_shared(inp, need=None):
    sh = {}
    w_mod = inp["w_mod"]
    sh["wmod"] = np.ascontiguousarray(
        w_mod.reshape(DEPTH, KC, 128, 18, 512).transpose(0, 3, 2, 1, 4)).reshape(DEPTH, 18, 128, KC * 512)
    sh["bmod"] = np.ascontiguousarray(inp["b_mod"].reshape(DEPTH, 72, 128).transpose(0, 2, 1))
    ng = inp["norm_g"].reshape(DEPTH, 3, KC, 128).transpose(0, 3, 1, 2)
    sh["ng2"] = np.ascontiguousarray(np.repeat(ng[..., None], 2, axis=-1)).reshape(DEPTH, 128, 48)
    sh["fg"] = fm(inp["final_g"])
    if need is not None and "s5_win" in need:
        sh.update(prep_s5(inp))
    if need is not None and "ev_win" in need:
        sh.update(prep_even(inp))
    if need is not None and "w13" not in need:
        return sh
    w13 = inp["ffn_w13"]
    a = w13[..., :DFF].reshape(DEPTH, 2, KC, 128, NI, 128)
    b = w13[..., DFF:].reshape(DEPTH, 2, KC, 128, NI, 128)
    ab = np.stack([a, b], axis=5)
    sh["w13"] = np.ascontiguousarray(ab.transpose(0, 1, 4, 3, 2, 5, 6)).reshape(DEPTH, 2, NI, 128, KC * 256)
    w2 = inp["ffn_w2"].reshape(DEPTH, 2, NI, 128, KC, 128)
    sh["w2"] = np.ascontiguousarray(w2.transpose(0, 1, 4, 3, 2, 5)).reshape(DEPTH, 2, KC, 128, NI * 128)
    return sh


def prep_s5(inp):
    o = {}
    w_in = inp["o_w_in"]; lre = inp["o_lam_re"]; lim = inp["o_lam_im"]; lst = inp["o_log_step"]
    bre = inp["o_b_re"]; bim = inp["o_b_im"]; cre = inp["o_c_re"]; cim = inp["o_c_im"]
    od = inp["o_d"]; wout = inp["o_w_out"]
    NJ = w_in.shape[0]
    win_s = np.zeros((NJ, 8, 128, KC, 128), np.float32)
    wout_s = np.zeros((NJ, 8, 128, 2048), np.float32)
    dd = np.zeros((NJ, 8, 128, 128), np.float32)
    lamA = np.zeros((NJ, 8, 128, 3, 2, 2, 2, 64), np.float32)
    bA = np.zeros((NJ, 8, 128, 2, 2, 2, 2, 64), np.float32)
    lamB = np.zeros((NJ, 2, 64, 3, 16, 2), np.float32)
    cB = np.zeros((NJ, 16, 2, 64, 2, 2, 4, 32), np.float32)
    bB = np.zeros((NJ, 16, 2, 64, 2, 2, 4, 32), np.float32)
    for j in range(NJ):
        wk = w_in[j].reshape(KC, 128, 512)
        for tq in range(8):
            for sl in range(4):
                g = 4 * tq + sl
                r0 = 32 * sl
                win_s[j, tq, :, :, r0:r0 + 16] = wk[:, :, 16 * g:16 * g + 16].transpose(1, 0, 2)
                wout_s[j, tq, r0:r0 + 16, :] = wout[j, 16 * g:16 * g + 16, :]
                dd[j, tq, np.arange(r0, r0 + 16), np.arange(r0, r0 + 16)] = od[j, 16 * g:16 * g + 16]
                pr, s2 = sl // 2, sl % 2
                for d in range(2):
                    lamA[j, tq, :, 0, pr, d, s2, :] = lre[j, d, g][None, :]
                    lamA[j, tq, :, 1, pr, d, s2, :] = lim[j, d, g][None, :]
                    lamA[j, tq, :, 2, pr, d, s2, :] = lst[j, d, g]
                    bA[j, tq, r0:r0 + 16, 0, pr, d, s2, :] = bre[j, d, g].T
                    bA[j, tq, r0:r0 + 16, 1, pr, d, s2, :] = bim[j, d, g].T
        for Pp in range(16):
            for s2 in range(2):
                g = 2 * Pp + s2
                sl = g % 4
                for d in range(2):
                    lamB[j, s2, :, 0, Pp, d] = lre[j, d, g]
                    lamB[j, s2, :, 1, Pp, d] = lim[j, d, g]
                    lamB[j, s2, :, 2, Pp, d] = lst[j, d, g]
                    cB[j, Pp, s2, :, 0, d, sl, 0:16] = cre[j, d, g].T
                    cB[j, Pp, s2, :, 1, d, sl, 0:16] = cim[j, d, g].T
                    bB[j, Pp, s2, :, 0, d, sl, 0:16] = bre[j, d, g]
                    bB[j, Pp, s2, :, 1, d, sl, 0:16] = bim[j, d, g]
    o["s5_win"] = win_s.reshape(NJ, 8, 128, KC * 128)
    o["s5_wout"] = wout_s
    o["s5_dd"] = dd
    o["s5_lamA"] = lamA.reshape(NJ, 8, 128, 3 * 512)
    o["s5_bA"] = bA.reshape(NJ, 8, 128, 2 * 512)
    o["s5_lamB"] = lamB.reshape(NJ, 128, 3 * 32)
    o["s5_cB"] = cB.reshape(NJ, 16, 128, 2 * 2 * 128)
    o["s5_bB"] = bB.reshape(NJ, 16, 128, 2 * 2 * 128)
    return o


def prep_even(inp):
    o = {}
    NJ = inp["e_w_in"].shape[0]
    o["ev_win"] = np.ascontiguousarray(inp["e_w_in"].reshape(NJ, KC, 128, 3088).transpose(0, 2, 1, 3))
    o["ev_convw"] = np.ascontiguousarray(inp["e_conv_w"].reshape(NJ, 31, 4, 128).transpose(0, 3, 2, 1))
    cv = np.stack([inp["e_conv_b"], inp["e_cln_g"], inp["e_cln_b"]], axis=1)
    o["ev_cvec"] = np.ascontiguousarray(cv.reshape(NJ, 3, 4, 128).transpose(0, 3, 1, 2))
    o["ev_sconv"] = np.ascontiguousarray(inp["e_sconv_w"].reshape(NJ, 5, 12, 128).transpose(0, 3, 2, 1))
    dh = np.stack([inp["e_a_log"].reshape(NJ, 8), inp["e_dt_bias"].reshape(NJ, 8)], axis=1)
    o["ev_dh"] = np.ascontiguousarray(np.broadcast_to(dh[:, None], (NJ, 64, 2, 8)))
    o["ev_og"] = np.ascontiguousarray(inp["e_onorm_g"].reshape(NJ, 128, 1))
    o["ev_wout"] = np.ascontiguousarray(inp["e_w_out"].reshape(NJ, KC, 128, 1024).transpose(0, 2, 1, 3))
    c = np.zeros((128, 6, 128), np.float32)
    c[:, 0, :] = np.eye(128)
    a = np.arange(64)
    c[:64, 1, :64] = (a[:, None] <= a[None, :])
    c[:64, 2, :64] = (a[:, None] >= a[None, :])
    c[:64, 3, :64] = (a[:, None] > a[None, :])
    c[:64, 4, :64] = (a[:, None] < a[None, :])
    c[:, 5, :] = 1.0
    o["ev_const"] = c
    return o


def prep_core(inp, b):
    m = {}
    xa = np.concatenate([inp["ctx"][b], inp["x"][b]], axis=0)
    m["xin"] = np.ascontiguousarray(xa.T)
    cv = np.stack([fm(inp["c"][b]), fm(inp["c_ctx"])], axis=-1)
    m["cvec"] = np.ascontiguousarray(cv)
    return m


_CACHE = {}


def kernel(**inputs):
    inp = {k: np.asarray(v, dtype=np.float32) for k, v in inputs.items()}
    B = _CACHE.get("ncores", inp["x"].shape[0])
    if "nc" not in _CACHE:
        k = K()
        _CACHE["nc"] = k.build()
        _CACHE["need"] = set(k.din.keys())
    nc = _CACHE["nc"]
    need = _CACHE["need"]
    sh = prep_shared(inp, need)
    xover = _CACHE.get("xin_override")
    in_maps = []
    for b in range(B):
        m = dict(sh)
        m.update(prep_core(inp, b))
        if xover is not None:
            m["xin"] = xover
        in_maps.append({k: v for k, v in m.items() if k in need})
    _CACHE["in_maps"] = in_maps
    res = run_bass_kernel_spmd(nc, in_maps, core_ids=list(range(B)))
    _CACHE["res"] = res
    if "out" not in res.results[0]:
        return None
    out = np.stack([np.ascontiguousarray(r["out"].T) for r in res.results], axis=0)
    return out.astype(np.float32)
```

```python
import numpy as np
from contextlib import ExitStack
import concourse.bass as bass
import concourse.mybir as mybir
from concourse.bass_utils import run_bass_kernel_spmd

F32 = mybir.dt.float32
BF16 = mybir.dt.bfloat16
AF = mybir.ActivationFunctionType
ALU = mybir.AluOpType

D = 1024
KC = 8
NCTX = 256
NLAT = 4096
T = NCTX + NLAT
DFF = 2816
NI = DFF // 128
DEPTH = 4
EPS = 1e-6
INVBF_DEFAULT = 2
TILES = [(0, 256)] + [(256 + 512 * j, 512) for j in range(8)]
SUPER = [[0, 1, 2], [3, 4, 5], [6, 7, 8]]


class Buf:
    __slots__ = ("name", "w", "r")

    def __init__(self, name=""):
        self.name = name
        self.w = None
        self.r = {}


class Prog:
    ENG = ("pe", "act", "dve", "pool", "sp")

    def __init__(self, nc, n_dma_sems=48):
        self.nc = nc
        self.q = {k: [] for k in self.ENG}
        self.cnt = {k: 0 for k in self.ENG}
        self.seen = {k: {} for k in self.ENG}
        self.n_dma_sems = n_dma_sems
        self.dma_k = 0
        self.dma_last = {}
        self.n_ops = 0

    def _deps(self, eng, reads, writes, extra=()):
        need = {}

        def add(tok):
            if tok is None:
                return
            k, v = tok
            if need.get(k, 0) < v:
                need[k] = v

        for b in reads:
            add(b.w)
        for b in writes:
            add(b.w)
            for k, v in b.r.items():
                add((k, v))
        for t in extra:
            add(t)
        seen = self.seen[eng]
        waits = []
        for k, v in need.items():
            if k == "pe" and eng == "pe":
                continue
            if seen.get(k, 0) >= v:
                continue
            seen[k] = v
            waits.append((k, v))
        return waits

    def _commit(self, tok, reads, writes):
        k, v = tok
        for b in reads:
            if b.r.get(k, 0) < v:
                b.r[k] = v
        for b in writes:
            b.w = tok
            b.r = {}

    def op(self, eng, fn, reads=(), writes=()):
        waits = self._deps(eng, reads, writes)
        self.cnt[eng] += 1
        tok = (eng, self.cnt[eng])
        self.q[eng].append((waits, fn, tok, 1))
        self._commit(tok, reads, writes)
        self.n_ops += 1
        return tok

    def dma(self, eng, out_ap, in_ap, reads=(), writes=(), **kw):
        s = self.dma_k % self.n_dma_sems
        self.dma_k += 1
        prev = self.dma_last.get(s, 0)
        extra = [(("d", s), prev)] if prev else []
        waits = self._deps(eng, reads, writes, extra)
        val = prev + 16
        self.dma_last[s] = val
        tok = (("d", s), val)

        def fn(e, out_ap=out_ap, in_ap=in_ap, kw=kw):
            return e.dma_start(out=out_ap, in_=in_ap, **kw)

        self.q[eng].append((waits, fn, tok, 16))
        self._commit(tok, reads, writes)
        self.n_ops += 1
        return tok

    def barrier(self):
        toks = [(k, self.cnt[k]) for k in ("pe", "act", "dve", "pool") if self.cnt[k]]
        toks += [(("d", s), v) for s, v in self.dma_last.items()]
        for eng in self.ENG:
            waits = []
            seen = self.seen[eng]
            for k, v in toks:
                if seen.get(k, 0) >= v:
                    continue
                seen[k] = v
                waits.append((k, v))
            if waits:
                self.q[eng].append((waits, None, None, 0))

    def emit(self):
        nc = self.nc
        with ExitStack() as es:
            sems = {}
            for k in ("pe", "act", "dve", "pool"):
                sems[k] = es.enter_context(nc.semaphore("s_" + k))
            for s in range(min(self.n_dma_sems, max(1, self.dma_k))):
                sems[("d", s)] = es.enter_context(nc.semaphore("s_d%d" % s))
            fin = [(("d", s), v) for s, v in self.dma_last.items()]
            self.q["sp"].append((fin, None, None, 0))
            block = es.enter_context(nc.Block())

            def run(name, e):
                for waits, fn, tok, inc in self.q[name]:
                    for k, v in waits:
                        e.wait_ge(sems[k], v)
                    if fn is not None:
                        ins = fn(e)
                        ins.then_inc(sems[tok[0]], inc)

            @block.sync
            def _(e):
                run("sp", e)

            @block.tensor
            def _(e):
                run("pe", e)

            @block.scalar
            def _(e):
                run("act", e)

            @block.vector
            def _(e):
                run("dve", e)

            @block.gpsimd
            def _(e):
                run("pool", e)


class Ring:
    def __init__(self, tiles):
        self.tiles = tiles
        self.bufs = [Buf() for _ in tiles]
        self.i = 0

    def next(self):
        t, b = self.tiles[self.i], self.bufs[self.i]
        self.i = (self.i + 1) % len(self.tiles)
        return t, b


class K:
    def __init__(self, stages=None, layers=None, dump=False):
        self.stages = stages
        self.layers = list(range(DEPTH)) if layers is None else layers
        self.dump = dump
        nc = self.nc = bass.Bass("TRN2", target_bir_lowering=False)
        self.P = Prog(nc)
        self.es = ExitStack()
        self.din = {}
        self.n_alloc = 0

    def dram_in(self, name, shape):
        ap = self.nc.dram_tensor(name, list(shape), F32, kind="ExternalInput").ap()
        self.din[name] = ap
        return ap

    def sb(self, shape, dtype=F32, es=None, name=None):
        self.n_alloc += 1
        nm = name or ("t%d" % self.n_alloc)
        return (es or self.es).enter_context(self.nc.sbuf_tensor(nm, list(shape), dtype))

    def ring(self, n, shape, dtype=F32, es=None):
        return Ring([self.sb(shape, dtype, es) for _ in range(n)])

    def psum(self):
        return self.ps_ring.next()

    def load_cast(self, dst, b_dst, src, stg, eng="pool"):
        F = dst.shape[-1]
        st, b_st = stg.next()
        self.P.dma("sp", st[:, :F], src, writes=[b_st])
        self.cp(eng, (dst, b_dst), (st[:, :F], b_st))

    def build(self):
        nc, P = self.nc, self.P
        self.xin = self.dram_in("xin", [D, T])
        self.cvec = self.dram_in("cvec", [128, KC, 2])
        self.wmod = self.dram_in("wmod", [DEPTH, 18, 128, KC * 512])
        self.bmod = self.dram_in("bmod", [DEPTH, 128, 72])
        self.ng2 = self.dram_in("ng2", [DEPTH, 128, 3 * KC * 2])
        self.fg = self.dram_in("fg", [128, KC])
        if self.stages is None or "ffn" in self.stages:
            self.w13 = self.dram_in("w13", [DEPTH, 2, NI, 128, KC * 256])
            self.w2 = self.dram_in("w2", [DEPTH, 2, KC, 128, NI * 128])
        self.out = nc.dram_tensor("out", [D, NLAT], F32, kind="ExternalOutput").ap()
        self.xT = nc.dram_tensor("xT_scratch", [D, T], F32, kind="Internal").ap()
        self.xT_v = self.xT.rearrange("(kc p) t -> p kc t", p=128)
        self.xin_v = self.xin.rearrange("(kc p) t -> p kc t", p=128)
        self.out_v = self.out.rearrange("(kc p) t -> p kc t", p=128)
        self.xbuf = [[Buf("x%d_%d" % (kc, t)) for t in range(len(TILES))] for kc in range(KC)]

        self.ps_ring = Ring([self.es.enter_context(nc.psum_tensor("ps%d" % i, [128, 512], F32))
                             for i in range(8)])
        self.ones_bf = self.sb([128, 128], BF16)
        self.b_const = Buf("const")
        P.op("dve", lambda e: e.memset(self.ones_bf[:], 1.0), writes=[self.b_const])
        self.eps_t = self.sb([128, 1])
        P.op("dve", lambda e: e.memset(self.eps_t[:], EPS), writes=[self.b_const])
        self.sc_bf = self.sb([128, KC, 2], BF16)
        self.b_sc = Buf()
        cv = self.sb([128, KC, 2])
        b_cv = Buf()
        P.dma("sp", cv[:], self.cvec, writes=[b_cv])
        P.op("act", lambda e: e.activation(out=self.sc_bf[:], in_=cv[:], func=AF.Silu),
             reads=[b_cv], writes=[self.b_sc])
        self.fg_t = self.sb([128, KC])
        self.b_fg = Buf()
        P.dma("sp", self.fg_t[:], self.fg, writes=[self.b_fg])
        self.modv = self.sb([128, 9, KC, 2])
        self.gs = self.sb([128, 3, KC, 2])
        self.geff = self.sb([128, 3, KC, 2])
        self.b_mod = Buf("mod")

        self.decl_mixer_inputs()
        first = True
        for l in self.layers:
            self.mod_stage(l)
            self.ffn_stage(l, 0, src_in=first)
            first = False
            self.mixer_stage(l)
            self.ffn_stage(l, 1)
        if self.dump:
            self.dump_stage()
        else:
            self.final_stage()
        P.emit()
        self.es.close()
        return nc

    def mod_stage(self, l):
        nc, P = self.nc, self.P
        es = ExitStack()
        wr = self.ring(2, [128, KC, 512], BF16, es)
        wstg = self.ring(2, [128, KC * 512], F32, es)
        bm = self.sb([128, 72], F32, es)
        ng = self.sb([128, 3, KC, 2], F32, es)
        b_bm, b_ng = Buf(), Buf()
        P.dma("sp", bm[:], self.bmod[l], writes=[b_bm])
        P.dma("sp", ng[:].rearrange("p a k s -> p (a k s)"), self.ng2[l], writes=[b_ng])
        ps, b_ps = self.psum()
        for blk in range(18):
            wt, b_w = wr.next()
            self.load_cast(wt[:].rearrange("p k c -> p (k c)"), b_w, self.wmod[l, blk], wstg, eng="dve")
            for fc in range(4):
                c = blk * 4 + fc
                for kc in range(KC):
                    P.op("pe", lambda e, wt=wt, fc=fc, kc=kc, c=c: e.matmul(
                        ps[:, 2 * c:2 * c + 2], wt[:, kc, fc * 128:(fc + 1) * 128], self.sc_bf[:, kc, :],
                        start=(kc == 0), stop=(kc == KC - 1)),
                        reads=[b_w, self.b_sc], writes=[b_ps])
        modv = self.modv
        for s in range(2):
            P.op("dve", lambda e, s=s: e.tensor_tensor(
                out=modv[:, :, :, s].rearrange("p m k -> p (m k)"),
                in0=ps[:, 0:144].rearrange("p (c s) -> p c s", s=2)[:, :, s],
                in1=bm[:], op=ALU.add),
                reads=[b_ps, b_bm], writes=[self.b_mod])
        for i in range(3):
            P.op("dve", lambda e, i=i: e.scalar_tensor_tensor(
                out=self.gs[:, i].rearrange("p k s -> p (k s)"),
                in0=modv[:, 3 * i + 1].rearrange("p k s -> p (k s)"), scalar=1.0,
                in1=ng[:, i].rearrange("p k s -> p (k s)"), op0=ALU.add, op1=ALU.mult),
                reads=[self.b_mod, b_ng], writes=[self.b_mod])
            gsc = 0.5 if i != 1 else 1.0
            P.op("dve", lambda e, i=i, gsc=gsc: e.tensor_scalar(
                out=self.geff[:, i].rearrange("p k s -> p (k s)"),
                in0=modv[:, 3 * i + 2].rearrange("p k s -> p (k s)"),
                scalar1=gsc, scalar2=None, op0=ALU.mult),
                reads=[self.b_mod], writes=[self.b_mod])
        P.barrier()
        es.close()

    def modulate(self, xt, b_x, n, i_norm, s, h_ap_fn, b_h, rs):
        P = self.P
        sq, b_sq = rs["sq"].next()
        P.op("act", lambda e: e.activation(out=sq[:, :, :n], in_=xt[:, :, :n], func=AF.Square),
             reads=[b_x], writes=[b_sq])
        ps, b_ps = self.psum()
        for kc in range(KC):
            P.op("pe", lambda e, kc=kc: e.matmul(ps[:, :n], self.ones_bf[:], sq[:, kc, :n],
                                                 start=(kc == 0), stop=(kc == KC - 1)),
                 reads=[b_sq, self.b_const], writes=[b_ps])
        rstd, b_r = rs["rstd"].next()
        P.op("act", lambda e: e.activation(out=rstd[:, :n], in_=ps[:, :n], func=AF.Sqrt,
                                           scale=1.0 / D, bias=self.eps_t[:, 0:1]),
             reads=[b_ps, self.b_const], writes=[b_r])
        P.op("dve", lambda e: e.reciprocal(out=rstd[:, :n], in_=rstd[:, :n]), reads=[b_r], writes=[b_r])
        for kc in range(KC):
            tmp, b_t = rs["tmp"].next()
            P.op("dve", lambda e, kc=kc, tmp=tmp: e.tensor_tensor(
                out=tmp[:, :n], in0=xt[:, kc, :n], in1=rstd[:, :n], op=ALU.mult),
                reads=[b_x, b_r], writes=[b_t])
            if i_norm is None:
                P.op("act", lambda e, kc=kc, tmp=tmp: e.activation(
                    out=h_ap_fn(kc), in_=tmp[:, :n], func=AF.Identity,
                    scale=self.fg_t[:, kc:kc + 1]),
                    reads=[b_t, self.b_fg], writes=[b_h])
            else:
                P.op("act", lambda e, kc=kc, tmp=tmp: e.activation(
                    out=h_ap_fn(kc), in_=tmp[:, :n], func=AF.Identity,
                    scale=self.gs[:, i_norm, kc, s:s + 1], bias=self.modv[:, 3 * i_norm, kc, s:s + 1]),
                    reads=[b_t, self.b_mod], writes=[b_h])

    def mk_scratch(self, es):
        return {"sq": self.ring(2, [128, KC, 512], BF16, es),
                "rstd": self.ring(2, [128, 512], F32, es),
                "tmp": self.ring(3, [128, 512], F32, es)}

    def load_x(self, xr, t, src_in=False):
        s0, n = TILES[t]
        xt, b_x = xr.next()
        src = self.xin_v if src_in else self.xT_v
        self.P.dma("sp", xt[:, :, :n], src[:, :, s0:s0 + n],
                   reads=([] if src_in else [self.xbuf[kc][t] for kc in range(KC)]), writes=[b_x])
        return xt, b_x

    def ffn_stage(self, l, f, src_in=False):
        if self.stages is not None and "ffn" not in self.stages:
            if src_in:
                self.copy_in()
            return
        nc, P = self.nc, self.P
        i_norm = 0 if f == 0 else 2
        es = ExitStack()
        rs = self.mk_scratch(es)
        xr = self.ring(1, [128, KC, 512], F32, es)
        wstg = self.ring(2, [128, NI * 128], F32, es)
        TS = 1536
        h = self.sb([128, KC, TS], BF16, es)
        act = self.sb([128, NI, TS], BF16, es)
        w13r = self.ring(3, [128, KC, 256], BF16, es)
        w2r = self.ring(2, [128, NI, 128], BF16, es)
        sar = self.ring(2, [128, 512], F32, es)
        xor_ = self.ring(3, [128, 512], F32, es)
        xnr = self.ring(3, [128, 512], F32, es)
        b_h = [Buf() for _ in range(3)]
        b_act = [[Buf() for _ in range(3)] for _ in range(NI)]
        src_v = self.xin_v if src_in else self.xT_v
        for st in SUPER:
            offs = []
            off = 0
            for ti, t in enumerate(st):
                s0, n = TILES[t]
                s = 1 if t == 0 else 0
                xt, b_x = self.load_x(xr, t, src_in)
                self.modulate(xt, b_x, n, i_norm, s,
                              (lambda kc, off=off, n=n: h[:, kc, off:off + n]), b_h[ti], rs)
                offs.append(off)
                off += n
            for i in range(NI):
                wt, b_w = w13r.next()
                self.load_cast(wt[:].rearrange("p k c -> p (k c)"), b_w, self.w13[l, f, i], wstg)
                for ti, t in enumerate(st):
                    s0, n = TILES[t]
                    off = offs[ti]
                    pa, b_pa = self.psum()
                    pb, b_pb = self.psum()
                    for (pp, b_pp, c0) in ((pa, b_pa, 0), (pb, b_pb, 128)):
                        for kc in range(KC):
                            P.op("pe", lambda e, pp=pp, c0=c0, kc=kc, wt=wt, off=off, n=n: e.matmul(
                                pp[:, :n], wt[:, kc, c0:c0 + 128], h[:, kc, off:off + n],
                                start=(kc == 0), stop=(kc == KC - 1)),
                                reads=[b_w, b_h[ti]], writes=[b_pp])
                    sa, b_sa = sar.next()
                    P.op("act", lambda e, sa=sa, pa=pa, n=n: e.activation(
                        out=sa[:, :n], in_=pa[:, :n], func=AF.Silu), reads=[b_pa], writes=[b_sa])
                    P.op("dve", lambda e, sa=sa, pb=pb, n=n, i=i, off=off: e.tensor_tensor(
                        out=act[:, i, off:off + n], in0=sa[:, :n], in1=pb[:, :n], op=ALU.mult),
                        reads=[b_sa, b_pb], writes=[b_act[i][ti]])
            for oc in range(KC):
                wt, b_w = w2r.next()
                self.load_cast(wt[:].rearrange("p i c -> p (i c)"), b_w, self.w2[l, f, oc], wstg)
                for ti, t in enumerate(st):
                    s0, n = TILES[t]
                    s = 1 if t == 0 else 0
                    off = offs[ti]
                    po, b_po = self.psum()
                    for i in range(NI):
                        P.op("pe", lambda e, po=po, i=i, wt=wt, off=off, n=n: e.matmul(
                            po[:, :n], wt[:, i, :], act[:, i, off:off + n],
                            start=(i == 0), stop=(i == NI - 1)),
                            reads=[b_w, b_act[i][ti]], writes=[b_po])
                    xo, b_xo = xor_.next()
                    P.dma("sp", xo[:, :n], src_v[:, oc, s0:s0 + n],
                          reads=([] if src_in else [self.xbuf[oc][t]]), writes=[b_xo])
                    xn, b_xn = xnr.next()
                    P.op("dve", lambda e, xn=xn, po=po, xo=xo, n=n, oc=oc, s=s: e.scalar_tensor_tensor(
                        out=xn[:, :n], in0=po[:, :n], scalar=self.geff[:, i_norm, oc, s:s + 1],
                        in1=xo[:, :n], op0=ALU.mult, op1=ALU.add),
                        reads=[b_po, b_xo, self.b_mod], writes=[b_xn])
                    P.dma("sp", self.xT_v[:, oc, s0:s0 + n], xn[:, :n], reads=[b_xn],
                          writes=[self.xbuf[oc][t]])
        P.barrier()
        es.close()

    def copy_in(self):
        P = self.P
        es = ExitStack()
        xr = self.ring(2, [128, KC, 512], F32, es)
        for t, (s0, n) in enumerate(TILES):
            xt, b_x = xr.next()
            P.dma("sp", xt[:, :, :n], self.xin_v[:, :, s0:s0 + n], writes=[b_x])
            P.dma("sp", self.xT_v[:, :, s0:s0 + n], xt[:, :, :n], reads=[b_x],
                  writes=[self.xbuf[kc][t] for kc in range(KC)])
        P.barrier()
        es.close()

    def mixer_stage(self, l):
        if self.stages is not None and "mix" not in self.stages:
            return
        if l % 2 == 1:
            self.s5_stage(l)
        else:
            self.even_stage(l)

    def dump_stage(self):
        P = self.P
        es = ExitStack()
        self.xdump = self.nc.dram_tensor("xdump", [D, T], F32, kind="ExternalOutput").ap()
        xd_v = self.xdump.rearrange("(kc p) t -> p kc t", p=128)
        xr = self.ring(2, [128, KC, 512], F32, es)
        for t in range(len(TILES)):
            s0, n = TILES[t]
            xt, b_x = self.load_x(xr, t)
            P.dma("sp", xd_v[:, :, s0:s0 + n], xt[:, :, :n], reads=[b_x])
        P.barrier()
        es.close()

    def tt(self, eng, out, a, b, op):
        self.P.op(eng, lambda e: e.tensor_tensor(out=out[0], in0=a[0], in1=b[0], op=op),
                  reads=[a[1], b[1]], writes=[out[1]])

    def ts(self, eng, out, a, s1, op0, s2=None, op1=None, sb=()):
        s1a = s1[0] if isinstance(s1, tuple) else s1
        s2a = s2[0] if isinstance(s2, tuple) else s2
        rd = [a[1]] + [x[1] for x in (s1, s2) if isinstance(x, tuple)]
        if op1 is None:
            self.P.op(eng, lambda e: e.tensor_scalar(out=out[0], in0=a[0], scalar1=s1a, scalar2=None, op0=op0),
                      reads=rd, writes=[out[1]])
        else:
            self.P.op(eng, lambda e: e.tensor_scalar(out=out[0], in0=a[0], scalar1=s1a, scalar2=s2a,
                                                     op0=op0, op1=op1), reads=rd, writes=[out[1]])

    def stt(self, out, a, sc, b, op0, op1):
        sca = sc[0] if isinstance(sc, tuple) else sc
        rd = [a[1], b[1]] + ([sc[1]] if isinstance(sc, tuple) else [])
        self.P.op("dve", lambda e: e.scalar_tensor_tensor(out=out[0], in0=a[0], scalar=sca, in1=b[0],
                                                          op0=op0, op1=op1), reads=rd, writes=[out[1]])

    def actf(self, out, a, func, scale=1.0, bias=None):
        sca = scale[0] if isinstance(scale, tuple) else scale
        rd = [a[1]] + [x[1] for x in (scale, bias) if isinstance(x, tuple)]
        if bias is None:
            self.P.op("act", lambda e: e.activation(out=out[0], in_=a[0], func=func, scale=sca),
                      reads=rd, writes=[out[1]])
        else:
            ba = bias[0] if isinstance(bias, tuple) else bias
            self.P.op("act", lambda e: e.activation(out=out[0], in_=a[0], func=func, scale=sca, bias=ba),
                      reads=rd, writes=[out[1]])

    def cp(self, eng, out, a):
        if eng == "act":
            self.P.op("act", lambda e: e.activation(out=out[0], in_=a[0], func=AF.Identity),
                      reads=[a[1]], writes=[out[1]])
        else:
            self.P.op(eng, lambda e: e.tensor_copy(out=out[0], in_=a[0]), reads=[a[1]], writes=[out[1]])

    def mm(self, ps, lhsT, rhs, start, stop):
        self.P.op("pe", lambda e: e.matmul(ps[0], lhsT[0], rhs[0], start=start, stop=stop),
                  reads=[lhsT[1], rhs[1]], writes=[ps[1]])


    def decl_mixer_inputs(self):
        if self.stages is not None and "mix" not in self.stages:
            return
        if any(l % 2 == 1 for l in self.layers):
            self.s5_win = self.dram_in("s5_win", [2, 8, 128, KC * 128])
            self.s5_lamA = self.dram_in("s5_lamA", [2, 8, 128, 3 * 512])
            self.s5_bA = self.dram_in("s5_bA", [2, 8, 128, 2 * 512])
            self.s5_lamB = self.dram_in("s5_lamB", [2, 128, 3 * 32])
            self.s5_cB = self.dram_in("s5_cB", [2, 16, 128, 2 * 2 * 128])
            self.s5_bB = self.dram_in("s5_bB", [2, 16, 128, 2 * 2 * 128])
            self.s5_dd = self.dram_in("s5_dd", [2, 8, 128, 128])
            self.s5_wout = self.dram_in("s5_wout", [2, 8, 128, 2048])
        if any(l % 2 == 0 for l in self.layers):
            self.decl_even_inputs()

    def discretize(self, lre, lim, lst, F, es, tag):
        PI = float(np.pi)

        def new(dtype=F32):
            return (self.sb([128, F], dtype, es)[:], Buf())

        cache = getattr(self, "_disc_cache", None)
        if cache is None:
            cache = self._disc_cache = {}
        key = (tag, F, id(es))
        if key not in cache:
            cache[key] = [new() for _ in range(9)] + [new(mybir.dt.int32)]
        A, B, C, Dd, E_, F_, G, H, I, KI = cache[key]
        dt, lr = A, B
        self.actf(dt, lst, AF.Exp)
        self.ts("dve", lr, lre, -1e-4, ALU.min)
        self.tt("dve", C, lr, dt, ALU.mult)
        mag = Dd
        self.actf(mag, C, AF.Exp)
        th = C
        self.tt("dve", th, lim, dt, ALU.mult)
        outs = []
        for sh, o in ((0.0, I), (PI / 2, A)):
            w, kf, r, m = E_, F_, G, H
            self.ts("dve", w, th, sh, ALU.add)
            self.ts("dve", KI, w, 1.0 / (2 * PI), ALU.mult)
            self.cp("dve", kf, KI)
            self.stt(r, kf, -2 * PI, w, ALU.mult, ALU.add)
            self.ts("dve", m, r, PI, ALU.is_gt, -2 * PI, ALU.mult)
            self.tt("dve", r, r, m, ALU.add)
            self.ts("dve", r, r, PI, ALU.min, -PI, ALU.max)
            self.actf(o, r, AF.Sin)
            outs.append(o)
        sn, cs = outs
        are, aim = E_, F_
        self.tt("dve", are, mag, cs, ALU.mult)
        self.tt("dve", aim, mag, sn, ALU.mult)
        self.tt("dve", G, lr, lr, ALU.mult)
        self.tt("dve", H, lim, lim, ALU.mult)
        self.tt("dve", H, H, G, ALU.add)
        self.P.op("dve", lambda e: e.reciprocal(out=H[0], in_=H[0]), reads=[H[1]], writes=[H[1]])
        nre = Dd
        self.ts("dve", nre, are, -1.0, ALU.add)
        cre, cim = I, A
        self.tt("dve", G, nre, lr, ALU.mult)
        self.tt("dve", C, aim, lim, ALU.mult)
        self.tt("dve", G, G, C, ALU.add)
        self.tt("dve", cre, G, H, ALU.mult)
        self.tt("dve", G, aim, lr, ALU.mult)
        self.tt("dve", C, nre, lim, ALU.mult)
        self.tt("dve", G, G, C, ALU.subtract)
        self.tt("dve", cim, G, H, ALU.mult)
        return {"are": are, "aim": aim, "cre": cre, "cim": cim, "t1": G, "t2": C}

    def cmul(self, orr, oi, ar, ai, br, bi, t1, t2):
        self.tt("dve", t1, ar, br, ALU.mult)
        self.tt("dve", t2, ai, bi, ALU.mult)
        self.tt("dve", orr, t1, t2, ALU.subtract)
        self.tt("dve", t1, ar, bi, ALU.mult)
        self.tt("dve", t2, ai, br, ALU.mult)
        self.tt("dve", oi, t1, t2, ALU.add)

    def s5_stage(self, l):
        nc, P = self.nc, self.P
        j = l // 2
        NCH = T // 8
        HALF = NCH // 2
        es = ExitStack()
        U = self.sb([128, 8, T], BF16, es)
        b_U = [Buf() for _ in range(8)]
        Up = [U[:, tq, :].rearrange("p (i c) -> p i c", i=8) for tq in range(8)]

        es0 = ExitStack()
        rs = self.mk_scratch(es0)
        xr = self.ring(2, [128, KC, 512], F32, es0)
        hr = self.ring(2, [128, KC, 512], BF16, es0)
        win = self.sb([128, 8, KC, 128], BF16, es0)
        b_win = Buf()
        wstg0 = self.ring(2, [128, KC * 128], F32, es0)
        for tq in range(8):
            self.load_cast(win[:, tq].rearrange("p k c -> p (k c)"), b_win, self.s5_win[j, tq], wstg0, eng="dve")
        for t, (s0, n) in enumerate(TILES):
            sidx = 1 if t == 0 else 0
            xt, b_x = self.load_x(xr, t)
            h, b_h = hr.next()
            self.modulate(xt, b_x, n, 1, sidx, (lambda kc, h=h, n=n: h[:, kc, :n]), b_h, rs)
            for tq in range(8):
                ps, b_ps = self.psum()
                for kc in range(KC):
                    self.mm((ps[:, :n], b_ps), (win[:, tq, kc, :], b_win), (h[:, kc, :n], b_h),
                            kc == 0, kc == KC - 1)
                c0, nc_ = s0 // 8, n // 8
                self.cp("act" if tq % 2 == 0 else "dve",
                        (Up[tq][:, :, c0:c0 + nc_].rearrange("p i c -> p c i"), b_U[tq]),
                        (ps[:, :n].rearrange("p (c i) -> p c i", i=8), b_ps))
        P.barrier()
        es0.close()

        esB = ExitStack()
        lamB = self.sb([128, 3, 32], F32, esB)
        b_lamB = Buf()
        P.dma("sp", lamB[:].rearrange("p a f -> p (a f)"), self.s5_lamB[j], writes=[b_lamB])
        dB = self.discretize((lamB[:, 0, :], b_lamB), (lamB[:, 1, :], b_lamB), (lamB[:, 2, :], b_lamB), 32, esB, "B")
        PW = self.sb([128, 9, 2, 32], F32, esB)
        NPW = self.sb([128, 9, 2, 32], F32, esB)
        b_PW = Buf()
        P.op("dve", lambda e: e.memset(PW[:, 0, 0, :], 1.0), writes=[b_PW])
        P.op("dve", lambda e: e.memset(PW[:, 0, 1, :], 0.0), writes=[b_PW])
        self.cp("dve", (PW[:, 1, 0, :], b_PW), dB["are"])
        self.cp("dve", (PW[:, 1, 1, :], b_PW), dB["aim"])
        for m in range(2, 9):
            self.cmul((PW[:, m, 0, :], b_PW), (PW[:, m, 1, :], b_PW), (PW[:, m - 1, 0, :], b_PW),
                      (PW[:, m - 1, 1, :], b_PW), dB["are"], dB["aim"], dB["t1"], dB["t2"])
        self.ts("dve", (NPW[:].rearrange("p m r f -> p (m r f)"), b_PW),
                (PW[:].rearrange("p m r f -> p (m r f)"), b_PW), -1.0, ALU.mult)
        ARAR = self.sb([128, 2, 32], F32, esB)
        NAIAI = self.sb([128, 2, 32], F32, esB)
        b_A2 = Buf()
        for d in range(2):
            src_r = PW[:, 8, 0, :].rearrange("p (q d) -> p q d", d=2)[:, :, d]
            src_i = PW[:, 8, 1, :].rearrange("p (q d) -> p q d", d=2)[:, :, d]
            nsrc_i = NPW[:, 8, 1, :].rearrange("p (q d) -> p q d", d=2)[:, :, d]
            self.cp("dve", (ARAR[:, d, 0:16], b_A2), (src_r, b_PW))
            self.cp("dve", (ARAR[:, d, 16:32], b_A2), (src_r, b_PW))
            self.cp("dve", (NAIAI[:, d, 0:16], b_A2), (nsrc_i, b_PW))
            self.cp("dve", (NAIAI[:, d, 16:32], b_A2), (src_i, b_PW))

        E = self.sb([128, 2, 32, NCH], BF16, esB)
        b_E = [Buf(), Buf()]
        es1 = ExitStack()
        rawA = self.ring(2, [128, 3, 256], F32, es1)
        rawb = self.ring(2, [128, 2, 256], F32, es1)
        EBr = self.ring(2, [128, 8, 2, 256], BF16, es1)
        cur = [self.sb([128, 2, 256], F32, es1) for _ in range(2)]
        b_cur = [Buf(), Buf()]
        lamA_v = self.s5_lamA.rearrange("j q p (a f) -> j q p a f", a=3)
        bA_v = self.s5_bA.rearrange("j q p (a f) -> j q p a f", a=2)
        for tq in range(8):
            for pr in range(2):
                Pp = 2 * tq + pr
                es_t = es1
                la, b_la = rawA.next()
                rb, b_rb = rawb.next()
                P.dma("sp", la[:], lamA_v[j, tq, :, :, pr * 256:(pr + 1) * 256], writes=[b_la])
                P.dma("sp", rb[:], bA_v[j, tq, :, :, pr * 256:(pr + 1) * 256], writes=[b_rb])
                dA = self.discretize((la[:, 0, :], b_la), (la[:, 1, :], b_la), (la[:, 2, :], b_la), 256, es_t, "A")
                EB, b_EB = EBr.next()
                self.cmul((cur[0][:, 0, :], b_cur[0]), (cur[0][:, 1, :], b_cur[0]), dA["cre"], dA["cim"],
                          (rb[:, 0, :], b_rb), (rb[:, 1, :], b_rb), dA["t1"], dA["t2"])
                for k in range(8):
                    c0, c1 = cur[k % 2], cur[(k + 1) % 2]
                    self.cp("act", (EB[:, k].rearrange("p r f -> p (r f)"), b_EB),
                            (c0[:].rearrange("p r f -> p (r f)"), b_cur[k % 2]))
                    if k < 7:
                        self.cmul((c1[:, 0, :], b_cur[(k + 1) % 2]), (c1[:, 1, :], b_cur[(k + 1) % 2]),
                                  dA["are"], dA["aim"], (c0[:, 0, :], b_cur[k % 2]), (c0[:, 1, :], b_cur[k % 2]),
                                  dA["t1"], dA["t2"])
                for d in range(2):
                    for ri in range(2):
                        for hf in range(2):
                            ps, b_ps = self.psum()
                            for ip in range(8):
                                k = 7 - ip if d == 0 else ip
                                col = d * 128
                                self.mm((ps[:, :HALF], b_ps), (EB[:, k, ri, col:col + 128], b_EB),
                                        (Up[tq][:, ip, hf * HALF:(hf + 1) * HALF], b_U[tq]), ip == 0, ip == 7)
                            self.cp("act" if (ri + hf) % 2 == 0 else "dve",
                                    (E[:, d, ri * 16 + Pp, hf * HALF:(hf + 1) * HALF], b_E[d]), (ps[:, :HALF], b_ps))
        P.barrier()
        es1.close()

        es2 = ExitStack()
        SS = [[self.sb([128, 48], F32, es2) for _ in range(2)] for _ in range(2)]
        b_SS = [[Buf(), Buf()], [Buf(), Buf()]]
        T1 = [self.ring(2, [128, 32], F32, es2) for _ in range(2)]
        T2 = [self.ring(2, [128, 32], F32, es2) for _ in range(2)]
        for d in range(2):
            for k in range(2):
                P.op("dve", lambda e, d=d, k=k: e.memset(SS[d][k][:], 0.0), writes=[b_SS[d][k]])
        order = [list(range(NCH)), list(range(31, -1, -1)) + list(range(NCH - 1, 31, -1))]
        for step in range(NCH):
            for d in range(2):
                c = order[d][step]
                cur, nxt = SS[d][step % 2], SS[d][(step + 1) % 2]
                b_cur, b_nxt = b_SS[d][step % 2], b_SS[d][(step + 1) % 2]
                t1, b_t1 = T1[d].next()
                t2, b_t2 = T2[d].next()
                ec = (E[:, d, :, c], b_E[d])
                self.tt("dve", (t1[:], b_t1), (ARAR[:, d, :], b_A2), (cur[:, 0:32], b_cur), ALU.mult)
                self.tt("dve", (t2[:], b_t2), (NAIAI[:, d, :], b_A2), (cur[:, 16:48], b_cur), ALU.mult)
                self.tt("dve", (t1[:], b_t1), (t1[:], b_t1), (t2[:], b_t2), ALU.add)
                self.tt("dve", (nxt[:, 0:32], b_nxt), (t1[:], b_t1), ec, ALU.add)
                self.cp("act", ec, (cur[:, 0:32], b_cur))
                self.cp("act", (nxt[:, 32:48], b_nxt), (nxt[:, 0:16], b_nxt))
        P.barrier()
        es2.close()

        es3 = ExitStack()
        craw = self.ring(2, [128, 2, 2, 128], F32, es3)
        braw = self.ring(2, [128, 2, 2, 128], F32, es3)
        CAr = self.ring(2, [128, 2, 9, 2, 128], BF16, es3)
        BBr = self.ring(2, [128, 2, 2, 128], BF16, es3)
        Ktr = self.ring(2, [128, 15, 128], BF16, es3)
        ddr = self.ring(2, [128, 128], F32, es3)
        zst = self.ring(1, [128, T], BF16, es3)
        tmpc = self.ring(8, [128, 128], F32, es3)
        for tq in range(8):
            CA = []; BB = []
            for pr in range(2):
                Pp = 2 * tq + pr
                cr, b_cr = craw.next()
                br_, b_br = braw.next()
                P.dma("sp", cr[:].rearrange("p a d f -> p (a d f)"), self.s5_cB[j, Pp], writes=[b_cr])
                P.dma("sp", br_[:].rearrange("p a d f -> p (a d f)"), self.s5_bB[j, Pp], writes=[b_br])
                ca, b_ca = CAr.next()
                bb, b_bb = BBr.next()
                for d in range(2):
                    col = Pp * 2 + d
                    cre = (dB["cre"][0][:, col:col + 1], dB["cre"][1])
                    cim = (dB["cim"][0][:, col:col + 1], dB["cim"][1])
                    tA, b_tA = tmpc.next()
                    self.actf((tA[:], b_tA), (br_[:, 0, d, :], b_br), AF.Identity, cre)
                    ncim, b_nc = tmpc.next()
                    self.ts("dve", (ncim[:, 0:1], b_nc), cim, -1.0, ALU.mult)
                    self.stt((bb[:, d, 0, :], b_bb), (br_[:, 1, d, :], b_br), (ncim[:, 0:1], b_nc), (tA[:], b_tA),
                             ALU.mult, ALU.add)
                    tB, b_tB = tmpc.next()
                    self.actf((tB[:], b_tB), (br_[:, 1, d, :], b_br), AF.Identity, cre)
                    self.stt((bb[:, d, 1, :], b_bb), (br_[:, 0, d, :], b_br), cim, (tB[:], b_tB), ALU.mult, ALU.add)
                    for m in range(9):
                        are_m = (PW[:, m, 0, col:col + 1], b_PW)
                        aim_m = (PW[:, m, 1, col:col + 1], b_PW)
                        nare_m = (NPW[:, m, 0, col:col + 1], b_PW)
                        naim_m = (NPW[:, m, 1, col:col + 1], b_PW)
                        tC, b_tC = tmpc.next()
                        self.actf((tC[:], b_tC), (cr[:, 0, d, :], b_cr), AF.Identity, are_m)
                        self.stt((ca[:, d, m, 0, :], b_ca), (cr[:, 1, d, :], b_cr), naim_m, (tC[:], b_tC),
                                 ALU.mult, ALU.add)
                        tD, b_tD = tmpc.next()
                        self.actf((tD[:], b_tD), (cr[:, 0, d, :], b_cr), AF.Identity, naim_m)
                        self.stt((ca[:, d, m, 1, :], b_ca), (cr[:, 1, d, :], b_cr), nare_m, (tD[:], b_tD),
                                 ALU.mult, ALU.add)
                CA.append((ca, b_ca)); BB.append((bb, b_bb))
            Kt, b_Kt = Ktr.next()
            dd, b_dd = ddr.next()
            P.dma("sp", dd[:], self.s5_dd[j, tq], writes=[b_dd])
            for d in range(2):
                for m in range(1, 8):
                    ps, b_ps = self.psum()
                    n_mm = 0
                    for pr in range(2):
                        for ri in range(2):
                            self.mm((ps[:, :128], b_ps), (BB[pr][0][:, d, ri, :], BB[pr][1]),
                                    (CA[pr][0][:, d, m, ri, :], CA[pr][1]), n_mm == 0, n_mm == 3)
                            n_mm += 1
                    mi = 7 + m if d == 0 else 7 - m
                    self.cp("act", (Kt[:, mi, :], b_Kt), (ps[:, :128], b_ps))
            ps, b_ps = self.psum()
            n_mm = 0
            for d in range(2):
                for pr in range(2):
                    for ri in range(2):
                        self.mm((ps[:, :128], b_ps), (BB[pr][0][:, d, ri, :], BB[pr][1]),
                                (CA[pr][0][:, d, 0, ri, :], CA[pr][1]), n_mm == 0, n_mm == 7)
                        n_mm += 1
            self.tt("dve", (Kt[:, 7, :], b_Kt), (ps[:, :128], b_ps), (dd[:], b_dd), ALU.add)
            z, b_z = zst.next()
            zc = z[:].rearrange("p (c i) -> p c i", i=8)
            for i in range(8):
                for hf in range(2):
                    ps, b_ps = self.psum()
                    n_mm = 0
                    for ip in range(8):
                        self.mm((ps[:, :HALF], b_ps), (Kt[:, 7 + (i - ip), :], b_Kt),
                                (Up[tq][:, ip, hf * HALF:(hf + 1) * HALF], b_U[tq]), n_mm == 0, False)
                        n_mm += 1
                    for pr in range(2):
                        Pp = 2 * tq + pr
                        for d in range(2):
                            m = i + 1 if d == 0 else 8 - i
                            for ri in range(2):
                                n_mm += 1
                                self.mm((ps[:, :HALF], b_ps), (CA[pr][0][:, d, m, ri, :], CA[pr][1]),
                                        (E[:, d, ri * 16 + Pp, hf * HALF:(hf + 1) * HALF], b_E[d]),
                                        False, n_mm == 16)
                    self.actf((zc[:, hf * HALF:(hf + 1) * HALF, i], b_z), (ps[:, :HALF], b_ps), AF.Gelu_apprx_tanh)
            self.cp("pool", (U[:, tq, :], b_U[tq]), (z[:], b_z))
        P.barrier()
        es3.close()
        esB.close()

        es4 = ExitStack()
        wo = self.sb([128, 8, 2048], BF16, es4)
        b_wo = Buf()
        wstg4 = self.ring(2, [128, 2048], F32, es4)
        for tq in range(8):
            self.load_cast(wo[:, tq, :], b_wo, self.s5_wout[j, tq], wstg4, eng="dve")
        sgr = self.ring(2, [128, 512], F32, es4)
        yr = self.ring(2, [128, 512], F32, es4)
        xor_ = self.ring(3, [128, 512], F32, es4)
        xnr = self.ring(3, [128, 512], F32, es4)
        for t, (s0, n) in enumerate(TILES):
            sidx = 1 if t == 0 else 0
            for mo in range(8):
                pa, b_pa = self.psum()
                pg, b_pg = self.psum()
                for (pp, b_pp, c0) in ((pa, b_pa, mo * 128), (pg, b_pg, 1024 + mo * 128)):
                    for tq in range(8):
                        self.mm((pp[:, :n], b_pp), (wo[:, tq, c0:c0 + 128], b_wo), (U[:, tq, s0:s0 + n], b_U[tq]),
                                tq == 0, tq == 7)
                sg, b_sg = sgr.next()
                self.actf((sg[:, :n], b_sg), (pg[:, :n], b_pg), AF.Sigmoid)
                y, b_y = yr.next()
                self.tt("dve", (y[:, :n], b_y), (sg[:, :n], b_sg), (pa[:, :n], b_pa), ALU.mult)
                self.residual_out(y, b_y, mo, t, xor_, xnr)
        P.barrier()
        es4.close()
        es.close()

    def residual_out(self, y, b_y, oc, t, xor_, xnr):
        P = self.P
        s0, n = TILES[t]
        sidx = 1 if t == 0 else 0
        xo, b_xo = xor_.next()
        P.dma("sp", xo[:, :n], self.xT_v[:, oc, s0:s0 + n], reads=[self.xbuf[oc][t]], writes=[b_xo])
        xn, b_xn = xnr.next()
        self.stt((xn[:, :n], b_xn), (y[:, :n], b_y), (self.geff[:, 1, oc, sidx:sidx + 1], self.b_mod),
                 (xo[:, :n], b_xo), ALU.mult, ALU.add)
        P.dma("sp", self.xT_v[:, oc, s0:s0 + n], xn[:, :n], reads=[b_xn], writes=[self.xbuf[oc][t]])


    def decl_even_inputs(self):
        nc = self.nc
        self.ev_win = self.dram_in("ev_win", [2, 128, KC, 3088])
        self.ev_convw = self.dram_in("ev_convw", [2, 128, 4, 31])
        self.ev_cvec = self.dram_in("ev_cvec", [2, 128, 3, 4])
        self.ev_sconv = self.dram_in("ev_sconv", [2, 128, 12, 5])
        self.ev_dh = self.dram_in("ev_dh", [2, 64, 2, 8])
        self.ev_og = self.dram_in("ev_og", [2, 128, 1])
        self.ev_wout = self.dram_in("ev_wout", [2, 128, KC, 1024])
        self.ev_const = self.dram_in("ev_const", [128, 6, 128])
        def scr(name, n):
            return nc.dram_tensor(name, [n, 128, T], F32, kind="Internal").ap()
        self.d_qkvraw = scr("d_qkvraw", 12)
        self.d_qkvf = scr("d_qkvf", 12)
        self.d_gate = scr("d_gate", 4)
        self.d_conv = scr("d_conv", 4)
        self.d_ot = [scr("d_otf", 4), scr("d_otb", 4)]
        nt = len(TILES)
        self.bq_raw = [Buf() for _ in range(nt)]
        self.bq_f = [Buf() for _ in range(nt)]
        self.bq_gate = [Buf() for _ in range(nt)]
        self.bq_conv = [Buf() for _ in range(nt)]
        self.bq_ot = [[Buf() for _ in range(nt)] for _ in range(2)]

    def even_stage(self, l):
        P = self.P
        j = l // 2
        esL = ExitStack()
        cst = self.sb([128, 6, 128], F32, esL)
        b_cst = Buf()
        P.dma("sp", cst[:].rearrange("p a f -> p (a f)"), self.ev_const.rearrange("p a f -> p (a f)"), writes=[b_cst])
        self.cst, self.b_cst = cst, b_cst
        self.LA = self.sb([64, 68, 8], F32, esL)
        self.BETA = self.sb([64, 68, 8], F32, esL)
        self.b_lab = Buf()
        import os
        ph = os.environ.get("EVPH", "0a,0b,1,2,3").split(",")
        if "0a" in ph:
            self.ev_e0a(l, j)
        if "0b" in ph:
            self.ev_e0b(l, j)
        if "1" in ph:
            self.ev_e1(l, j)
        if "2" in ph:
            self.ev_e2(l, j, esL)
        if "3" in ph:
            self.ev_e3(l, j)
        P.barrier()
        esL.close()

    def ev_e0a(self, l, j):
        P = self.P
        es = ExitStack()
        rs = self.mk_scratch(es)
        xr = self.ring(2, [128, KC, 512], F32, es)
        hr = self.ring(2, [128, KC, 512], BF16, es)
        NW = 2064
        w = self.sb([128, KC, NW], BF16, es)
        b_w = Buf()
        wstg = self.ring(2, [128, NW], F32, es)
        for kc in range(KC):
            self.load_cast(w[:, kc, :], b_w, self.ev_win[j, :, kc, 1024:3088], wstg, eng="dve")
        dh = self.sb([64, 2, 8], F32, es)
        b_dh = Buf()
        P.dma("sp", dh[:].rearrange("p a f -> p (a f)"), self.ev_dh[j].rearrange("p a f -> p (a f)"), writes=[b_dh])
        nA = self.sb([64, 8], F32, es)
        self.actf((nA[:], b_dh), (dh[:, 0, :], b_dh), AF.Exp)
        self.ts("dve", (nA[:], b_dh), (nA[:], b_dh), -1.0, ALU.mult)
        stg = self.ring(4, [128, 512], F32, es)
        xa = self.sb([64, 8, 8], F32, es)
        b_xa = Buf()
        for t, (s0, n) in enumerate(TILES):
            sidx = 1 if t == 0 else 0
            xt, b_x = self.load_x(xr, t)
            h, b_h = hr.next()
            self.modulate(xt, b_x, n, 1, sidx, (lambda kc, h=h, n=n: h[:, kc, :n]), b_h, rs)
            for fc in range(16):
                ps, b_ps = self.psum()
                for kc in range(KC):
                    self.mm((ps[:, :n], b_ps), (w[:, kc, fc * 128:(fc + 1) * 128], b_w), (h[:, kc, :n], b_h),
                            kc == 0, kc == KC - 1)
                st, b_st = stg.next()
                if fc < 12:
                    self.cp("act" if fc % 2 == 0 else "dve", (st[:, :n], b_st), (ps[:, :n], b_ps))
                    P.dma("sp", self.d_qkvraw[fc, :, s0:s0 + n], st[:, :n], reads=[b_st], writes=[self.bq_raw[t]])
                else:
                    self.actf((st[:, :n], b_st), (ps[:, :n], b_ps), AF.Silu)
                    P.dma("sp", self.d_gate[fc - 12, :, s0:s0 + n], st[:, :n], reads=[b_st], writes=[self.bq_gate[t]])
            nchk = n // 64
            c0 = s0 // 64
            ps, b_ps = self.psum()
            for ci in range(nchk):
                for kc in range(KC):
                    self.mm((ps[0:64, ci * 16:(ci + 1) * 16], b_ps), (h[:, kc, ci * 64:(ci + 1) * 64], b_h),
                            (w[:, kc, 2048:2064], b_w), kc == 0, kc == KC - 1)
            ab = ps[0:64, 0:nchk * 16].rearrange("p (c d a h) -> p c d a h", d=2, a=2, h=4)
            for d in range(2):
                self.tt("dve", (xa[:, :nchk, d * 4:(d + 1) * 4], b_xa), (ab[:, :, d, 0, :], b_ps),
                        (dh[:, 1, d * 4:(d + 1) * 4].unsqueeze(1).to_broadcast([64, nchk, 4]), b_dh), ALU.add)
            self.actf((xa[:, :nchk, :], b_xa), (xa[:, :nchk, :], b_xa), AF.Exp)
            self.actf((xa[:, :nchk, :], b_xa), (xa[:, :nchk, :], b_xa), AF.Ln, 1.0, 1.0)
            self.tt("dve", (self.LA[:, c0:c0 + nchk, :], self.b_lab), (xa[:, :nchk, :], b_xa),
                    (nA[:].unsqueeze(1).to_broadcast([64, nchk, 8]), b_dh), ALU.mult)
            for d in range(2):
                self.actf((self.BETA[:, c0:c0 + nchk, d * 4:(d + 1) * 4], self.b_lab), (ab[:, :, d, 1, :], b_ps),
                          AF.Sigmoid)
        P.barrier()
        es.close()

    def ev_e0b(self, l, j):
        P = self.P
        es = ExitStack()
        rs = self.mk_scratch(es)
        xr = self.ring(2, [128, KC, 512], F32, es)
        hr = self.ring(2, [128, KC, 512], BF16, es)
        w = self.sb([128, KC, 1024], BF16, es)
        b_w = Buf()
        wstg = self.ring(2, [128, 1024], F32, es)
        for kc in range(KC):
            self.load_cast(w[:, kc, :], b_w, self.ev_win[j, :, kc, 0:1024], wstg, eng="dve")
        cw = self.sb([128, 4, 31], F32, es)
        cv = self.sb([128, 3, 4], F32, es)
        b_cw = Buf()
        P.dma("sp", cw[:].rearrange("p a f -> p (a f)"), self.ev_convw[j].rearrange("p a f -> p (a f)"), writes=[b_cw])
        P.dma("sp", cv[:].rearrange("p a f -> p (a f)"), self.ev_cvec[j].rearrange("p a f -> p (a f)"), writes=[b_cw])
        DG = self.sb([128, 4, 31, 128], BF16, es)
        b_DG = Buf()
        for cc in range(4):
            for k in range(31):
                self.ts("dve", (DG[:, cc, k, :], b_DG), (self.cst[:, 0, :], self.b_cst),
                        (cw[:, cc, k:k + 1], b_cw), ALU.mult)
        sgr = self.ring(2, [128, 512], F32, es)
        cinr = self.ring(2, [128, 4, 752], BF16, es)
        hcr = self.ring(2, [128, 4, 512], F32, es)
        hbr = self.ring(2, [128, 2, 4, 512], BF16, es)
        mr = self.ring(6, [128, 512], F32, es)
        yr = self.ring(3, [128, 512], F32, es)
        for t, (s0, n) in enumerate(TILES):
            sidx = 1 if t == 0 else 0
            xt, b_x = self.load_x(xr, t)
            h, b_h = hr.next()
            self.modulate(xt, b_x, n, 1, sidx, (lambda kc, h=h, n=n: h[:, kc, :n]), b_h, rs)
            cin, b_cin = cinr.next()
            W = 256 if t == 0 else 64
            R = n // W
            WP = W + 30
            P.op("pool", lambda e, cin=cin: e.memset(cin[:].rearrange("p a f -> p (a f)"), 0.0), writes=[b_cin])
            for cc in range(4):
                pa, b_pa = self.psum()
                pg, b_pg = self.psum()
                for (pp, b_pp, c0) in ((pa, b_pa, cc * 128), (pg, b_pg, 512 + cc * 128)):
                    for kc in range(KC):
                        self.mm((pp[:, :n], b_pp), (w[:, kc, c0:c0 + 128], b_w), (h[:, kc, :n], b_h),
                                kc == 0, kc == KC - 1)
                sg, b_sg = sgr.next()
                self.actf((sg[:, :n], b_sg), (pg[:, :n], b_pg), AF.Sigmoid)
                civ = cin[:, cc, 0:R * WP].rearrange("p (r w) -> p r w", w=WP)[:, :, 15:15 + W]
                self.tt("dve", (civ, b_cin), (sg[:, :n].rearrange("p (r w) -> p r w", w=W), b_sg),
                        (pa[:, :n].rearrange("p (r w) -> p r w", w=W), b_pa), ALU.mult)
            hc, b_hc = hcr.next()
            hb, b_hb = hbr.next()
            if t == 0:
                groups = [(15, 256, 0, 1)]
            else:
                groups = [(15, 3 * WP + W, 0, 4), (4 * WP + 15, 3 * WP + W, 4, 4)]
            for cc in range(4):
                for (a0, N, r0, nr) in groups:
                    ps, b_ps = self.psum()
                    taps = [15] + [k for k in range(31) if k != 15]
                    for ti, k in enumerate(taps):
                        sft = k - 15
                        self.mm((ps[:, 0:N], b_ps), (DG[:, cc, k, :], b_DG), (cin[:, cc, a0 + sft:a0 + sft + N], b_cin),
                                ti == 0, ti == 30)
                    if t == 0:
                        src = ps[:, 0:256]
                        dst = hc[:, cc, 0:256]
                    else:
                        src = ps[:, 0:nr * WP].rearrange("p (r w) -> p r w", w=WP)[:, :, 0:W]
                        dst = hc[:, cc, r0 * W:(r0 + nr) * W].rearrange("p (r w) -> p r w", w=W)
                    self.actf((dst, b_hc), (src, b_ps), AF.Identity, 1.0, (cv[:, 0, cc:cc + 1], b_cw))
            self.cp("pool", (hb[:, 0, :, :n], b_hb), (hc[:, :, :n], b_hc))
            self.P.op("act", lambda e, hb=hb, hc=hc, n=n: e.activation(out=hb[:, 1, :, :n], in_=hc[:, :, :n],
                                                                      func=AF.Square), reads=[b_hc], writes=[b_hb])
            p1, b_p1 = self.psum()
            p2, b_p2 = self.psum()
            for (pp, b_pp, a) in ((p1, b_p1, 0), (p2, b_p2, 1)):
                for cc in range(4):
                    self.mm((pp[:, :n], b_pp), (self.ones_bf[:], self.b_const), (hb[:, a, cc, :n], b_hb),
                            cc == 0, cc == 3)
            mean, b_mean = mr.next()
            m2, b_m2 = mr.next()
            var, b_var = mr.next()
            self.ts("dve", (mean[:, :n], b_mean), (p1[:, :n], b_p1), 1.0 / 512, ALU.mult)
            self.tt("dve", (m2[:, :n], b_m2), (mean[:, :n], b_mean), (mean[:, :n], b_mean), ALU.mult)
            self.stt((var[:, :n], b_var), (p2[:, :n], b_p2), 1.0 / 512, (m2[:, :n], b_m2), ALU.mult, ALU.subtract)
            self.ts("dve", (var[:, :n], b_var), (var[:, :n], b_var), 0.0, ALU.max)
            self.actf((var[:, :n], b_var), (var[:, :n], b_var), AF.Sqrt, 1.0, (self.eps_t[:, 0:1], self.b_const))
            self.P.op("dve", lambda e, var=var, n=n: e.reciprocal(out=var[:, :n], in_=var[:, :n]),
                      reads=[b_var], writes=[b_var])
            for cc in range(4):
                y, b_y = yr.next()
                self.tt("dve", (y[:, :n], b_y), (hc[:, cc, :n], b_hc), (mean[:, :n], b_mean), ALU.subtract)
                self.tt("dve", (y[:, :n], b_y), (y[:, :n], b_y), (var[:, :n], b_var), ALU.mult)
                self.actf((y[:, :n], b_y), (y[:, :n], b_y), AF.Silu, (cv[:, 1, cc:cc + 1], b_cw),
                          (cv[:, 2, cc:cc + 1], b_cw))
                P.dma("sp", self.d_conv[cc, :, s0:s0 + n], y[:, :n], reads=[b_y], writes=[self.bq_conv[t]])
        P.barrier()
        es.close()

    def ev_e1(self, l, j):
        P = self.P
        es = ExitStack()
        sw = self.sb([128, 12, 5], F32, es)
        b_sw = Buf()
        P.dma("sp", sw[:].rearrange("p a f -> p (a f)"), self.ev_sconv[j].rearrange("p a f -> p (a f)"), writes=[b_sw])
        Rr = self.ring(2, [128, 12, 516], F32, es)
        accr = self.ring(2, [128, 12, 512], F32, es)
        prodr = self.ring(10, [128, 512], F32, es)
        b_accc = [[Buf() for _ in range(12)] for _ in range(2)]
        sqr = self.ring(2, [128, 8, 512], BF16, es)
        rnr = self.ring(3, [128, 512], F32, es)
        nt = len(TILES)
        for t, (s0, n) in enumerate(TILES):
            seg0, seg1 = (0, NCTX) if t == 0 else (NCTX, T)
            lo, hi = max(seg0, s0 - 2), min(seg1, s0 + n + 2)
            Rt, b_R = Rr.next()
            P.op("pool", lambda e, Rt=Rt: e.memset(Rt[:].rearrange("p a f -> p (a f)"), 0.0), writes=[b_R])
            nb = [self.bq_raw[tt_] for tt_ in (t - 1, t, t + 1) if 0 <= tt_ < nt]
            P.dma("sp", Rt[:, :, lo - (s0 - 2):hi - (s0 - 2)],
                  self.d_qkvraw.rearrange("a p t -> p a t")[:, :, lo:hi], reads=nb, writes=[b_R])
            acc, b_acc = accr.next()
            self.P.op("dve", lambda e, acc=acc: e.memset(acc[:, 0, 0:1], 0.0), writes=[b_acc])
            for cc in range(12):
                prods = []
                for k in range(5):
                    pk, b_pk = prodr.next()
                    self.actf((pk[:, :n], b_pk), (Rt[:, cc, k:k + n], b_R), AF.Identity, (sw[:, cc, k:k + 1], b_sw))
                    prods.append((pk, b_pk))
                b_ac = b_accc[accr.i][cc]
                self.P.op("dve", lambda e, acc=acc, cc=cc, n=n, p0=prods[0][0], p1=prods[1][0]: e.tensor_tensor(
                    out=acc[:, cc, :n], in0=p0[:, :n], in1=p1[:, :n], op=ALU.add),
                    reads=[prods[0][1], prods[1][1], b_acc], writes=[b_ac])
                for k in range(2, 5):
                    self.tt("dve", (acc[:, cc, :n], b_ac), (acc[:, cc, :n], b_ac), (prods[k][0][:, :n], prods[k][1]),
                            ALU.add)
            self.P.op("dve", lambda e, acc=acc: e.tensor_copy(out=acc[:, 0, 0:1], in_=acc[:, 0, 0:1]),
                      reads=[b_accc[accr.i][cc] for cc in range(12)], writes=[b_acc])
            self.P.op("act", lambda e, acc=acc, n=n: e.activation(out=acc[:, :, :n], in_=acc[:, :, :n], func=AF.Silu),
                      reads=[b_acc], writes=[b_acc])
            sq, b_sq = sqr.next()
            self.P.op("act", lambda e, acc=acc, sq=sq, n=n: e.activation(out=sq[:, :, :n], in_=acc[:, 0:8, :n],
                                                                      func=AF.Square), reads=[b_acc], writes=[b_sq])
            for cc in range(8):
                ps, b_ps = self.psum()
                self.mm((ps[:, :n], b_ps), (self.ones_bf[:], self.b_const), (sq[:, cc, :n], b_sq), True, True)
                rn, b_rn = rnr.next()
                self.actf((rn[:, :n], b_rn), (ps[:, :n], b_ps), AF.Sqrt, 1.0, (self.eps_t[:, 0:1], self.b_const))
                self.P.op("dve", lambda e, rn=rn, n=n: e.reciprocal(out=rn[:, :n], in_=rn[:, :n]),
                          reads=[b_rn], writes=[b_rn])
                if cc < 4:
                    self.stt((acc[:, cc, :n], b_acc), (acc[:, cc, :n], b_acc), 128.0 ** -0.5, (rn[:, :n], b_rn),
                             ALU.mult, ALU.mult)
                else:
                    self.tt("dve", (acc[:, cc, :n], b_acc), (acc[:, cc, :n], b_acc), (rn[:, :n], b_rn), ALU.mult)
            P.dma("sp", self.d_qkvf.rearrange("a p t -> p a t")[:, :, s0:s0 + n], acc[:, :, :n], reads=[b_acc],
                  writes=[self.bq_f[t]])
        P.barrier()
        es.close()


    def ev_e2(self, l, j, esL):
        P = self.P
        es = ExitStack()
        cst, b_c = self.cst, self.b_cst
        I64 = (cst[0:64, 0, 0:64], b_c)
        ID128 = (cst[:, 0, :], b_c)
        ones64 = (cst[0:64, 5, 0:64], b_c)
        ones64x128 = (cst[0:64, 5, :], b_c)
        UM = [(cst[0:64, 1, 0:64], b_c), (cst[0:64, 2, 0:64], b_c)]
        SM = [(cst[0:64, 3, 0:64], b_c), (cst[0:64, 4, 0:64], b_c)]

        def bc_h(ap):
            return ap.unsqueeze(1).to_broadcast([64, 4, 64])

        GTK = self.sb([64, 68, 8], F32, es)
        EGT = self.sb([64, 68, 8], F32, es)
        BG = self.sb([64, 68, 8], F32, es)
        ETAIL = self.sb([64, 68, 8], F32, es)
        NBETA = self.sb([64, 68, 8], F32, es)
        EGL = self.sb([128, 68, 8], F32, es)
        b_tab = Buf()
        LAb = (self.LA, self.b_lab)
        for d in range(2):
            hd = slice(d * 4, d * 4 + 4)
            ps, b_ps = self.psum()
            self.mm((ps[0:64, 0:272], b_ps), UM[d], (self.LA[:, :, hd], self.b_lab), True, True)
            self.cp("dve", (GTK[:, :, hd], b_tab), (ps[0:64, 0:272].rearrange("p (c h) -> p c h", h=4), b_ps))
            ps2, b_ps2 = self.psum()
            self.mm((ps2[:, 0:272], b_ps2), ones64x128, (self.LA[:, :, hd], self.b_lab), True, True)
            self.actf((EGL[:, :, hd], b_tab), (ps2[:, 0:272].rearrange("p (c h) -> p c h", h=4), b_ps2), AF.Exp)
            self.tt("dve", (ETAIL[:, :, hd], b_tab), (ps2[0:64, 0:272].rearrange("p (c h) -> p c h", h=4), b_ps2),
                    (GTK[:, :, hd], b_tab), ALU.subtract)
        fl = lambda tl: tl[:].rearrange("p c h -> p (c h)")
        self.ts("dve", (fl(ETAIL), b_tab), (fl(ETAIL), b_tab), 0.0, ALU.min)
        self.actf((fl(ETAIL), b_tab), (fl(ETAIL), b_tab), AF.Exp)
        self.actf((fl(EGT), b_tab), (fl(GTK), b_tab), AF.Exp)
        self.tt("dve", (fl(BG), b_tab), (fl(self.BETA), self.b_lab), (fl(EGT), b_tab), ALU.mult)
        self.ts("dve", (fl(NBETA), b_tab), (fl(self.BETA), self.b_lab), -1.0, ALU.mult)

        def R(n, shape, dt=F32):
            return self.ring(n, shape, dt, es)
        qkvr = R(4, [128, 12, 64]); qkbr = R(4, [128, 8, 64], BF16)
        L1r = R(4, [64, 4, 64]); L1nr = R(4, [64, 4, 64])
        egr = R(4, [128, 4, 64])
        dsr = R(4, [64, 4, 64]); dir_ = R(4, [64, 4, 64])
        ktr = R(4, [64, 4, 128], BF16); kbgr = R(4, [64, 4, 128]); vbr = R(4, [64, 4, 128], BF16)
        pmr = R(6, [64, 4, 64]); ptr = R(6, [64, 4, 64]); xtr = R(6, [64, 4, 64])
        pmbr = R(6, [64, 4, 64], BF16); ptbr = R(6, [64, 4, 64], BF16); xtbr = R(4, [64, 4, 64], BF16)
        import os as _os2
        INVBF = int(_os2.environ.get("INVBF", "%d" % INVBF_DEFAULT))
        qktr = R(4, [64, 4, 64], BF16); ttbr = R(4, [64, 4, 64], BF16)
        nwr = R(4, [128, 4, 64], BF16); qdr = R(4, [128, 4, 64], BF16)
        vnr = R(4, [64, 4, 128], BF16)
        S = [self.sb([128, 4, 128], F32, es) for _ in range(2)]
        Sb = [self.sb([128, 4, 128], BF16, es) for _ in range(2)]
        b_S = [Buf(), Buf()]; b_Sb = [Buf(), Buf()]
        OTs = [R(2, [128, 4, 512]), R(2, [128, 4, 512])]
        for d in range(2):
            P.op("dve", lambda e, d=d: e.memset(S[d][:].rearrange("p h v -> p (h v)"), 0.0), writes=[b_S[d]])
            P.op("pool", lambda e, d=d: e.memset(Sb[d][:].rearrange("p h v -> p (h v)"), 0.0), writes=[b_Sb[d]])
        qkvf_v = self.d_qkvf.rearrange("a p t -> p a t")

        def tile_of_chunk(c):
            return 0 if c < 4 else 1 + (c - 4) // 8

        import os as _os
        PPN = float(_os.environ.get("PPN", "99"))

        class _Stop(Exception):
            pass

        def sec(k):
            if k > PPN:
                raise _Stop()

        def prepass(c, d):
            hd = slice(d * 4, d * 4 + 4)
            t = tile_of_chunk(c)
            qkv, b_q = qkvr.next()
            P.dma("sp", qkv[:], qkvf_v[:, :, c * 64:(c + 1) * 64], reads=[self.bq_f[t]], writes=[b_q])
            qkb, b_qb = qkbr.next()
            self.cp("pool", (qkb[:], b_qb), (qkv[:, 0:8, :], b_q))
            L1, b_L1 = L1r.next()
            self.tt("dve", (L1[:], b_L1), (bc_h(UM[d][0]), b_c),
                    (self.LA[:, c, hd].unsqueeze(2).to_broadcast([64, 4, 64]), self.b_lab), ALU.mult)
            sec(2)
            ps, b_ps = self.psum()
            self.mm((ps[:, 0:256], b_ps), ones64x128, (L1[:].rearrange("p h i -> p (h i)"), b_L1), True, True)
            eg, b_eg = egr.next()
            self.actf((eg[:].rearrange("p h i -> p (h i)"), b_eg), (ps[:, 0:256], b_ps), AF.Exp)
            qd, b_qd = qdr.next()
            self.tt("dve", (qd[:], b_qd), (qkv[:, 0:4, :], b_q), (eg[:], b_eg), ALU.mult)
            sec(3)
            gp, b_gp = L1nr.next()
            self.tt("dve", (gp[:], b_gp), (ps[0:64, 0:256].rearrange("p (h i) -> p h i", h=4), b_ps),
                    (GTK[:, c, hd].unsqueeze(2).to_broadcast([64, 4, 64]), b_tab), ALU.subtract)
            ds, b_ds = dsr.next(); di, b_di = dir_.next()
            self.ts("dve", (ds[:], b_ds), (gp[:], b_gp), 0.0, ALU.max)
            self.actf((ds[:], b_ds), (ds[:], b_ds), AF.Exp, -1.0)
            self.tt("dve", (ds[:], b_ds), (ds[:], b_ds), (bc_h(SM[d][0]), b_c), ALU.mult)
            self.ts("dve", (di[:], b_di), (gp[:], b_gp), 0.0, ALU.min)
            self.actf((di[:], b_di), (di[:], b_di), AF.Exp)
            self.tt("pool", (di[:], b_di), (di[:], b_di), (bc_h(UM[d][0]), b_c), ALU.mult)
            sec(4)
            psk, b_psk = self.psum(); psv, b_psv = self.psum()
            for h in range(4):
                P.op("pe", lambda e, h=h, psk=psk, qkv=qkv: e.transpose(psk[0:64, h * 128:(h + 1) * 128],
                                                                      qkv[:, 4 + h, :], cst[:, 0, :]),
                     reads=[b_q, b_c], writes=[b_psk])
                P.op("pe", lambda e, h=h, psv=psv, qkv=qkv: e.transpose(psv[0:64, h * 128:(h + 1) * 128],
                                                                      qkv[:, 8 + h, :], cst[:, 0, :]),
                     reads=[b_q, b_c], writes=[b_psv])
            K3 = psk[0:64, :].rearrange("p (h k) -> p h k", h=4)
            V3 = psv[0:64, :].rearrange("p (h k) -> p h k", h=4)

            def bc_k(tl):
                return tl[:, c, hd].unsqueeze(2).to_broadcast([64, 4, 128])
            kt, b_kt = ktr.next(); kbg, b_kbg = kbgr.next(); vb, b_vb = vbr.next()
            self.tt("dve", (kt[:], b_kt), (K3, b_psk), (bc_k(ETAIL), b_tab), ALU.mult)
            self.tt("dve", (kbg[:], b_kbg), (K3, b_psk), (bc_k(BG), b_tab), ALU.mult)
            self.tt("dve", (vb[:], b_vb), (V3, b_psv), (bc_k(self.BETA), self.b_lab), ALU.mult)
            sec(5)
            pskk, b_pskk = self.psum(); psqk, b_psqk = self.psum()
            for h in range(4):
                self.mm((pskk[0:64, h * 64:(h + 1) * 64], b_pskk), (qkb[:, 4 + h, :], b_qb), (qkb[:, 4 + h, :], b_qb),
                        True, True)
                self.mm((psqk[0:64, h * 64:(h + 1) * 64], b_psqk), (qkb[:, 4 + h, :], b_qb), (qkb[:, h, :], b_qb),
                        True, True)
            Pm, b_Pm = pmr.next()
            self.tt("dve", (Pm[:], b_Pm), (pskk[0:64, 0:256].rearrange("p (h i) -> p h i", h=4), b_pskk),
                    (ds[:], b_ds), ALU.mult)
            self.tt("dve", (Pm[:], b_Pm), (Pm[:], b_Pm),
                    (NBETA[:, c, hd].unsqueeze(2).to_broadcast([64, 4, 64]), b_tab), ALU.mult)
            qkt, b_qkt = qktr.next()
            self.tt("dve", (qkt[:], b_qkt), (psqk[0:64, 0:256].rearrange("p (h i) -> p h i", h=4), b_psqk),
                    (di[:], b_di), ALU.mult)
            sec(6)
            pst, b_pst = self.psum()
            for h in range(4):
                self.mm((pst[0:64, h * 64:(h + 1) * 64], b_pst), (Pm[:, h, :], b_Pm), I64, True, True)
            sec(6.3)
            PT, b_PT = ptr.next()
            self.cp("act", (PT[:].rearrange("p h i -> p (h i)"), b_PT), (pst[0:64, 0:256], b_pst))
            sec(6.6)
            XT, b_XT = xtr.next()
            self.tt("dve", (XT[:], b_XT), (PT[:], b_PT), (bc_h(I64[0]), b_c), ALU.add)
            sec(7)
            for lvl in range(1, 6):
                lowp = lvl >= INVBF
                psP, b_psP = self.psum()
                for h in range(4):
                    self.mm((psP[0:64, h * 64:(h + 1) * 64], b_psP), (PT[:, h, :], b_PT), (Pm[:, h, :], b_Pm), True, True)
                if lvl < 5:
                    psT, b_psT = self.psum()
                    for h in range(4):
                        self.mm((psT[0:64, h * 64:(h + 1) * 64], b_psT), (Pm[:, h, :], b_Pm), (PT[:, h, :], b_PT),
                                True, True)
                Pm2, b_Pm2 = (pmbr if lowp else pmr).next()
                self.cp("act", (Pm2[:].rearrange("p h i -> p (h i)"), b_Pm2), (psP[0:64, 0:256], b_psP))
                if lvl < 5:
                    PT2, b_PT2 = (ptbr if lowp else ptr).next()
                    self.cp("dve", (PT2[:].rearrange("p h i -> p (h i)"), b_PT2), (psT[0:64, 0:256], b_psT))
                if lowp:
                    XTr_, b_XTr = xtbr.next()
                    self.cp("pool", (XTr_[:].rearrange("p h i -> p (h i)"), b_XTr),
                            (XT[:].rearrange("p h i -> p (h i)"), b_XT))
                else:
                    XTr_, b_XTr = XT, b_XT
                psX, b_psX = self.psum()
                for h in range(4):
                    self.mm((psX[0:64, h * 64:(h + 1) * 64], b_psX), (Pm2[:, h, :], b_Pm2), (XTr_[:, h, :], b_XTr), True, True)
                XT2, b_XT2 = xtr.next()
                self.tt("dve", (XT2[:].rearrange("p h i -> p (h i)"), b_XT2), (XT[:].rearrange("p h i -> p (h i)"), b_XT),
                        (psX[0:64, 0:256], b_psX), ALU.add)
                Pm, b_Pm = Pm2, b_Pm2
                if lvl < 5:
                    PT, b_PT = PT2, b_PT2
                XT, b_XT = XT2, b_XT2
            sec(8)
            ttb, b_ttb = ttbr.next()
            self.cp("act", (ttb[:].rearrange("p h i -> p (h i)"), b_ttb), (XT[:].rearrange("p h i -> p (h i)"), b_XT))
            psw, b_psw = self.psum()
            for h in range(4):
                self.mm((psw[:, h * 64:(h + 1) * 64], b_psw), (kbg[:, h, :], b_kbg), (XT[:, h, :], b_XT), True, True)
            nw, b_nw = nwr.next()
            self.actf((nw[:].rearrange("p h i -> p (h i)"), b_nw), (psw[:, 0:256], b_psw), AF.Identity, -1.0)
            return dict(ttb=(ttb, b_ttb), vb=(vb, b_vb), nw=(nw, b_nw), qd=(qd, b_qd), qkt=(qkt, b_qkt), kt=(kt, b_kt))

        cur_ot = [None, None]

        def step(c, d, pp):
            t = tile_of_chunk(c)
            s0, n = TILES[t]
            col = c * 64 - s0
            ttb, vb, nw, qd, qkt, kt = pp["ttb"], pp["vb"], pp["nw"], pp["qd"], pp["qkt"], pp["kt"]
            psv, b_psv = self.psum()
            for h in range(4):
                self.mm((psv[0:64, h * 128:(h + 1) * 128], b_psv), (ttb[0][:, h, :], ttb[1]), (vb[0][:, h, :], vb[1]),
                        True, False)
                self.mm((psv[0:64, h * 128:(h + 1) * 128], b_psv), (nw[0][:, h, :], nw[1]), (Sb[d][:, h, :], b_Sb[d]),
                        False, True)
            vn, b_vn = vnr.next()
            self.cp("act", (vn[:].rearrange("p h v -> p (h v)"), b_vn), (psv[0:64, :], b_psv))
            pso, b_pso = self.psum()
            for h in range(4):
                self.mm((pso[:, h * 64:(h + 1) * 64], b_pso), (Sb[d][:, h, :], b_Sb[d]), (qd[0][:, h, :], qd[1]),
                        True, False)
                self.mm((pso[:, h * 64:(h + 1) * 64], b_pso), (vn[:, h, :], b_vn), (qkt[0][:, h, :], qkt[1]),
                        False, True)
            first = (col == 0) if d == 0 else (col == n - 64)
            last = (col == n - 64) if d == 0 else (col == 0)
            if first:
                cur_ot[d] = OTs[d].next()
            ot, b_ot = cur_ot[d]
            self.cp("dve", (ot[:, :, col:col + 64], b_ot), (pso[:, 0:256].rearrange("p (h i) -> p h i", h=4), b_pso))
            if last:
                P.dma("sp", self.d_ot[d].rearrange("a p t -> p a t")[:, :, s0:s0 + n], ot[:, :, :n], reads=[b_ot],
                      writes=[self.bq_ot[d][t]])
            pss, b_pss = self.psum()
            for h in range(4):
                self.mm((pss[:, h * 128:(h + 1) * 128], b_pss), (kt[0][:, h, :], kt[1]), (vn[:, h, :], b_vn), True, True)
            for h in range(4):
                self.stt((S[d][:, h, :], b_S[d]), (S[d][:, h, :], b_S[d]), (EGL[:, c, d * 4 + h:d * 4 + h + 1], b_tab),
                         (pss[:, h * 128:(h + 1) * 128], b_pss), ALU.mult, ALU.add)
            self.cp("pool", (Sb[d][:].rearrange("p h v -> p (h v)"), b_Sb[d]), (S[d][:].rearrange("p h v -> p (h v)"), b_S[d]))

        order = [list(range(68)), [3, 2, 1, 0] + list(range(67, 3, -1))]
        import os
        ev2 = os.environ.get("EV2", "all")
        if ev2 == "tab":
            P.barrier(); es.close(); return
        if ev2 == "pre1":
            try:
                prepass(0, 0)
            except _Stop:
                pass
            P.barrier(); es.close(); return
        if ev2 == "pre":
            for sidx in range(68):
                prepass(order[0][sidx], 0); prepass(order[1][sidx], 1)
            P.barrier(); es.close(); return
        NST = int(os.environ.get("EV2N", "68"))
        pend = [prepass(order[0][0], 0), prepass(order[1][0], 1)]
        for sidx in range(NST):
            nxt = [None, None]
            if sidx + 1 < NST:
                nxt = [prepass(order[0][sidx + 1], 0), prepass(order[1][sidx + 1], 1)]
            step(order[0][sidx], 0, pend[0])
            step(order[1][sidx], 1, pend[1])
            pend = nxt
        P.barrier()
        es.close()

    def ev_e3(self, l, j):
        P = self.P
        es = ExitStack()
        wo = self.sb([128, KC, 1024], BF16, es)
        b_wo = Buf()
        wstg = self.ring(2, [128, 1024], F32, es)
        for kc in range(KC):
            self.load_cast(wo[:, kc, :], b_wo, self.ev_wout[j, :, kc, :], wstg, eng="dve")
        og = self.sb([128, 1], F32, es)
        b_og = Buf()
        P.dma("sp", og[:], self.ev_og[j], writes=[b_og])
        ofr = self.ring(2, [128, 4, 512], F32, es)
        obr = self.ring(2, [128, 4, 512], F32, es)
        gtr = self.ring(2, [128, 4, 512], F32, es)
        cvr = self.ring(2, [128, 4, 512], F32, es)
        sqr = self.ring(2, [128, 4, 512], BF16, es)
        mixr = self.ring(2, [128, 8, 512], BF16, es)
        rnr = self.ring(3, [128, 512], F32, es)
        xor_ = self.ring(3, [128, 512], F32, es)
        xnr = self.ring(3, [128, 512], F32, es)
        def issue_loads(t):
            s0, n = TILES[t]
            of, b_of = ofr.next(); ob, b_ob = obr.next(); gt, b_gt = gtr.next(); cv, b_cv = cvr.next()
            P.dma("sp", of[:, :, :n], self.d_ot[0].rearrange("a p t -> p a t")[:, :, s0:s0 + n],
                  reads=[self.bq_ot[0][t]], writes=[b_of])
            P.dma("sp", ob[:, :, :n], self.d_ot[1].rearrange("a p t -> p a t")[:, :, s0:s0 + n],
                  reads=[self.bq_ot[1][t]], writes=[b_ob])
            P.dma("sp", gt[:, :, :n], self.d_gate.rearrange("a p t -> p a t")[:, :, s0:s0 + n],
                  reads=[self.bq_gate[t]], writes=[b_gt])
            P.dma("sp", cv[:, :, :n], self.d_conv.rearrange("a p t -> p a t")[:, :, s0:s0 + n],
                  reads=[self.bq_conv[t]], writes=[b_cv])
            return (of, b_of, ob, b_ob, gt, b_gt, cv, b_cv)

        pending = issue_loads(0)
        for t, (s0, n) in enumerate(TILES):
            of, b_of, ob, b_ob, gt, b_gt, cv, b_cv = pending
            if t + 1 < len(TILES):
                pending = issue_loads(t + 1)
            self.tt("pool", (of[:, :, :n], b_of), (of[:, :, :n], b_of), (ob[:, :, :n], b_ob), ALU.add)
            sq, b_sq = sqr.next()
            self.P.op("act", lambda e, sq=sq, of=of, n=n: e.activation(out=sq[:, :, :n], in_=of[:, :, :n], func=AF.Square),
                      reads=[b_of], writes=[b_sq])
            mix, b_mix = mixr.next()
            self.cp("pool", (mix[:, 0:4, :n], b_mix), (cv[:, :, :n], b_cv))
            for h in range(4):
                ps, b_ps = self.psum()
                self.mm((ps[:, :n], b_ps), (self.ones_bf[:], self.b_const), (sq[:, h, :n], b_sq), True, True)
                rn, b_rn = rnr.next()
                self.actf((rn[:, :n], b_rn), (ps[:, :n], b_ps), AF.Sqrt, 1.0 / 128, (self.eps_t[:, 0:1], self.b_const))
                self.P.op("dve", lambda e, rn=rn, n=n: e.reciprocal(out=rn[:, :n], in_=rn[:, :n]),
                          reads=[b_rn], writes=[b_rn])
                self.tt("dve", (rn[:, :n], b_rn), (rn[:, :n], b_rn), (of[:, h, :n], b_of), ALU.mult)
                self.stt((mix[:, 4 + h, :n], b_mix), (rn[:, :n], b_rn), (og[:, 0:1], b_og), (gt[:, h, :n], b_gt),
                         ALU.mult, ALU.mult)
            for oc in range(KC):
                ps, b_ps = self.psum()
                for kc in range(KC):
                    self.mm((ps[:, :n], b_ps), (wo[:, kc, oc * 128:(oc + 1) * 128], b_wo), (mix[:, kc, :n], b_mix),
                            kc == 0, kc == KC - 1)
                self.residual_out(ps, b_ps, oc, t, xor_, xnr)
        P.barrier()
        es.close()

    def final_stage(self):
        P = self.P
        es = ExitStack()
        rs = self.mk_scratch(es)
        xr = self.ring(2, [128, KC, 512], F32, es)
        yr = self.ring(2, [128, KC, 512], F32, es)
        for t in range(1, len(TILES)):
            s0, n = TILES[t]
            xt, b_x = self.load_x(xr, t)
            y, b_y = yr.next()
            self.modulate(xt, b_x, n, None, 0, (lambda kc, y=y, n=n: y[:, kc, :n]), b_y, rs)
            P.dma("sp", self.out_v[:, :, s0 - NCTX:s0 - NCTX + n], y[:, :, :n], reads=[b_y])
        P.barrier()
        es.close()


def fm(v):
    return np.ascontiguousarray(v.reshape(KC, 128).T)


def prep_shared(inp, need=None):
    sh = {}
    w_mod = inp["w_mod"]
    sh["wmod"] = np.ascontiguousarray(
        w_mod.reshape(DEPTH, KC, 128, 18, 512).transpose(0, 3, 2, 1, 4)).reshape(DEPTH, 18, 128, KC * 512)
    sh["bmod"] = np.ascontiguousarray(inp["b_mod"].reshape(DEPTH, 72, 128).transpose(0, 2, 1))
    ng = inp["norm_g"].reshape(DEPTH, 3, KC, 128).transpose(0, 3, 1, 2)
    sh["ng2"] = np.ascontiguousarray(np.repeat(ng[..., None], 2, axis=-1)).reshape(DEPTH, 128, 48)
    sh["fg"] = fm(inp["final_g"])
    if need is not None and "s5_win" in need:
        sh.update(prep_s5(inp))
    if need is not None and "ev_win" in need:
        sh.update(prep_even(inp))
    if need is not None and "w13" not in need:
        return sh
    w13 = inp["ffn_w13"]
    a = w13[..., :DFF].reshape(DEPTH, 2, KC, 128, NI, 128)
    b = w13[..., DFF:].reshape(DEPTH, 2, KC, 128, NI, 128)
    ab = np.stack([a, b], axis=5)
    sh["w13"] = np.ascontiguousarray(ab.transpose(0, 1, 4, 3, 2, 5, 6)).reshape(DEPTH, 2, NI, 128, KC * 256)
    w2 = inp["ffn_w2"].reshape(DEPTH, 2, NI, 128, KC, 128)
    sh["w2"] = np.ascontiguousarray(w2.transpose(0, 1, 4, 3, 2, 5)).reshape(DEPTH, 2, KC, 128, NI * 128)
    return sh


def prep_s5(inp):
    o = {}
    w_in = inp["o_w_in"]; lre = inp["o_lam_re"]; lim = inp["o_lam_im"]; lst = inp["o_log_step"]
    bre = inp["o_b_re"]; bim = inp["o_b_im"]; cre = inp["o_c_re"]; cim = inp["o_c_im"]
    od = inp["o_d"]; wout = inp["o_w_out"]
    NJ = w_in.shape[0]
    win_s = np.zeros((NJ, 8, 128, KC, 128), np.float32)
    wout_s = np.zeros((NJ, 8, 128, 2048), np.float32)
    dd = np.zeros((NJ, 8, 128, 128), np.float32)
    lamA = np.zeros((NJ, 8, 128, 3, 2, 2, 2, 64), np.float32)
    bA = np.zeros((NJ, 8, 128, 2, 2, 2, 2, 64), np.float32)
    lamB = np.zeros((NJ, 2, 64, 3, 16, 2), np.float32)
    cB = np.zeros((NJ, 16, 2, 64, 2, 2, 4, 32), np.float32)
    bB = np.zeros((NJ, 16, 2, 64, 2, 2, 4, 32), np.float32)
    for j in range(NJ):
        wk = w_in[j].reshape(KC, 128, 512)
        for tq in range(8):
            for sl in range(4):
                g = 4 * tq + sl
                r0 = 32 * sl
                win_s[j, tq, :, :, r0:r0 + 16] = wk[:, :, 16 * g:16 * g + 16].transpose(1, 0, 2)
                wout_s[j, tq, r0:r0 + 16, :] = wout[j, 16 * g:16 * g + 16, :]
                dd[j, tq, np.arange(r0, r0 + 16), np.arange(r0, r0 + 16)] = od[j, 16 * g:16 * g + 16]
                pr, s2 = sl // 2, sl % 2
                for d in range(2):
                    lamA[j, tq, :, 0, pr, d, s2, :] = lre[j, d, g][None, :]
                    lamA[j, tq, :, 1, pr, d, s2, :] = lim[j, d, g][None, :]
                    lamA[j, tq, :, 2, pr, d, s2, :] = lst[j, d, g]
                    bA[j, tq, r0:r0 + 16, 0, pr, d, s2, :] = bre[j, d, g].T
                    bA[j, tq, r0:r0 + 16, 1, pr, d, s2, :] = bim[j, d, g].T
        for Pp in range(16):
            for s2 in range(2):
                g = 2 * Pp + s2
                sl = g % 4
                for d in range(2):
                    lamB[j, s2, :, 0, Pp, d] = lre[j, d, g]
                    lamB[j, s2, :, 1, Pp, d] = lim[j, d, g]
                    lamB[j, s2, :, 2, Pp, d] = lst[j, d, g]
                    cB[j, Pp, s2, :, 0, d, sl, 0:16] = cre[j, d, g].T
                    cB[j, Pp, s2, :, 1, d, sl, 0:16] = cim[j, d, g].T
                    bB[j, Pp, s2, :, 0, d, sl, 0:16] = bre[j, d, g]
                    bB[j, Pp, s2, :, 1, d, sl, 0:16] = bim[j, d, g]
    o["s5_win"] = win_s.reshape(NJ, 8, 128, KC * 128)
    o["s5_wout"] = wout_s
    o["s5_dd"] = dd
    o["s5_lamA"] = lamA.reshape(NJ, 8, 128, 3 * 512)
    o["s5_bA"] = bA.reshape(NJ, 8, 128, 2 * 512)
    o["s5_lamB"] = lamB.reshape(NJ, 128, 3 * 32)
    o["s5_cB"] = cB.reshape(NJ, 16, 128, 2 * 2 * 128)
    o["s5_bB"] = bB.reshape(NJ, 16, 128, 2 * 2 * 128)
    return o


def prep_even(inp):
    o = {}
    NJ = inp["e_w_in"].shape[0]
    o["ev_win"] = np.ascontiguousarray(inp["e_w_in"].reshape(NJ, KC, 128, 3088).transpose(0, 2, 1, 3))
    o["ev_convw"] = np.ascontiguousarray(inp["e_conv_w"].reshape(NJ, 31, 4, 128).transpose(0, 3, 2, 1))
    cv = np.stack([inp["e_conv_b"], inp["e_cln_g"], inp["e_cln_b"]], axis=1)
    o["ev_cvec"] = np.ascontiguousarray(cv.reshape(NJ, 3, 4, 128).transpose(0, 3, 1, 2))
    o["ev_sconv"] = np.ascontiguousarray(inp["e_sconv_w"].reshape(NJ, 5, 12, 128).transpose(0, 3, 2, 1))
    dh = np.stack([inp["e_a_log"].reshape(NJ, 8), inp["e_dt_bias"].reshape(NJ, 8)], axis=1)
    o["ev_dh"] = np.ascontiguousarray(np.broadcast_to(dh[:, None], (NJ, 64, 2, 8)))
    o["ev_og"] = np.ascontiguousarray(inp["e_onorm_g"].reshape(NJ, 128, 1))
    o["ev_wout"] = np.ascontiguousarray(inp["e_w_out"].reshape(NJ, KC, 128, 1024).transpose(0, 2, 1, 3))
    c = np.zeros((128, 6, 128), np.float32)
    c[:, 0, :] = np.eye(128)
    a = np.arange(64)
    c[:64, 1, :64] = (a[:, None] <= a[None, :])
    c[:64, 2, :64] = (a[:, None] >= a[None, :])
    c[:64, 3, :64] = (a[:, None] > a[None, :])
    c[:64, 4, :64] = (a[:, None] < a[None, :])
    c[:, 5, :] = 1.0
    o["ev_const"] = c
    return o


def prep_core(inp, b):
    m = {}
    xa = np.concatenate([inp["ctx"][b], inp["x"][b]], axis=0)
    m["xin"] = np.ascontiguousarray(xa.T)
    cv = np.stack([fm(inp["c"][b]), fm(inp["c_ctx"])], axis=-1)
    m["cvec"] = np.ascontiguousarray(cv)
    return m


_CACHE = {}


def kernel(**inputs):
    inp = {k: np.asarray(v, dtype=np.float32) for k, v in inputs.items()}
    B = _CACHE.get("ncores", inp["x"].shape[0])
    if "nc" not in _CACHE:
        k = K()
        _CACHE["nc"] = k.build()
        _CACHE["need"] = set(k.din.keys())
    nc = _CACHE["nc"]
    need = _CACHE["need"]
    sh = prep_shared(inp, need)
    xover = _CACHE.get("xin_override")
    in_maps = []
    for b in range(B):
        m = dict(sh)
        m.update(prep_core(inp, b))
        if xover is not None:
            m["xin"] = xover
        in_maps.append({k: v for k, v in m.items() if k in need})
    _CACHE["in_maps"] = in_maps
    res = run_bass_kernel_spmd(nc, in_maps, core_ids=list(range(B)))
    _CACHE["res"] = res
    if "out" not in res.results[0]:
        return None
    out = np.stack([np.ascontiguousarray(r["out"].T) for r in res.results], axis=0)
    return out.astype(np.float32)
```
